# Optimizing a Trainium2 kernel written in Bass

```python
import math
import jax, jax.numpy as jnp
from jax import lax
import numpy as np


D_MODEL = 1024
BATCH = 8
SEQ = 2048
DEPTH = 2

CHUNK = 64
N_MEM = 256
HEAD_DIM = 64
MIX_WIDTH = (D_MODEL * 3) // 4
N_MIX_HEADS = MIX_WIDTH // HEAD_DIM
MEM_WIDTH = D_MODEL - MIX_WIDTH
N_MEM_HEADS = MEM_WIDTH // HEAD_DIM
D_FF = 2816
DECAY_LORA = 64
AAA_LORA = 64
GATE_LORA = 128
Q_BLOCK = 128
N_MIXERS = 2
N_RWKV = (DEPTH + 1) // 2
N_SB = DEPTH // 2
ALPHA = (2 * DEPTH) ** 0.25
BETA = (8 * DEPTH) ** -0.25
LN_EPS = 1e-5
LNX_EPS = 64e-5
RWKV_SHIFT = 3 * MIX_WIDTH + DECAY_LORA + AAA_LORA + GATE_LORA
RWKV_IN = RWKV_SHIFT + MEM_WIDTH
SB_IN = 3 * MIX_WIDTH + MEM_WIDTH

kernel_name = "hybrid_rwkv7_stickbreak_macaron_deepnorm"


def layer_norm(x, g, b):
    xf = x.astype(jnp.float32)
    mu = jnp.mean(xf, axis=-1, keepdims=True)
    var = jnp.mean(jnp.square(xf - mu), axis=-1, keepdims=True)
    return ((xf - mu) * lax.rsqrt(var + LN_EPS) * g + b).astype(x.dtype)


def swiglu(x, w_gate, w_up, w_down):
    return (jax.nn.silu(x @ w_gate) * (x @ w_up)) @ w_down


def split_heads(t):
    b, s, _ = t.shape
    return t.reshape(b, s, -1, HEAD_DIM).transpose(0, 2, 1, 3)


def merge_heads(t):
    b, h, s, d = t.shape
    return t.transpose(0, 2, 1, 3).reshape(b, s, h * d)


def token_shift(p, mu):
    prev = jnp.pad(p, ((0, 0), (1, 0), (0, 0)))[:, :-1]
    return p + (prev - p) * mu


def memory_attention(mq, mem_k, mem_v):
    q = split_heads(mq)
    scores = jnp.einsum('bhsd,bhmd->bhsm', q, mem_k).astype(jnp.float32) / math.sqrt(HEAD_DIM)
    p = jax.nn.softmax(scores, axis=-1).astype(mq.dtype)
    return merge_heads(jnp.einsum('bhsm,bhmd->bhsd', p, mem_v))


def rwkv7_mix(z_in, mu, w0, w_up, a0, a_up, g_up, k_k, k_a, r_k, lnx_g, lnx_b):
    bsz, seq, _ = z_in.shape
    dt = z_in.dtype
    z = token_shift(z_in, mu)
    c = MIX_WIDTH
    r = z[..., :c]
    k = z[..., c:2 * c]
    v = z[..., 2 * c:3 * c]
    o = 3 * c
    dw = z[..., o:o + DECAY_LORA]
    o += DECAY_LORA
    da = z[..., o:o + AAA_LORA]
    o += AAA_LORA
    dg = z[..., o:o + GATE_LORA]
    w = -jax.nn.softplus(-(w0 + jnp.tanh(dw) @ w_up)) - 0.5
    a = jax.nn.sigmoid(a0 + da @ a_up)
    g = jax.nn.sigmoid(dg) @ g_up

    def hs(t):
        return t.astype(jnp.float32).reshape(bsz, seq, N_MIX_HEADS, HEAD_DIM)

    kk = hs(k * k_k)
    kk = kk * lax.rsqrt(jnp.maximum(jnp.sum(kk * kk, axis=-1, keepdims=True), 1e-24))
    k = hs(k * (1.0 + (a - 1.0) * k_a))
    r, v, a = hs(r), hs(v), hs(a)
    decay = jnp.exp(-jnp.exp(hs(w)))

    n_chunks = seq // CHUNK

    def to_chunks(t):
        return t.transpose(1, 0, 2, 3).reshape(n_chunks, CHUNK, bsz, N_MIX_HEADS, HEAD_DIM)

    xs = (to_chunks(r), to_chunks(decay), to_chunks(k), to_chunks(v), to_chunks(kk), to_chunks(kk * a))

    def step(state, inp):
        r_t, w_t, k_t, v_t, kk_t, b_t = inp
        sa = jnp.einsum('bhij,bhj->bhi', state, kk_t)
        state = (state * w_t[:, :, None, :] - sa[..., :, None] * b_t[..., None, :]
                 + v_t[..., :, None] * k_t[..., None, :])
        return state, jnp.einsum('bhij,bhj->bhi', state, r_t)

    def chunk_step(state, chunk_inp):
        return lax.scan(step, state, chunk_inp)

    state0 = jnp.zeros((bsz, N_MIX_HEADS, HEAD_DIM, HEAD_DIM), jnp.float32)
    _, out = lax.scan(chunk_step, state0, xs)
    out = out.reshape(seq, bsz, N_MIX_HEADS, HEAD_DIM).transpose(1, 0, 2, 3)
    m = jnp.mean(out, axis=-1, keepdims=True)
    var = jnp.mean(jnp.square(out - m), axis=-1, keepdims=True)
    out = ((out - m) * lax.rsqrt(var + LNX_EPS)).reshape(bsz, seq, c) * lnx_g + lnx_b
    bonus = jnp.sum(r * k * r_k, axis=-1, keepdims=True) * v
    y = (out + bonus.reshape(bsz, seq, c)) * g
    return y.astype(dt)


def stick_breaking_mix(q, k, v):
    q, k, v = split_heads(q), split_heads(k), split_heads(v)
    seq = q.shape[2]
    scale = 1.0 / math.sqrt(HEAD_DIM)
    outs = []
    for blk in range(seq // Q_BLOCK):
        start = blk * Q_BLOCK
        end = start + Q_BLOCK
        kb = k[:, :, :end]
        vb = v[:, :, :end]
        z = jnp.einsum('bhtd,bhsd->bhts', q[:, :, start:end], kb).astype(jnp.float32) * scale
        t_pos = start + jnp.arange(Q_BLOCK)[:, None]
        s_pos = jnp.arange(end)[None, :]
        before = s_pos < t_pos
        log_keep = jnp.where(before, -jax.nn.softplus(z), 0.0)
        later = lax.cumsum(log_keep, axis=3, reverse=True) - log_keep
        att = jnp.where(before, jnp.exp(jax.nn.log_sigmoid(z) + later), 0.0)
        outs.append(jnp.einsum('bhts,bhsd->bhtd', att.astype(v.dtype), vb))
    return merge_heads(jnp.concatenate(outs, axis=2))


def setup_inputs(seed: int = 0) -> dict:
    key = jax.random.key(seed)
    ks = jax.random.split(key, 32)
    f32 = jnp.float32
    d, f, c = D_MODEL, D_FF, MIX_WIDTH

    def nrm(k, shape, scale):
        return jax.random.normal(k, shape, f32) * scale

    return {
        "x": nrm(ks[0], (BATCH, SEQ, d), 1.0),
        "mem": nrm(ks[1], (BATCH, N_MEM, d), 1.0),
        "ffn1_w_gate": nrm(ks[2], (DEPTH, d, f), d ** -0.5),
        "ffn1_w_up": nrm(ks[3], (DEPTH, d, f), d ** -0.5),
        "ffn1_w_down": nrm(ks[4], (DEPTH, f, d), BETA * f ** -0.5),
        "ffn2_w_gate": nrm(ks[5], (DEPTH, d, f), d ** -0.5),
        "ffn2_w_up": nrm(ks[6], (DEPTH, d, f), d ** -0.5),
        "ffn2_w_down": nrm(ks[7], (DEPTH, f, d), BETA * f ** -0.5),
        "ln_g": 1.0 + nrm(ks[8], (DEPTH, 3, d), 0.02),
        "ln_b": nrm(ks[9], (DEPTH, 3, d), 0.02),
        "w_out": nrm(ks[10], (DEPTH, d, d), BETA * d ** -0.5),
        "w_mem_kv": nrm(ks[11], (d, 2 * MEM_WIDTH), d ** -0.5),
        "rwkv_w_in": nrm(ks[12], (N_RWKV, d, RWKV_IN), d ** -0.5),
        "rwkv_mu": jax.random.uniform(ks[13], (N_RWKV, RWKV_SHIFT), f32),
        "rwkv_w0": jax.random.uniform(ks[14], (N_RWKV, c), f32, -6.0, -1.0),
        "rwkv_w_up": nrm(ks[15], (N_RWKV, DECAY_LORA, c), 0.1 * DECAY_LORA ** -0.5),
        "rwkv_a0": nrm(ks[16], (N_RWKV, c), 0.5),
        "rwkv_a_up": nrm(ks[17], (N_RWKV, AAA_LORA, c), 0.5 * AAA_LORA ** -0.5),
        "rwkv_g_up": nrm(ks[18], (N_RWKV, GATE_LORA, c), GATE_LORA ** -0.5),
        "rwkv_k_k": 1.0 + nrm(ks[19], (N_RWKV, c), 0.1),
        "rwkv_k_a": 1.0 + nrm(ks[20], (N_RWKV, c), 0.1),
        "rwkv_r_k": nrm(ks[21], (N_RWKV, N_MIX_HEADS, HEAD_DIM), 0.1),
        "rwkv_lnx_g": 1.0 + nrm(ks[22], (N_RWKV, c), 0.02),
        "rwkv_lnx_b": nrm(ks[23], (N_RWKV, c), 0.02),
        "sb_w_in": nrm(ks[24], (N_SB, d, SB_IN), d ** -0.5),
    }


def reference(x, mem, ffn1_w_gate, ffn1_w_up, ffn1_w_down, ffn2_w_gate, ffn2_w_up, ffn2_w_down,
              ln_g, ln_b, w_out, w_mem_kv, rwkv_w_in, rwkv_mu, rwkv_w0, rwkv_w_up, rwkv_a0,
              rwkv_a_up, rwkv_g_up, rwkv_k_k, rwkv_k_a, rwkv_r_k, rwkv_lnx_g, rwkv_lnx_b, sb_w_in):
    mem_kv = mem @ w_mem_kv
    mem_k = split_heads(mem_kv[..., :MEM_WIDTH])
    mem_v = split_heads(mem_kv[..., MEM_WIDTH:])
    c = MIX_WIDTH
    for i in range(DEPTH):
        x = layer_norm(ALPHA * x + 0.5 * swiglu(x, ffn1_w_gate[i], ffn1_w_up[i], ffn1_w_down[i]),
                       ln_g[i, 0], ln_b[i, 0])
        j = i // N_MIXERS
        if i % N_MIXERS == 0:
            proj = x @ rwkv_w_in[j]
            mix = rwkv7_mix(proj[..., :RWKV_SHIFT], rwkv_mu[j], rwkv_w0[j], rwkv_w_up[j],
                            rwkv_a0[j], rwkv_a_up[j], rwkv_g_up[j], rwkv_k_k[j], rwkv_k_a[j],
                            rwkv_r_k[j], rwkv_lnx_g[j], rwkv_lnx_b[j])
            mq = proj[..., RWKV_SHIFT:]
        else:
            proj = x @ sb_w_in[j]
            mix = stick_breaking_mix(proj[..., :c], proj[..., c:2 * c], proj[..., 2 * c:3 * c])
            mq = proj[..., 3 * c:]
        heads = jnp.concatenate([mix, memory_attention(mq, mem_k, mem_v)], axis=-1)
        x = layer_norm(ALPHA * x + heads @ w_out[i], ln_g[i, 1], ln_b[i, 1])
        x = layer_norm(ALPHA * x + 0.5 * swiglu(x, ffn2_w_gate[i], ffn2_w_up[i], ffn2_w_down[i]),
                       ln_g[i, 2], ln_b[i, 2])
    return x
```

```python
import numpy as np
from contextlib import ExitStack

import concourse.bass as bass
import concourse.mybir as mybir
from concourse.bass_utils import run_bass_kernel_spmd

F32 = mybir.dt.float32
BF16 = mybir.dt.bfloat16
AF = mybir.ActivationFunctionType
ALU = mybir.AluOpType

D = 1024
S = 2048
DFF = 2816
NKC = 8
NFC = 22
NMEM = 256
ALPHA = 4.0 ** 0.25
LN_EPS = 1e-5
LNX_EPS = 64e-5
DECAY_C = -float(np.exp(-0.5))

ENGS = ['pe', 'act', 'dve', 'pool', 'sp']
SAME_ENGINE_SYNC = True


class Res:
    __slots__ = ('name', 'last_w', 'readers', 'dreaders', 'sem', 'dma_cnt', 'excl')

    def __init__(self, name):
        self.name = name
        self.excl = False
        self.last_w = None
        self.readers = {}
        self.dreaders = []
        self.sem = None
        self.dma_cnt = 0


class OpRec:
    __slots__ = ('eng', 'fn', 'is_dma', 'deps', 'flagged', 'count', 'dres', 'dval', 'idx')


class Prog:
    def __init__(self, nc, es):
        self.nc = nc
        self.es = es
        self.q = {e: [] for e in ENGS}
        self.nres = 0
        self.all_dma = []
        self.bar = {e: None for e in ENGS}

    def res(self, name=None):
        self.nres += 1
        return Res((name or "r") + str(self.nres))

    def _track(self, op, reads, writes):
        deps = []
        for r in reads:
            if r.last_w is not None:
                deps.append(r.last_w)
        for w in writes:
            if w.last_w is not None:
                deps.append(w.last_w)
            deps.extend(w.readers.values())
            deps.extend(w.dreaders)
        b = self.bar[op.eng]
        if b is not None:
            deps.extend(b)
            self.bar[op.eng] = None
        for r in reads:
            if op.is_dma:
                r.dreaders.append(op)
            else:
                r.readers[op.eng] = op
        for w in writes:
            w.last_w = op
            w.readers = {}
            w.dreaders = []
        best = {}
        out = []
        for d in deps:
            if d is op:
                continue
            if d.is_dma:
                out.append(d)
                continue
            if (not op.is_dma) and d.eng == 'pe' and op.eng == 'pe':
                continue
            if (not op.is_dma) and d.eng == op.eng and not SAME_ENGINE_SYNC:
                continue
            if d.eng not in best or best[d.eng].idx < d.idx:
                best[d.eng] = d
        out.extend(best.values())
        op.deps = out

    def op(self, eng, fn, reads=(), writes=()):
        if any(r.excl for r in reads):
            writes = list(writes) + [r for r in reads if r.excl]
            reads = [r for r in reads if not r.excl]
        o = OpRec()
        o.eng = eng
        o.fn = fn
        o.is_dma = False
        o.flagged = False
        o.count = None
        o.dres = None
        o.dval = None
        o.idx = len(self.q[eng])
        self._track(o, reads, writes)
        self.q[eng].append(o)
        return o

    def dma(self, eng, fn, sres, reads=(), writes=()):
        o = OpRec()
        o.eng = eng
        o.fn = fn
        o.is_dma = True
        o.flagged = False
        o.count = None
        o.dres = sres
        sres.dma_cnt += 16
        o.dval = sres.dma_cnt
        o.idx = len(self.q[eng])
        self._track(o, reads, writes)
        self.q[eng].append(o)
        self.all_dma.append(o)
        return o

    def barrier(self):
        deps = list(self.all_dma)
        self.all_dma = []
        for e in ENGS:
            for o in reversed(self.q[e]):
                if not o.is_dma:
                    deps.append(o)
                    break
        for e in ENGS:
            prev = self.bar[e]
            self.bar[e] = (prev or []) + deps

    def finalize(self, final_waits=()):
        nc = self.nc
        es = self.es
        for e in ENGS:
            for o in self.q[e]:
                for d in o.deps:
                    if not d.is_dma:
                        d.flagged = True
        for o in final_waits:
            if not o.is_dma:
                o.flagged = True
        esem = {}
        for e in ENGS:
            c = 0
            for o in self.q[e]:
                if o.is_dma:
                    if o.dres.sem is None:
                        o.dres.sem = es.enter_context(nc.semaphore("d_" + o.dres.name))
                elif o.flagged:
                    c += 1
                    o.count = c
            esem[e] = es.enter_context(nc.semaphore("e_" + e))

        def tok(d):
            if d.is_dma:
                return d.dres.sem, d.dval
            return esem[d.eng], d.count

        def emit(ename, eng):
            known = {}
            for o in self.q[ename]:
                need = {}
                for d in o.deps:
                    s, v = tok(d)
                    k = id(s)
                    if known.get(k, 0) >= v:
                        continue
                    if k not in need or need[k][1] < v:
                        need[k] = (s, v)
                for k, (s, v) in need.items():
                    eng.wait_ge(s, v)
                    known[k] = v
                ins = o.fn(eng)
                if o.is_dma:
                    ins.then_inc(o.dres.sem, 16)
                elif o.flagged:
                    ins.then_inc(esem[ename], 1)
            if ename == 'sp':
                for d in final_waits:
                    s, v = tok(d)
                    eng.wait_ge(s, v)

        with nc.Block() as block:
            @block.tensor
            def _(eng):
                emit('pe', eng)

            @block.scalar
            def _(eng):
                emit('act', eng)

            @block.vector
            def _(eng):
                emit('dve', eng)

            @block.gpsimd
            def _(eng):
                emit('pool', eng)

            @block.sync
            def _(eng):
                emit('sp', eng)


class CutHere(Exception):
    pass


import os
RW_CUT = int(os.environ.get("RW_CUT", "0"))


def cut(n):
    if RW_CUT == n:
        raise CutHere()


class Tile:
    __slots__ = ('ap', 'res')

    def __init__(self, ap, res):
        if not isinstance(ap, bass.AP):
            ap = ap[:]
        self.ap = ap
        self.res = res


AX = mybir.AxisListType
RW_ORDER = [18, 19] + [c for ch in range(6) for c in (ch, 6 + ch, 12 + ch)] + [20, 21]
PC_MU, PC_W0, PC_A0, PC_KK, PC_KA, PC_RK, PC_LG, PC_LB = 0, 20, 26, 32, 38, 44, 50, 56


class Builder:
    STAGES = ["l0_x1", "l0_x2", "l0_x3", "l1_x1", "l1_x2", "l1_x3"]

    def __init__(self, stop_after=None, start_at=0, dbg=(), rw_blocks=8):
        self.stop_after = stop_after
        self.start_at = start_at
        self.rw_blocks = rw_blocks
        last = self.STAGES.index(stop_after) if stop_after else 5
        self.run_stages = self.STAGES[start_at:last + 1]
        self.dbg_names = set(dbg)
        self.dbg_outs = []
        self.nc = bass.Bass("TRN2", target_bir_lowering=False)
        self.es = ExitStack()
        self.bank_rr = 0

    def sb(self, name, shape, dt):
        return self.es.enter_context(self.nc.sbuf_tensor(name, shape, dt))

    def dram_in(self, name, shape):
        return self.nc.dram_tensor(name, list(shape), F32, kind="ExternalInput").ap()

    def mm(self, out, lhsT, rhs, start, stop, reads, writes, skip=False):
        return self.P.op('pe', lambda e: e.matmul(out, lhsT=lhsT, rhs=rhs, start=start, stop=stop, skip_group_check=skip),
                         reads=reads, writes=writes)

    def tr(self, out, in_, reads, writes, start=True, stop=True):
        ident = self.IDB16.ap
        return self.P.op('pe', lambda e: e.matmul(out, lhsT=in_, rhs=ident, start=start, stop=stop),
                         reads=list(reads) + [self.IDB16.res], writes=writes)

    def act(self, out, in_, func, reads, writes, scale=None, bias=None):
        kw = {}
        if scale is not None:
            kw['scale'] = scale
        if bias is not None:
            kw['bias'] = bias
        return self.P.op('act', lambda e: e.activation(out=out, in_=in_, func=func, **kw),
                         reads=reads, writes=writes)

    def tt(self, eng, out, in0, in1, op, reads, writes):
        return self.P.op(eng, lambda e: e.tensor_tensor(out=out, in0=in0, in1=in1, op=op),
                         reads=reads, writes=writes)

    def ts(self, eng, out, in0, s1, s2, op0, op1, reads, writes):
        if op1 is None:
            return self.P.op(eng, lambda e: e.tensor_scalar(out=out, in0=in0, scalar1=s1, scalar2=None, op0=op0),
                             reads=reads, writes=writes)
        return self.P.op(eng, lambda e: e.tensor_scalar(out=out, in0=in0, scalar1=s1, scalar2=s2, op0=op0, op1=op1),
                         reads=reads, writes=writes)

    def stt(self, out, in0, scalar, in1, op0, op1, reads, writes):
        return self.P.op('dve', lambda e: e.scalar_tensor_tensor(out=out, in0=in0, scalar=scalar, in1=in1,
                                                                 op0=op0, op1=op1),
                         reads=reads, writes=writes)

    def rsqrt(self, out, in_, eps, reads, writes, eng='dve'):
        self.P.op('act', lambda e: e.activation(out=out, in_=in_, func=AF.Sqrt, bias=self.cst(eps), scale=1.0),
                  reads=list(reads) + [self.CONST.res], writes=writes)
        return self.P.op(eng, lambda e: e.reciprocal(out=out, in_=out), reads=writes, writes=writes)

    def cst(self, v, n=128, base=0):
        c = self.const_cols[v]
        return self.CONST.ap[base:base + n, c:c + 1]

    def copy(self, eng, out, in_, reads, writes):
        if eng == 'act':
            return self.P.op('act', lambda e: e.copy(out=out, in_=in_), reads=reads, writes=writes)
        return self.P.op(eng, lambda e: e.tensor_copy(out=out, in_=in_), reads=reads, writes=writes)

    def memset(self, eng, ap, v, res):
        return self.P.op(eng, lambda e: e.memset(ap, v), writes=[res])

    def dma_in(self, eng, dst, src, res):
        return self.P.dma(eng, lambda e: e.dma_start(out=dst, in_=src), res, writes=[res])

    def bank(self):
        b = self.ps[3 + self.bank_rr % 5]
        self.bank_rr += 1
        return b

    def dbg(self, name, ap, shape, res_list, dt=F32):
        if name not in self.dbg_names:
            return
        t = self.nc.dram_tensor("dbg_" + name, list(shape), dt, kind="ExternalOutput").ap()
        r = self.P.res("dbg_" + name)
        o = self.P.dma('sp', lambda e: e.dma_start(out=t, in_=ap), r, reads=res_list)
        self.dbg_outs.append(o)

    def xr(self, kind, kc, t0, T):
        rr = self.XFr if kind == 'f' else self.XBr
        return [rr[kc][b] for b in range(t0 // 256, (t0 + T + 255) // 256)]

    def w_schedule(self):
        sched = []
        for i in range(4):
            sched.append(('A', self.wmem_d[i]))
        for st in self.run_stages:
            L = int(st[1])
            if st.endswith("x2"):
                sched.extend(self.attn_w_schedule(L))
                continue
            which = 1 if st.endswith("x1") else 2
            g, u, d = self.wd[f"g{which}"], self.wd[f"u{which}"], self.wd[f"d{which}"]
            for half in range(2):
                for fc in range(NFC):
                    sched.append(('A', g[L, fc]))
                    sched.append(('A', u[L, fc]))
                for dc in range(NKC):
                    sched.append(('D', d[L, dc]))
        return sched

    def attn_w_schedule(self, L):
        s = []
        if L == 0:
            for blk in range(self.rw_blocks):
                for cc in RW_ORDER:
                    s.append(('A', self.rwin_d[cc]))
                for dc in range(NKC):
                    s.append(('A', self.wout_d[0, dc]))
        else:
            for cc in range(20):
                s.append(('A', self.sbin_d[cc]))
            for tt in range(4):
                for dc in range(NKC):
                    s.append(('A', self.wout_d[1, dc]))
        return s

    def w_init(self):
        self.NA = 6
        self.ND = 2
        self.wslots = {'A': [], 'D': []}
        for i in range(self.NA):
            t = self.sb(f"wa{i}", [128, NKC, 128], BF16)
            self.wslots['A'].append(Tile(t, self.P.res(f"wa{i}")))
        for i in range(self.ND):
            o = 23552 + i * 1408
            t = self.carve(o, 1408, BF16, "p (f d) -> p f d", f=NFC)
            self.wslots['D'].append(Tile(t, self.P.res(f"wd{i}")))
        self.wsched = self.w_schedule()
        self.w_issued = 0
        self.w_next = 0
        self.w_kcount = {'A': 0, 'D': 0}
        self.w_tile_slot = []
        self.w_ahead = {'A': 3, 'D': 1}

    def _w_issue_one(self):
        kind, src = self.wsched[self.w_issued]
        n = self.w_kcount[kind]
        self.w_kcount[kind] = n + 1
        slots = self.wslots[kind]
        t = slots[n % len(slots)]
        self.w_tile_slot.append(t)
        if kind == 'A':
            dst = t.ap[:].rearrange("p a b -> p (a b)")
            self.P.dma('pool', lambda e: e.dma_start(out=dst, in_=src), t.res, writes=[t.res])
        else:
            dst = t.ap.rearrange("p a b -> p (a b)").rearrange("p (h x) -> p h x", h=2)
            s2 = src.rearrange("p (h x) -> p h x", h=2)
            self.P.dma('pool', lambda e: e.dma_start(out=dst, in_=s2), t.res, writes=[t.res])
        self.w_issued += 1

    def w_get(self, kind):
        i = self.w_next
        assert self.wsched[i][0] == kind, (i, self.wsched[i][0], kind)
        while self.w_issued <= i:
            self._w_issue_one()
        ahead_cnt = {'A': 0, 'D': 0}
        for j in range(i + 1, self.w_issued):
            ahead_cnt[self.wsched[j][0]] += 1
        while self.w_issued < len(self.wsched):
            k = self.wsched[self.w_issued][0]
            if ahead_cnt[k] >= self.w_ahead[k]:
                break
            self._w_issue_one()
            ahead_cnt[k] += 1
        self.w_next += 1
        return self.w_tile_slot[i]

    def build(self):
        nc = self.nc
        es = self.es
        with es:
            self.P = P = Prog(nc, es)
            self.xT = self.dram_in("xT", [D, S])
            self.memT = self.dram_in("memT", [D, NMEM])
            self.outT = nc.dram_tensor("outT", [D, S], F32, kind="ExternalOutput").ap()
            self.wd = {}
            for which in (1, 2):
                self.wd[f"g{which}"] = self.dram_in(f"g{which}", [2, NFC, 128, NKC * 128])
                self.wd[f"u{which}"] = self.dram_in(f"u{which}", [2, NFC, 128, NKC * 128])
                self.wd[f"d{which}"] = self.dram_in(f"d{which}", [2, NKC, 128, NFC * 128])
            self.lng_d = self.dram_in("lng", [128, 48])
            self.lnb_d = self.dram_in("lnb", [128, 48])
            self.wout_d = self.dram_in("wout", [2, NKC, 128, NKC * 128])
            self.wmem_d = self.dram_in("wmem", [4, 128, NKC * 128])
            self.rwin_d = self.dram_in("rwin", [22, 128, NKC * 128])
            self.sbin_d = self.dram_in("sbin", [20, 128, NKC * 128])
            self.rpar_d = self.dram_in("rpar", [128, 64])
            self.lw_d = self.dram_in("lw", [128, 768])
            self.gw_d = self.dram_in("gw", [128, 768])

            self.XF = self.sb("XF", [128, NKC, S], F32)
            self.XFr = [[P.res(f"xf{k}_{t}") for t in range(8)] for k in range(NKC)]
            self.XBr = [[P.res(f"xb{k}_{t}") for t in range(8)] for k in range(NKC)]
            self.LNG = Tile(self.sb("LNG", [128, 48], F32), P.res("lng"))
            self.LNB = Tile(self.sb("LNB", [128, 48], F32), P.res("lnb"))
            self.ones_f = Tile(self.sb("ones_f", [128, 128], F32), P.res("ones_f"))
            self.ones_b = Tile(self.sb("ones_b", [128, 128], BF16), P.res("ones_b"))
            self.IDF = Tile(self.sb("IDF", [128, 128], F32), P.res("idf"))
            self.BDF = Tile(self.sb("BDF", [128, 128], F32), P.res("bdf"))
            self.IDB16 = Tile(self.sb("IDB16", [128, 128], BF16), P.res("idb16"))
            self.MK = Tile(self.sb("MK", [128, 2, NMEM], BF16), P.res("mk"))
            self.MV = Tile(self.sb("MV", [128, 2, 2, 128], BF16), P.res("mv"))
            self.ARENA_F32 = 32600
            self.arena = self.sb("arena", [128, self.ARENA_F32], F32)
            self.w_init()
            self.ps = []
            for i in range(8):
                t = es.enter_context(nc.psum_tensor(f"ps{i}", [128, 512], F32))
                self.ps.append(Tile(t, P.res(f"ps{i}")))
                self.ps[-1].res.excl = True

            self.memset('dve', self.ones_f.ap[:], 1.0, self.ones_f.res)
            self.memset('dve', self.ones_b.ap[:], 1.0, self.ones_b.res)
            self.CONST = Tile(self.sb("CONST", [128, 8], F32), P.res("const"))
            self.const_cols = {}
            for ci, cv in enumerate([4.0 * LN_EPS, LN_EPS, LNX_EPS, 1.0, 0.0]):
                self.const_cols[cv] = ci
                self.memset('dve', self.CONST.ap[:, ci:ci + 1], cv, self.CONST.res)
            P.op('pool', lambda e: e.affine_select(out=self.IDF.ap[:], in_=self.ones_f.ap[:], pattern=[[-1, 128]],
                                                   compare_op=ALU.is_equal, fill=0.0, base=0, channel_multiplier=1),
                 reads=[self.ones_f.res], writes=[self.IDF.res])
            self.copy('dve', self.IDB16.ap, self.IDF.ap, [self.IDF.res], [self.IDB16.res])
            self.memset('dve', self.BDF.ap[:], 0.0, self.BDF.res)
            self.memset('dve', self.BDF.ap[0:64, 0:64], 1.0, self.BDF.res)
            self.memset('dve', self.BDF.ap[64:128, 64:128], 1.0, self.BDF.res)

            self.dma_in('sp', self.LNG.ap[:], self.lng_d, self.LNG.res)
            self.dma_in('sp', self.LNB.ap[:], self.lnb_d, self.LNB.res)
            for kc in range(NKC):
                for t in range(8):
                    r = self.XFr[kc][t]
                    self.dma_in('sp', self.XF[:, kc, t * 256:(t + 1) * 256],
                                self.xT[kc * 128:(kc + 1) * 128, t * 256:(t + 1) * 256], r)

            self.arena_ffn()
            self.mem_setup()
            for kc in range(NKC):
                for t in range(4):
                    sl = slice(t * 512, (t + 1) * 512)
                    self.copy('dve' if (kc + t) % 2 == 0 else 'act', self.XB[:, kc, sl], self.XF[:, kc, sl],
                              reads=self.xr('f', kc, t * 512, 512), writes=self.xr('b', kc, t * 512, 512))

            for st in self.run_stages:
                L = int(st[1])
                if st.endswith("x1"):
                    self.ffn(L, 1)
                elif st.endswith("x3"):
                    self.ffn(L, 2)
                else:
                    P.barrier()
                    if L == 0:
                        try:
                            self.rwkv_stage()
                        except CutHere:
                            pass
                    else:
                        self.sb_stage()
                    P.barrier()
                    self.arena_ffn()
                    if L == 1:
                        for kc in range(NKC):
                            for t in range(4):
                                sl = slice(t * 512, (t + 1) * 512)
                                self.copy('dve' if (kc + t) % 2 == 0 else 'act', self.XB[:, kc, sl], self.XF[:, kc, sl],
                                          reads=self.xr('f', kc, t * 512, 512), writes=self.xr('b', kc, t * 512, 512))

            finals = list(self.dbg_outs)
            for kc in range(NKC):
                for t in range(4):
                    rl = self.xr('f', kc, t * 512, 512)
                    src = self.XF[:, kc, t * 512:(t + 1) * 512]
                    dst = self.outT[kc * 128:(kc + 1) * 128, t * 512:(t + 1) * 512]
                    o = P.dma('sp', (lambda dst, src: lambda e: e.dma_start(out=dst, in_=src))(dst, src), rl[0], reads=rl)
                    finals.append(o)
            P.finalize(final_waits=finals)
        return nc

    def carve(self, off_words, nwords, dt, pattern=None, **kw):
        ap = self.arena[:, off_words:off_words + nwords]
        if dt == BF16:
            ap = ap.bitcast(BF16)
        if pattern:
            ap = ap.rearrange(pattern, **kw)
        return ap

    def alloc(self, nelem, dt, name, pattern=None, **kw):
        nwords = nelem if dt == F32 else (nelem + 1) // 2
        ap = self.carve(self.aoff, nwords, dt, pattern, **kw)
        self.aoff += nwords
        assert self.aoff <= self.ARENA_F32, (name, self.aoff)
        return Tile(ap, self.P.res(name))

    def arena_ffn(self):
        P = self.P
        o = 0
        self.XB = self.carve(o, NKC * S // 2, BF16, "p (k t) -> p k t", k=NKC)
        o += NKC * S // 2
        self.HT = self.carve(o, NFC * 1024 // 2, BF16, "p (f t) -> p f t", f=NFC)
        o += NFC * 1024 // 2
        self.HTr = [[P.res(f"ht{f}_{t}") for t in range(2)] for f in range(NFC)]
        self.tmpf = []
        for i in range(8):
            self.tmpf.append(Tile(self.carve(o, 512, F32), P.res(f"tmpf{i}")))
            o += 512
        assert o <= self.ARENA_F32, o

    def mem_setup(self):
        P = self.P
        MT = self.HT[:, 0:4, :].rearrange("p a b -> p (a b)")[:, 0:NKC * NMEM].rearrange("p (k m) -> p k m", k=NKC)
        mtr = P.res("mt")
        for kc in range(NKC):
            P.dma('pool', (lambda kc: lambda e: e.dma_start(out=MT[:, kc, :], in_=self.memT[kc * 128:(kc + 1) * 128, :]))(kc),
                  mtr, writes=[mtr])
        for c in range(2):
            w = self.w_get('A')
            b = self.bank()
            for kc in range(NKC):
                self.mm(b.ap[:, 0:NMEM], w.ap[:, kc, :], MT[:, kc, :], kc == 0, kc == NKC - 1, [w.res, mtr], [b.res])
            self.copy('act', self.MK.ap[:, c, :], b.ap[:, 0:NMEM], [b.res], [self.MK.res])
        for c in range(2):
            w = self.w_get('A')
            b = self.bank()
            for mt in range(2):
                for kc in range(NKC):
                    self.mm(b.ap[:, mt * 128:(mt + 1) * 128], MT[:, kc, mt * 128:(mt + 1) * 128], w.ap[:, kc, :],
                            kc == 0, kc == NKC - 1, [w.res, mtr], [b.res])
            for mt in range(2):
                self.copy('act', self.MV.ap[:, mt, c, :], b.ap[:, mt * 128:(mt + 1) * 128], [b.res], [self.MV.res])

    def ffn(self, L, which):
        XF, XB, HT, ps = self.XF, self.XB, self.HT, self.ps
        lncol = (L * 3 + (0 if which == 1 else 2)) * 8
        it = 0
        for half in range(2):
            t0 = half * 1024
            for fc in range(NFC):
                wg = self.w_get('A')
                wu = self.w_get('A')
                for tt in range(2):
                    ts0 = t0 + tt * 512
                    tok = slice(ts0, ts0 + 512)
                    bg = ps[(it % 2) * 2]
                    bu = ps[(it % 2) * 2 + 1]
                    for kc in range(NKC):
                        self.mm(bg.ap[:], wg.ap[:, kc, :], XB[:, kc, tok], kc == 0, kc == NKC - 1,
                                [wg.res] + self.xr('b', kc, ts0, 512), [bg.res])
                    for kc in range(NKC):
                        self.mm(bu.ap[:], wu.ap[:, kc, :], XB[:, kc, tok], kc == 0, kc == NKC - 1,
                                [wu.res] + self.xr('b', kc, ts0, 512), [bu.res])
                    st = self.tmpf[it % 2]
                    self.act(st.ap, bg.ap[:], AF.Silu, [bg.res], [st.res])
                    self.tt('dve', HT[:, fc, tt * 512:(tt + 1) * 512], st.ap, bu.ap[:], ALU.mult,
                            [st.res, bu.res], [self.HTr[fc][tt]])
                    it += 1
            sbk = [(ps[2], ps[3]), (ps[0], ps[1])]
            for dc in range(NKC):
                wd = self.w_get('D')
                for tt in range(2):
                    ts0 = t0 + tt * 512
                    tok = slice(ts0, ts0 + 512)
                    by = ps[4 + (it % 2)]
                    for fc in range(NFC):
                        self.mm(by.ap[:], wd.ap[:, fc, :], HT[:, fc, tt * 512:(tt + 1) * 512], fc == 0, fc == NFC - 1,
                                [wd.res, self.HTr[fc][tt]], [by.res])
                    s1, s2 = sbk[tt]
                    self.resid_stats(dc, ts0, 512, by, 2.0 * ALPHA, s1, s2, self.tmpf[2 + it % 2])
                    it += 1
            for tt in range(2):
                s1, s2 = sbk[tt]
                self.ln_finish(t0 + tt * 512, 512, s1, s2, 4.0 * LN_EPS, lncol, self.tmpf)

    def resid_stats(self, dc, t0, T, by, xscale, s1, s2, sq):
        XF = self.XF
        tok = slice(t0, t0 + T)
        xr = self.xr('f', dc, t0, T)
        self.stt(XF[:, dc, tok], XF[:, dc, tok], xscale, by.ap[:, 0:T], ALU.mult, ALU.add, xr + [by.res], xr)
        self.act(sq.ap[:, 0:T], XF[:, dc, tok], AF.Square, xr, [sq.res])
        self.mm(s1.ap[:, 0:T], self.ones_f.ap[:], XF[:, dc, tok], dc == 0, dc == NKC - 1,
                [self.ones_f.res] + xr, [s1.res])
        self.mm(s2.ap[:, 0:T], self.ones_f.ap[:], sq.ap[:, 0:T], dc == 0, dc == NKC - 1,
                [self.ones_f.res, sq.res], [s2.res])

    def ln_finish(self, t0, T, s1, s2, eps, lncol, tmps, write_xb=True):
        XF, XB = self.XF, self.XB
        tok = slice(t0, t0 + T)
        mean, msq, rstd, mr = tmps[4], tmps[5], tmps[6], tmps[7]
        w = slice(0, T)
        self.act(mean.ap[:, w], s1.ap[:, w], AF.Copy, [s1.res], [mean.res], scale=1.0 / D)
        self.tt('dve', msq.ap[:, w], mean.ap[:, w], mean.ap[:, w], ALU.mult, [mean.res], [msq.res])
        self.stt(msq.ap[:, w], s2.ap[:, w], 1.0 / D, msq.ap[:, w], ALU.mult, ALU.subtract, [s2.res, msq.res], [msq.res])
        self.rsqrt(rstd.ap[:, w], msq.ap[:, w], eps, [msq.res], [rstd.res])
        self.tt('dve', mr.ap[:, w], mean.ap[:, w], rstd.ap[:, w], ALU.mult, [mean.res, rstd.res], [mr.res])
        for dc in range(NKC):
            xr = self.xr('f', dc, t0, T)
            u = tmps[dc % 2]
            self.tt('dve', u.ap[:, w], XF[:, dc, tok], rstd.ap[:, w], ALU.mult, xr + [rstd.res], [u.res])
            self.tt('dve', u.ap[:, w], u.ap[:, w], mr.ap[:, w], ALU.subtract, [u.res, mr.res], [u.res])
            g = self.LNG.ap[:, lncol + dc:lncol + dc + 1]
            b = self.LNB.ap[:, lncol + dc:lncol + dc + 1]
            self.act(XF[:, dc, tok], u.ap[:, w], AF.Identity, [u.res, self.LNG.res, self.LNB.res], xr, scale=g, bias=b)
            if write_xb:
                self.act(XB[:, dc, tok], u.ap[:, w], AF.Identity, [u.res, self.LNG.res, self.LNB.res],
                         self.xr('b', dc, t0, T), scale=g, bias=b)

    def mem_attn(self, MQ, mq_res, HD, hd_res, t0q, T, t0h, E):
        for h in range(4):
            cq, hp = h // 2, h % 2
            pr = slice(hp * 64, (hp + 1) * 64)
            for mt in range(2):
                b = self.bank()
                self.mm(b.ap[:, 0:T], self.MK.ap[pr, cq, mt * 128:(mt + 1) * 128], MQ[pr, cq, t0q:t0q + T], True, True,
                        [self.MK.res, mq_res], [b.res])
                self.act(E[mt].ap[:, 0:T], b.ap[:, 0:T], AF.Exp, [b.res], [E[mt].res], scale=0.125)
            bn = self.bank()
            bd = self.bank()
            for mt in range(2):
                self.mm(bn.ap[pr, 0:T], self.MV.ap[:, mt, cq, hp * 64:(hp + 1) * 64], E[mt].ap[:, 0:T], mt == 0, mt == 1,
                        [self.MV.res, E[mt].res], [bn.res])
            for mt in range(2):
                self.mm(bd.ap[pr, 0:T], self.ones_b.ap[:, 0:64], E[mt].ap[:, 0:T], mt == 0, mt == 1,
                        [self.ones_b.res, E[mt].res], [bd.res])
            rd = E[2]
            self.P.op('dve', (lambda o, i: lambda e: e.reciprocal(out=o, in_=i))(rd.ap[pr, 0:T], bd.ap[pr, 0:T]),
                      reads=[bd.res], writes=[rd.res])
            self.tt('dve', HD[pr, 6 + cq, t0h:t0h + T], bn.ap[pr, 0:T], rd.ap[pr, 0:T], ALU.mult,
                    [bn.res, rd.res], [hd_res])

    def out_proj_ln(self, L, HD, hd_res, t0h, t0, T, tmps):
        lncol = (L * 3 + 1) * 8
        s1, s2 = self.ps[0], self.ps[1]
        for dc in range(NKC):
            wo = self.w_get('A')
            by = self.bank()
            for cch in range(NKC):
                self.mm(by.ap[:, 0:T], wo.ap[:, cch, :], HD[:, cch, t0h:t0h + T], cch == 0, cch == NKC - 1,
                        [wo.res, hd_res], [by.res])
            self.resid_stats(dc, t0, T, by, ALPHA, s1, s2, tmps[2 + dc % 2])
        self.ln_finish(t0, T, s1, s2, LN_EPS, lncol, tmps)
    def rwkv_stage(self):
        P = self.P
        XB = self.XB
        T = 256
        self.aoff = NKC * S // 2
        A = self.alloc
        RP = A(64, F32, "rpar")
        OM = A(32, F32, "om")
        LW = A(768, BF16, "lw")
        GW = A(768, BF16, "gw")
        SM = A(T, F32, "scanmask")
        MLT = A(64, F32, "mlt")
        MLE = A(64, F32, "mle")
        MGT = A(64, F32, "mgt")
        IDB = A(64, F32, "idb")
        CAR = A(20, F32, "carry")
        HF = A(6 * 64, F32, "hf", "p (c i) -> p c i", c=6)
        HB = A(6 * 64, BF16, "hb", "p (c i) -> p c i", c=6)
        PR = [A(T + 1, F32, f"praw{i}") for i in range(3)]
        PL = PR[0:2]
        TD = A(T, BF16, "td")
        SDG = A(T, BF16, "sdg")
        NT = 14
        tm = [A(T, F32, f"rt{i}") for i in range(NT)]
        tb = [A(T, BF16, f"rtb{i}") for i in range(3)]
        ONH = A(768, BF16, "ONH")
        ONL = A(768, BF16, "ONL")
        At = A(6 * T, BF16, "At", "p (c t) -> p c t", c=6)
        Bt = A(6 * T, BF16, "Bt", "p (c t) -> p c t", c=6)
        Kt = A(6 * T, BF16, "Kt", "p (c t) -> p c t", c=6)
        Rt = A(6 * T, BF16, "Rt", "p (c t) -> p c t", c=6)
        Bh = A(2 * 768, BF16, "Bh", "p (r c) -> p r c", r=2)
        Kh = A(2 * 768, BF16, "Kh", "p (r c) -> p r c", r=2)
        Vt = A(2 * 768, BF16, "Vt", "p (r c) -> p r c", r=2)
        G = A(6 * T, F32, "G", "p (c t) -> p c t", c=6)
        BON = A(6 * T, F32, "BON", "p (c t) -> p c t", c=6)
        GC = A(6 * 4, F32, "gC", "p (c k) -> p c k", c=6)
        MX = [[A(6 * 64, BF16, f"X{b}{g}", "p (h t) -> p h t", h=6) for g in range(2)] for b in range(2)]
        MY = [[A(6 * 64, BF16, f"Y{b}{g}", "p (h t) -> p h t", h=6) for g in range(2)] for b in range(2)]
        ZT = [A(6 * 64, BF16, f"ZT{g}", "p (h t) -> p h t", h=6) for g in range(2)]
        AK = [A(6 * 64, BF16, f"AK{g}", "p (h t) -> p h t", h=6) for g in range(2)]
        RB = [A(6 * 64, BF16, f"RB{g}", "p (h t) -> p h t", h=6) for g in range(2)]
        RK = [A(6 * 64, BF16, f"RK{g}", "p (h t) -> p h t", h=6) for g in range(2)]
        W0 = [A(6 * 64, BF16, f"W0{g}", "p (h t) -> p h t", h=6) for g in range(2)]
        WB = [A(6 * 64, BF16, f"WB{g}", "p (h t) -> p h t", h=6) for g in range(2)]
        UB = [A(6 * 64, BF16, f"UB{g}", "p (h t) -> p h t", h=6) for g in range(2)]
        OT = [A(6 * 64, F32, f"OT{g}", "p (h t) -> p h t", h=6) for g in range(2)]
        OO = A(768, F32, "OO", "p (h t) -> p h t", h=12)
        ON = A(768, F32, "ON", "p (h t) -> p h t", h=12)
        SQ = ON
        ST = [A(12, F32, f"gst{i}") for i in range(4)]
        HD = A(NKC * T, BF16, "HD", "p (c t) -> p c t", c=NKC)
        MQ = A(2 * T, BF16, "MQ", "p (c t) -> p c t", c=2)
        E = [TD, SDG, tm[13]]
        lt = tm[0:8]

        self.dma_in('sp', RP.ap, self.rpar_d, RP.res)
        self.dma_in('pool', LW.ap, self.lw_d, LW.res)
        self.dma_in('pool', GW.ap, self.gw_d, GW.res)
        self.ts('dve', OM.ap[:, 0:20], RP.ap[:, PC_MU:PC_MU + 20], -1.0, 1.0, ALU.mult, ALU.add, [RP.res], [OM.res])
        self.ts('dve', OM.ap[:, 20:26], RP.ap[:, PC_KA:PC_KA + 6], -1.0, 1.0, ALU.mult, ALU.add, [RP.res], [OM.res])
        self.memset('dve', SM.ap, 1.0, SM.res)
        self.memset('dve', SM.ap[:, 0:T:64], 0.0, SM.res)
        self.memset('dve', CAR.ap, 0.0, CAR.res)
        self.memset('dve', HF.ap, 0.0, HF.res)
        self.memset('dve', HB.ap, 0.0, HB.res)
        ones = self.ones_f

        def asel(dst, pattern, cmul, op, base=0):
            for hp in range(2):
                pr = slice(hp * 64, (hp + 1) * 64)
                P.op('pool', (lambda pr: lambda e: e.affine_select(
                    out=dst.ap[pr, :], in_=ones.ap[pr, 0:64], pattern=pattern, compare_op=op, fill=0.0,
                    base=base, channel_multiplier=cmul))(pr), reads=[ones.res], writes=[dst.res])
        asel(MLT, [[1, 64]], -1, ALU.is_gt)
        asel(MLE, [[1, 64]], -1, ALU.is_ge)
        asel(MGT, [[-1, 64]], 1, ALU.is_gt)
        asel(IDB, [[-1, 64]], 1, ALU.is_equal)

        def bc6(m):
            return m.ap.unsqueeze(1).to_broadcast([128, 6, 64])
        cut(1)

        rpc = lambda c: RP.ap[:, c:c + 1]

        for blk in range(self.rw_blocks):
            t0 = blk * T
            tokb = slice(t0, t0 + T)
            wt = {}

            def inproj(cc, dst_tile, shifted):
                w = self.w_get('A')
                b = self.bank()
                for kc in range(NKC):
                    self.mm(b.ap[:, 0:T], w.ap[:, kc, :], XB[:, kc, tokb], kc == 0, kc == NKC - 1,
                            [w.res] + self.xr('b', kc, t0, T), [b.res])
                if shifted:
                    self.copy('pool', dst_tile.ap[:, 0:1], CAR.ap[:, cc:cc + 1], [CAR.res], [dst_tile.res])
                    self.copy('act', dst_tile.ap[:, 1:T + 1], b.ap[:, 0:T], [b.res], [dst_tile.res])
                    self.copy('pool', CAR.ap[:, cc:cc + 1], dst_tile.ap[:, T:T + 1], [dst_tile.res], [CAR.res])

            def shift(praw, cc, out):
                self.ts('pool', out.ap, praw.ap[:, 0:T], rpc(PC_MU + cc), None, ALU.mult, None, [praw.res, RP.res], [out.res])
                self.stt(out.ap, praw.ap[:, 1:T + 1], OM.ap[:, cc:cc + 1], out.ap, ALU.mult, ALU.add,
                         [praw.res, OM.res, out.res], [out.res])

            inproj(18, PL[0], True)
            inproj(19, PL[1], True)
            zl0, zl1 = tm[0], tm[1]
            shift(PL[0], 18, zl0)
            shift(PL[1], 19, zl1)
            self.act(TD.ap[0:64, :], zl0.ap[0:64, :], AF.Tanh, [zl0.res], [TD.res])
            self.copy('dve', TD.ap[64:128, :], zl0.ap[64:128, :], [zl0.res], [TD.res])
            self.act(SDG.ap, zl1.ap, AF.Sigmoid, [zl1.res], [SDG.res])
            cut(2)

            for ch in range(6):
                cs = slice(ch * 128, (ch + 1) * 128)
                inproj(ch, PR[0], True)
                inproj(6 + ch, PR[1], True)
                inproj(12 + ch, PR[2], True)
                zr, zk, zv = tm[2], tm[3], tm[4]
                shift(PR[0], ch, zr)
                shift(PR[1], 6 + ch, zk)
                shift(PR[2], 12 + ch, zv)
                bw, ba, bg_ = self.bank(), self.bank(), self.bank()
                self.mm(bw.ap[:, 0:T], LW.ap[0:64, cs], TD.ap[0:64, :], True, True, [LW.res, TD.res], [bw.res])
                self.mm(ba.ap[:, 0:T], LW.ap[64:128, cs], TD.ap[64:128, :], True, True, [LW.res, TD.res], [ba.res])
                self.mm(bg_.ap[:, 0:T], GW.ap[:, cs], SDG.ap, True, True, [GW.res, SDG.res], [bg_.res])
                sg, a = tm[5], tm[6]
                self.act(sg.ap, bw.ap[:, 0:T], AF.Sigmoid, [bw.res, RP.res], [sg.res], bias=rpc(PC_W0 + ch), scale=1.0)
                self.act(a.ap, ba.ap[:, 0:T], AF.Sigmoid, [ba.res, RP.res], [a.res], bias=rpc(PC_A0 + ch), scale=1.0)
                self.copy('act', G.ap[:, ch, :], bg_.ap[:, 0:T], [bg_.res], [G.res])
                cut(31)
                kk, t1 = tm[7], tm[8]
                self.ts('pool', kk.ap, zk.ap, rpc(PC_KK + ch), None, ALU.mult, None, [zk.res, RP.res], [kk.res])
                self.act(t1.ap, kk.ap, AF.Square, [kk.res], [t1.res])
                bs = self.bank()
                self.mm(bs.ap[:, 0:T], self.BDF.ap, t1.ap, True, True, [self.BDF.res, t1.res], [bs.res])
                self.ts('dve', t1.ap, bs.ap[:, 0:T], 1e-24, None, ALU.max, None, [bs.res], [t1.res])
                self.rsqrt(t1.ap, t1.ap, 0.0, [t1.res], [t1.res])
                self.tt('dve', kk.ap, kk.ap, t1.ap, ALU.mult, [kk.res, t1.res], [kk.res])
                cut(32)
                k, bb = tm[9], tm[10]
                self.ts('pool', t1.ap, a.ap, rpc(PC_KA + ch), OM.ap[:, 20 + ch:21 + ch], ALU.mult, ALU.add,
                        [a.res, RP.res, OM.res], [t1.res])
                self.tt('dve', k.ap, zk.ap, t1.ap, ALU.mult, [zk.res, t1.res], [k.res])
                self.tt('pool', bb.ap, kk.ap, a.ap, ALU.mult, [kk.res, a.res], [bb.res])
                Lp, e, t2 = tm[11], tm[12], tm[13]
                P.op('dve', lambda e_: e_.tensor_tensor_scan(out=Lp.ap, data0=SM.ap, data1=sg.ap, initial=0.0,
                                                             op0=ALU.mult, op1=ALU.add),
                     reads=[SM.res, sg.res], writes=[Lp.res])
                Lp4 = Lp.ap.rearrange("p (c t) -> p c t", c=4)
                cut(33)
                self.act(e.ap, Lp.ap, AF.Exp, [Lp.res], [e.res], scale=DECAY_C)
                self.copy('dve', GC.ap[:, ch, :], e.ap[:, 63:T:64], [e.res], [GC.res])
                self.tt('dve', Rt.ap[:, ch, :], zr.ap, e.ap, ALU.mult, [zr.res, e.res], [Rt.res])
                self.act(e.ap, Lp.ap, AF.Exp, [Lp.res], [e.res], scale=-DECAY_C)
                self.tt('dve', Bt.ap[:, ch, :], bb.ap, e.ap, ALU.mult, [bb.res, e.res], [Bt.res])
                self.tt('pool', Kt.ap[:, ch, :], k.ap, e.ap, ALU.mult, [k.res, e.res], [Kt.res])
                self.tt('pool', t2.ap, Lp.ap, sg.ap, ALU.subtract, [Lp.res, sg.res], [t2.res])
                self.act(e.ap, t2.ap, AF.Exp, [t2.res], [e.res], scale=DECAY_C)
                self.stt(At.ap[:, ch, :], kk.ap, -1.0, e.ap, ALU.mult, ALU.mult, [kk.res, e.res], [At.res])
                t24 = t2.ap.rearrange("p (c t) -> p c t", c=4)
                self.tt('dve', t24, Lp4[:, :, 63:64].to_broadcast([128, 4, 64]), Lp4, ALU.subtract, [Lp.res], [t2.res])
                self.act(e.ap, t2.ap, AF.Exp, [t2.res], [e.res], scale=DECAY_C)
                self.tt('dve', tb[0].ap, bb.ap, e.ap, ALU.mult, [bb.res, e.res], [tb[0].res])
                self.tt('pool', tb[1].ap, k.ap, e.ap, ALU.mult, [k.res, e.res], [tb[1].res])
                self.copy('pool', tb[2].ap, zv.ap, [zv.res], [tb[2].res])
                cut(34)
                for pr_ in range(2):
                    ps_ = slice(pr_ * 128, (pr_ + 1) * 128)
                    bt = self.bank()
                    self.tr(bt.ap[:, 0:128], tb[0].ap[:, ps_], [tb[0].res], [bt.res])
                    self.tr(bt.ap[:, 128:256], tb[1].ap[:, ps_], [tb[1].res], [bt.res])
                    self.tr(bt.ap[:, 256:384], tb[2].ap[:, ps_], [tb[2].res], [bt.res])
                    cut(341)
                    self.copy('act', Bh.ap[:, pr_, cs], bt.ap[:, 0:128], [bt.res], [Bh.res])
                    cut(342)
                    self.copy('dve', Kh.ap[:, pr_, cs], bt.ap[:, 128:256], [bt.res], [Kh.res])
                    cut(343)
                    self.copy('act', Vt.ap[:, pr_, cs], bt.ap[:, 256:384], [bt.res], [Vt.res])
                cut(35)
                self.tt('pool', t1.ap, zr.ap, k.ap, ALU.mult, [zr.res, k.res], [t1.res])
                self.ts('pool', t1.ap, t1.ap, rpc(PC_RK + ch), None, ALU.mult, None, [t1.res, RP.res], [t1.res])
                bs2 = self.bank()
                self.mm(bs2.ap[:, 0:T], self.BDF.ap, t1.ap, True, True, [self.BDF.res, t1.res], [bs2.res])
                self.tt('dve', BON.ap[:, ch, :], bs2.ap[:, 0:T], zv.ap, ALU.mult, [bs2.res, zv.res], [BON.res])
                cut(3)

            for c2 in range(2):
                w = self.w_get('A')
                b = self.bank()
                for kc in range(NKC):
                    self.mm(b.ap[:, 0:T], w.ap[:, kc, :], XB[:, kc, tokb], kc == 0, kc == NKC - 1,
                            [w.res] + self.xr('b', kc, t0, T), [b.res])
                self.copy('act', MQ.ap[:, c2, :], b.ap[:, 0:T], [b.res], [MQ.res])

            for pr_ in range(2):
                def hidx(hg, hh):
                    h = 2 * hh + hg
                    return h, hh, slice(hg * 64, hg * 64 + 64)

                def tsl(c2):
                    o = pr_ * 128 + c2 * 64
                    return slice(o, o + 64)

                def grp(dst, lhs, rhs, mask, extra=None):
                    for hg in range(2):
                        b = self.bank()
                        for c2 in range(2):
                            for hh in range(6):
                                h, ch, hp = hidx(hg, hh)
                                self.mm(b.ap[c2 * 64:(c2 + 1) * 64, hh * 64:(hh + 1) * 64],
                                        lhs.ap[hp, ch, tsl(c2)], rhs.ap[hp, ch, tsl(c2)], True, True,
                                        [lhs.res, rhs.res], [b.res])
                        bv = b.ap[:, 0:384].rearrange("p (h t) -> p h t", h=6)
                        self.tt('dve', dst[hg].ap, bv, bc6(mask), ALU.mult, [b.res, mask.res], [dst[hg].res])

                grp(MX[0], Bt, At, MLT)
                grp(MY[0], At, Bt, MGT)
                grp(AK, Kt, At, MLT)
                grp(RB, Bt, Rt, MLE)
                grp(RK, Kt, Rt, MLE)
                cut(4)
                for hg in range(2):
                    self.tt('pool', ZT[hg].ap, MX[0][hg].ap, bc6(IDB), ALU.add, [MX[0][hg].res, IDB.res], [ZT[hg].res])

                def grp2(dst, lhs, rhs, hg, accum_into=None, evac='act'):
                    for c2 in range(2):
                        b = self.bank()
                        pc = slice(c2 * 64, (c2 + 1) * 64)
                        for hh in range(6):
                            self.mm(b.ap[pc, hh * 64:(hh + 1) * 64], lhs.ap[pc, hh, :], rhs.ap[pc, hh, :], True, True,
                                    [lhs.res, rhs.res], [b.res])
                        bv = b.ap[pc, 0:384].rearrange("p (h t) -> p h t", h=6)
                        if accum_into is not None:
                            self.tt('dve', accum_into.ap[pc], bv, accum_into.ap[pc], ALU.add, [b.res, accum_into.res],
                                    [accum_into.res])
                        else:
                            self.copy(evac if c2 == 0 else ('dve' if evac == 'act' else 'act'), dst.ap[pc], bv, [b.res], [dst.res])

                cur = 0
                for kstep in range(6):
                    for hg in range(2):
                        Xc, Yc = MX[cur][hg], MY[cur][hg]
                        if kstep >= 1:
                            grp2(None, Yc, ZT[hg], hg, accum_into=ZT[hg])
                        if kstep <= 3:
                            grp2(MX[1 - cur][hg], Yc, Xc, hg, evac='act')
                        if kstep <= 4:
                            grp2(MY[1 - cur][hg], Xc, Yc, hg, evac='dve')
                    cur = 1 - cur
                for hg in range(2):
                    for c2 in range(2):
                        b = self.bank()
                        pc = slice(c2 * 64, (c2 + 1) * 64)
                        for hh in range(6):
                            h = 2 * hh + hg
                            self.mm(b.ap[pc, hh * 64:(hh + 1) * 64], AK[hg].ap[pc, hh, :], Vt.ap[pc, pr_, h * 64:(h + 1) * 64],
                                    True, True, [AK[hg].res, Vt.res], [b.res])
                        self.copy('act' if c2 == 0 else 'dve', W0[hg].ap[pc],
                                  b.ap[pc, 0:384].rearrange("p (h t) -> p h t", h=6), [b.res], [W0[hg].res])

                cut(5)
                ps = self.ps
                for c2 in range(2):
                    pc = slice(c2 * 64, (c2 + 1) * 64)
                    cglob = pr_ * 2 + c2
                    for hg in range(2):
                        b = ps[hg]
                        for hh in range(6):
                            h, ch, hp = hidx(hg, hh)
                            self.mm(b.ap[pc, hh * 64:(hh + 1) * 64], At.ap[hp, ch, tsl(c2)], HB.ap[hp, ch, :], True, True,
                                    [At.res, HB.res], [b.res])
                        self.tt('dve', WB[hg].ap[pc], b.ap[pc, 0:384].rearrange("p (h t) -> p h t", h=6), W0[hg].ap[pc],
                                ALU.add, [b.res, W0[hg].res], [WB[hg].res])
                    for hg in range(2):
                        b = ps[3 + hg]
                        for hh in range(6):
                            h, ch, hp = hidx(hg, hh)
                            self.mm(b.ap[pc, hh * 64:(hh + 1) * 64], Rt.ap[hp, ch, tsl(c2)], HB.ap[hp, ch, :], True, True,
                                    [Rt.res, HB.res], [b.res])
                        self.copy('act', OT[hg].ap[pc], b.ap[pc, 0:384].rearrange("p (h t) -> p h t", h=6), [b.res], [OT[hg].res])
                    for hg in range(2):
                        b = ps[hg]
                        for hh in range(6):
                            self.mm(b.ap[pc, hh * 64:(hh + 1) * 64], ZT[hg].ap[pc, hh, :], WB[hg].ap[pc, hh, :], True, True,
                                    [ZT[hg].res, WB[hg].res], [b.res])
                        self.copy('act', UB[hg].ap[pc], b.ap[pc, 0:384].rearrange("p (h t) -> p h t", h=6), [b.res], [UB[hg].res])
                    bH = ps[2]
                    for hg in range(2):
                        for hh in range(6):
                            h, ch, hp = hidx(hg, hh)
                            hs = slice(h * 64, (h + 1) * 64)
                            self.mm(bH.ap[hp, ch * 64:(ch + 1) * 64], Bh.ap[pc, pr_, hs], UB[hg].ap[pc, hh, :], True, False,
                                    [Bh.res, UB[hg].res], [bH.res])
                            self.mm(bH.ap[hp, ch * 64:(ch + 1) * 64], Kh.ap[pc, pr_, hs], Vt.ap[pc, pr_, hs], False, True,
                                    [Kh.res, Vt.res], [bH.res])
                    for hg in range(2):
                        b = ps[5 + hg]
                        for hh in range(6):
                            h = 2 * hh + hg
                            hs = slice(h * 64, (h + 1) * 64)
                            self.mm(b.ap[pc, hh * 64:(hh + 1) * 64], RB[hg].ap[pc, hh, :], UB[hg].ap[pc, hh, :], True, False,
                                    [RB[hg].res, UB[hg].res], [b.res])
                            self.mm(b.ap[pc, hh * 64:(hh + 1) * 64], RK[hg].ap[pc, hh, :], Vt.ap[pc, pr_, hs], False, True,
                                    [RK[hg].res, Vt.res], [b.res])
                        self.tt('dve', OO.ap.rearrange("p (c g) t -> p c g t", g=2)[pc, :, hg, :],
                                b.ap[pc, 0:384].rearrange("p (h t) -> p h t", h=6),
                                OT[hg].ap[pc], ALU.add, [b.res, OT[hg].res], [OO.res])
                    gcb = GC.ap[:, :, cglob:cglob + 1].to_broadcast([128, 6, 64])
                    self.tt('dve', HF.ap, HF.ap, gcb, ALU.mult, [HF.res, GC.res], [HF.res])
                    self.tt('dve', HF.ap, HF.ap, bH.ap[:, 0:384].rearrange("p (c i) -> p c i", c=6), ALU.add,
                            [HF.res, bH.res], [HF.res])
                    self.copy('act', HB.ap, HF.ap, [HF.res], [HB.res])

                cut(6)
                s1, s2, mean, rstd = ST
                P.op('dve', lambda e_: e_.tensor_reduce(out=s1.ap, in_=OO.ap, axis=AX.X, op=ALU.add),
                     reads=[OO.res], writes=[s1.res])
                self.act(SQ.ap, OO.ap, AF.Square, [OO.res], [SQ.res])
                P.op('dve', lambda e_: e_.tensor_reduce(out=s2.ap, in_=SQ.ap, axis=AX.X, op=ALU.add),
                     reads=[SQ.res], writes=[s2.res])
                self.ts('dve', mean.ap, s1.ap, 1.0 / 64, None, ALU.mult, None, [s1.res], [mean.res])
                self.tt('dve', s1.ap, mean.ap, mean.ap, ALU.mult, [mean.res], [s1.res])
                self.stt(s2.ap, s2.ap, 1.0 / 64, s1.ap, ALU.mult, ALU.subtract, [s2.res, s1.res], [s2.res])
                self.rsqrt(rstd.ap, s2.ap, LNX_EPS, [s2.res], [rstd.res])
                self.tt('dve', ON.ap, OO.ap, mean.ap.unsqueeze(2).to_broadcast([128, 12, 64]), ALU.subtract,
                        [OO.res, mean.res], [ON.res])
                self.tt('dve', ON.ap, ON.ap, rstd.ap.unsqueeze(2).to_broadcast([128, 12, 64]), ALU.mult,
                        [ON.res, rstd.res], [ON.res])
                ONf = ON.ap.rearrange("p h t -> p (h t)")
                self.copy('act', ONH.ap, ONf, [ON.res], [ONH.res])
                self.tt('dve', ONL.ap, ONf, ONH.ap, ALU.subtract, [ON.res, ONH.res], [ONL.res])
                for half in range(2):
                    bt = self.bank()
                    for j in range(3):
                        ch = half * 3 + j
                        self.tr(bt.ap[:, j * 128:(j + 1) * 128], ONH.ap[:, ch * 128:(ch + 1) * 128], [ONH.res], [bt.res],
                                True, False)
                        self.tr(bt.ap[:, j * 128:(j + 1) * 128], ONL.ap[:, ch * 128:(ch + 1) * 128], [ONL.res], [bt.res],
                                False, True)
                    for j in range(3):
                        ch = half * 3 + j
                        y = tm[j]
                        bsl = slice(pr_ * 128, (pr_ + 1) * 128)
                        self.act(y.ap[:, 0:128], bt.ap[:, j * 128:(j + 1) * 128], AF.Identity, [bt.res, RP.res], [y.res],
                                 scale=rpc(PC_LG + ch), bias=rpc(PC_LB + ch))
                        self.tt('dve', y.ap[:, 0:128], y.ap[:, 0:128], BON.ap[:, ch, bsl], ALU.add, [y.res, BON.res], [y.res])
                        self.tt('dve', HD.ap[:, ch, bsl], y.ap[:, 0:128], G.ap[:, ch, bsl], ALU.mult, [y.res, G.res], [HD.res])

            cut(7)
            self.mem_attn(MQ.ap, MQ.res, HD.ap, HD.res, 0, T, 0, E)
            self.dbg(f"hd{blk}", HD.ap, [128, NKC, T], [HD.res], BF16)
            self.out_proj_ln(0, HD.ap, HD.res, 0, t0, T, lt)
    def sb_stage(self):
        P = self.P
        XB = self.XB
        self.aoff = NKC * S // 2
        A = self.alloc
        QT = A(6 * S, BF16, "QT", "p (c t) -> p c t", c=6)
        KT = A(6 * S, BF16, "KT", "p (c t) -> p c t", c=6)
        VT = A(16 * 768, BF16, "VTs", "p (s c) -> p s c", s=16)
        MQ = A(2 * S, BF16, "MQs", "p (c t) -> p c t", c=2)
        QTr = [[P.res(f"qt{c}_{t}") for t in range(4)] for c in range(6)]
        KTr = [P.res(f"kt{c}") for c in range(6)]

        for cc in range(20):
            w = self.w_get('A')
            if 12 <= cc < 18:
                for g4 in range(4):
                    b = self.bank()
                    for j in range(4):
                        st = g4 * 4 + j
                        for kc in range(NKC):
                            self.mm(b.ap[:, j * 128:(j + 1) * 128], XB[:, kc, st * 128:(st + 1) * 128], w.ap[:, kc, :],
                                    kc == 0, kc == NKC - 1, [w.res] + self.xr('b', kc, st * 128, 128), [b.res])
                    dst = VT.ap[:, g4 * 4:(g4 + 1) * 4, (cc - 12) * 128:(cc - 11) * 128]
                    self.copy('act' if g4 % 2 == 0 else 'dve', dst, b.ap[:].rearrange("p (j c) -> p j c", j=4), [b.res], [VT.res])
                continue
            for tt in range(4):
                tok = slice(tt * 512, (tt + 1) * 512)
                b = self.bank()
                for kc in range(NKC):
                    self.mm(b.ap[:], w.ap[:, kc, :], XB[:, kc, tok], kc == 0, kc == NKC - 1,
                            [w.res] + self.xr('b', kc, tt * 512, 512), [b.res])
                eng = 'act' if tt % 2 == 0 else 'dve'
                if cc < 6:
                    self.copy(eng, QT.ap[:, cc, tok], b.ap[:], [b.res], [QTr[cc][tt]])
                elif cc < 12:
                    self.copy(eng, KT.ap[:, cc - 6, tok], b.ap[:], [b.res], [KTr[cc - 6]])
                else:
                    self.copy(eng, MQ.ap[:, cc - 18, tok], b.ap[:], [b.res], [MQ.res])
        P.barrier()

        save = self.aoff
        self.aoff = 0
        TRI = A(128, BF16, "tri")
        TRC = A(128, BF16, "trc")
        MSK = [A(512, BF16, f"dmask{o}") for o in range(4)]
        EX = [A(512, F32, f"ex{i}") for i in range(2)]
        SP = [A(512, F32, f"sp{i}") for i in range(2)]
        SPH = [A(512, BF16, f"sph{i}") for i in range(2)]
        SPL = [A(512, BF16, f"spl{i}") for i in range(2)]
        ARG = [A(512, F32, f"arg{i}") for i in range(2)]
        ATT = [A(512, BF16, f"att{i}") for i in range(2)]
        E = [A(512, BF16, "E0s"), A(512, BF16, "E1s"), A(512, F32, "E2s")]
        lt = [A(512, F32, f"lnts{i}") for i in range(1)]
        assert self.aoff <= NKC * S // 2, self.aoff
        self.aoff = save
        lt = lt + [A(512, F32, f"lnts{i}") for i in range(1, 8)]
        ones_b = self.ones_b
        P.op('pool', lambda e: e.affine_select(out=TRI.ap, in_=ones_b.ap[:], pattern=[[-1, 128]], compare_op=ALU.is_gt,
                                               fill=0.0, base=0, channel_multiplier=1), reads=[ones_b.res], writes=[TRI.res])
        P.op('pool', lambda e: e.affine_select(out=TRC.ap, in_=ones_b.ap[:], pattern=[[1, 128]], compare_op=ALU.is_ge,
                                               fill=0.0, base=0, channel_multiplier=-1), reads=[ones_b.res], writes=[TRC.res])
        self.memset('dve', EX[0].ap, 1.0, EX[0].res)
        for o in range(4):
            P.op('pool', (lambda o: lambda e: e.affine_select(out=MSK[o].ap, in_=EX[0].ap, pattern=[[1, 512]],
                                                              compare_op=ALU.is_gt, fill=0.0, base=-128 * o,
                                                              channel_multiplier=-1))(o),
                 reads=[EX[0].res], writes=[MSK[o].res])

        ps = self.ps
        it = 0
        for h in range(12):
            ch, hp = h // 2, slice((h % 2) * 64, (h % 2) * 64 + 64)
            for qt in range(4):
                qtok = slice(qt * 512, (qt + 1) * 512)
                nkt = 4 * (qt + 1)
                xs = ps[5 + (it % 2)]
                ob = ps[(it % 2)]
                prev = None
                for idx, kt in enumerate(range(nkt - 1, -1, -1)):
                    zb = ps[2 + (idx % 3)]
                    ex, sp, sph, spl, arg, att = (EX[idx % 2], SP[idx % 2], SPH[idx % 2], SPL[idx % 2], ARG[idx % 2],
                                                  ATT[idx % 2])
                    diag = kt - 4 * qt
                    self.mm(zb.ap[:], KT.ap[hp, ch, kt * 128:(kt + 1) * 128], QT.ap[hp, ch, qtok], True, True,
                            [KTr[ch], QTr[ch][qt]], [zb.res])
                    self.act(ex.ap, zb.ap[:], AF.Exp, [zb.res], [ex.res], scale=0.125)
                    self.act(sp.ap, ex.ap, AF.Ln, [ex.res], [sp.res], bias=self.cst(1.0), scale=1.0)
                    if diag >= 0:
                        self.tt('pool', sp.ap, sp.ap, MSK[diag].ap, ALU.mult, [sp.res, MSK[diag].res], [sp.res])
                    self.copy('pool', sph.ap, sp.ap, [sp.res], [sph.res])
                    self.tt('pool', spl.ap, sp.ap, sph.ap, ALU.subtract, [sp.res, sph.res], [spl.res])
                    if prev is not None:
                        self.mm(xs.ap[:], TRC.ap, prev[0].ap, False, False, [TRC.res, prev[0].res], [xs.res], skip=True)
                        self.mm(xs.ap[:], TRC.ap, prev[1].ap, False, False, [TRC.res, prev[1].res], [xs.res], skip=True)
                    self.mm(xs.ap[:], TRI.ap, sph.ap, prev is None, False, [TRI.res, sph.res], [xs.res], skip=prev is not None)
                    self.mm(xs.ap[:], TRI.ap, spl.ap, False, True, [TRI.res, spl.res], [xs.res], skip=prev is not None)
                    prev = (sph, spl)
                    self.stt(arg.ap, zb.ap[:], 0.125, sp.ap, ALU.mult, ALU.subtract, [zb.res, sp.res], [arg.res])
                    self.tt('dve', arg.ap, arg.ap, xs.ap[:], ALU.subtract, [arg.res, xs.res], [arg.res])
                    self.act(att.ap, arg.ap, AF.Exp, [arg.res], [att.res])
                    if diag >= 0:
                        self.tt('dve', att.ap, att.ap, MSK[diag].ap, ALU.mult, [att.res, MSK[diag].res], [att.res])
                    self.mm(ob.ap[hp, :], VT.ap[:, kt, h * 64:(h + 1) * 64], att.ap, idx == 0, idx == nkt - 1,
                            [VT.res, att.res], [ob.res])
                self.copy('act', QT.ap[hp, ch, qtok], ob.ap[hp, :], [ob.res], [QTr[ch][qt]])
                it += 1

        for tt in range(4):
            t0 = tt * 512
            HDv = None
            self.mem_attn_sb(MQ, t0, E)
            self.out_proj_ln_sb(QT, QTr, MQ, t0, lt)

    def mem_attn_sb(self, MQ, t0, E):
        T = 512
        for h in range(4):
            cq, hp = h // 2, h % 2
            pr = slice(hp * 64, (hp + 1) * 64)
            for mt in range(2):
                b = self.bank()
                self.mm(b.ap[:, 0:T], self.MK.ap[pr, cq, mt * 128:(mt + 1) * 128], MQ.ap[pr, cq, t0:t0 + T], True, True,
                        [self.MK.res, MQ.res], [b.res])
                self.act(E[mt].ap[:, 0:T], b.ap[:, 0:T], AF.Exp, [b.res], [E[mt].res], scale=0.125)
            bn = self.bank()
            bd = self.bank()
            for mt in range(2):
                self.mm(bn.ap[pr, 0:T], self.MV.ap[:, mt, cq, hp * 64:(hp + 1) * 64], E[mt].ap[:, 0:T], mt == 0, mt == 1,
                        [self.MV.res, E[mt].res], [bn.res])
            for mt in range(2):
                self.mm(bd.ap[pr, 0:T], self.ones_b.ap[:, 0:64], E[mt].ap[:, 0:T], mt == 0, mt == 1,
                        [self.ones_b.res, E[mt].res], [bd.res])
            rd = E[2]
            self.P.op('dve', (lambda o, i: lambda e: e.reciprocal(out=o, in_=i))(rd.ap[pr, 0:T], bd.ap[pr, 0:T]),
                      reads=[bd.res], writes=[rd.res])
            self.tt('dve', MQ.ap[pr, cq, t0:t0 + T], bn.ap[pr, 0:T], rd.ap[pr, 0:T], ALU.mult,
                    [bn.res, rd.res], [MQ.res])

    def out_proj_ln_sb(self, QT, QTr, MQ, t0, tmps):
        T = 512
        tt = t0 // 512
        lncol = (1 * 3 + 1) * 8
        s1, s2 = self.ps[0], self.ps[1]
        for dc in range(NKC):
            wo = self.w_get('A')
            by = self.bank()
            for cch in range(NKC):
                if cch < 6:
                    rhs, rr = QT.ap[:, cch, t0:t0 + T], QTr[cch][tt]
                else:
                    rhs, rr = MQ.ap[:, cch - 6, t0:t0 + T], MQ.res
                self.mm(by.ap[:, 0:T], wo.ap[:, cch, :], rhs, cch == 0, cch == NKC - 1, [wo.res, rr], [by.res])
            self.resid_stats(dc, t0, T, by, ALPHA, s1, s2, tmps[2 + dc % 2])
        self.ln_finish(t0, T, s1, s2, LN_EPS, lncol, tmps, write_xb=False)


def tile_a(w):
    lead = w.shape[:-2]
    n = w.shape[-1] // 128
    w = w.reshape(lead + (NKC, 128, n, 128))
    nd = len(lead)
    w = np.transpose(w, tuple(range(nd)) + (nd + 2, nd + 1, nd + 0, nd + 3))
    return np.ascontiguousarray(w).reshape(lead + (n, 128, NKC * 128))


def tile_d(w):
    lead = w.shape[:-2]
    w = w.reshape(lead + (NFC, 128, NKC, 128))
    nd = len(lead)
    w = np.transpose(w, tuple(range(nd)) + (nd + 2, nd + 1, nd + 0, nd + 3))
    return np.ascontiguousarray(w).reshape(lead + (NKC, 128, NFC * 128))


def cols(v, n):
    return np.asarray(v, dtype=np.float32).reshape(n, 128).T


def prep_shared(inputs):
    f = lambda a: np.asarray(a, dtype=np.float32)
    sh = {}
    for which in (1, 2):
        sh[f"g{which}"] = tile_a(f(inputs[f"ffn{which}_w_gate"]))
        sh[f"u{which}"] = tile_a(f(inputs[f"ffn{which}_w_up"]))
        sh[f"d{which}"] = tile_d(f(inputs[f"ffn{which}_w_down"]))
    sh["lng"] = np.ascontiguousarray(f(inputs["ln_g"]).reshape(6, NKC, 128).transpose(2, 0, 1).reshape(128, 48))
    sh["lnb"] = np.ascontiguousarray(f(inputs["ln_b"]).reshape(6, NKC, 128).transpose(2, 0, 1).reshape(128, 48))
    sh["wout"] = tile_a(f(inputs["w_out"]))
    sh["wmem"] = tile_a(f(inputs["w_mem_kv"]))
    sh["rwin"] = tile_a(f(inputs["rwkv_w_in"])[0])
    sh["sbin"] = tile_a(f(inputs["sb_w_in"])[0])
    rp = np.zeros((128, 64), np.float32)
    rp[:, PC_MU:PC_MU + 20] = cols(f(inputs["rwkv_mu"])[0], 20)
    for off, key in ((PC_W0, "rwkv_w0"), (PC_A0, "rwkv_a0"), (PC_KK, "rwkv_k_k"), (PC_KA, "rwkv_k_a"),
                     (PC_RK, "rwkv_r_k"), (PC_LG, "rwkv_lnx_g"), (PC_LB, "rwkv_lnx_b")):
        rp[:, off:off + 6] = cols(f(inputs[key])[0].reshape(-1), 6)
    sh["rpar"] = rp
    sh["lw"] = np.ascontiguousarray(np.concatenate([f(inputs["rwkv_w_up"])[0], f(inputs["rwkv_a_up"])[0]], axis=0))
    sh["gw"] = np.ascontiguousarray(f(inputs["rwkv_g_up"])[0])
    return sh


_NC_CACHE = {}


def get_nc(stop_after=None, start_at=0, dbg=(), rw_blocks=8):
    key = (stop_after, start_at, tuple(dbg), rw_blocks)
    if key not in _NC_CACHE:
        _NC_CACHE[key] = Builder(stop_after, start_at, dbg, rw_blocks).build()
    return _NC_CACHE[key]


def make_in_maps(inputs, cores=8, x_override=None):
    sh = prep_shared(inputs)
    x = np.asarray(inputs["x"], dtype=np.float32) if x_override is None else x_override
    mem = np.asarray(inputs["mem"], dtype=np.float32)
    in_maps = []
    for b in range(cores):
        m = dict(sh)
        m["xT"] = np.ascontiguousarray(x[b].T)
        m["memT"] = np.ascontiguousarray(mem[b].T)
        in_maps.append(m)
    return in_maps


def run(inputs, cores=8, stop_after=None, start_at=0, dbg=(), trace=False, x_override=None, rw_blocks=8):
    sh = prep_shared(inputs)
    x = np.asarray(inputs["x"], dtype=np.float32) if x_override is None else x_override
    mem = np.asarray(inputs["mem"], dtype=np.float32)
    in_maps = []
    for b in range(cores):
        m = dict(sh)
        m["xT"] = np.ascontiguousarray(x[b].T)
        m["memT"] = np.ascontiguousarray(mem[b].T)
        in_maps.append(m)
    nc = get_nc(stop_after, start_at, dbg, rw_blocks)
    res = run_bass_kernel_spmd(nc, in_maps, core_ids=list(range(cores)), trace=trace)
    out = np.stack([np.ascontiguousarray(r["outT"].T) for r in res.results], axis=0)
    return out, res


def kernel(**inputs):
    out, _ = run(inputs, cores=8)
    return out.astype(np.float32)
```

```python
import os
import numpy as np
from contextlib import ExitStack

import concourse.bass as bass
import concourse.mybir as mybir
from concourse.bass_utils import run_bass_kernel_spmd

F32 = mybir.dt.float32
BF16 = mybir.dt.bfloat16
AF = mybir.ActivationFunctionType
ALU = mybir.AluOpType

D = 1024
S = 2048
DFF = 2816
NKC = 8
NFC = 22
NMEM = 256
ALPHA = 4.0 ** 0.25
LN_EPS = 1e-5
LNX_EPS = 64e-5
DECAY_C = -float(np.exp(-0.5))

ENGS = ['pe', 'act', 'dve', 'pool', 'sp']
SAME_ENGINE_SYNC = os.environ.get("SES", "1") == "1"


class Res:
    __slots__ = ('name', 'last_w', 'readers', 'dreaders', 'sem', 'dma_cnt', 'excl')

    def __init__(self, name):
        self.name = name
        self.excl = False
        self.last_w = None
        self.readers = {}
        self.dreaders = []
        self.sem = None
        self.dma_cnt = 0


class OpRec:
    __slots__ = ('eng', 'fn', 'is_dma', 'deps', 'flagged', 'count', 'dres', 'dval', 'idx')


class Prog:
    def __init__(self, nc, es):
        self.nc = nc
        self.es = es
        self.q = {e: [] for e in ENGS}
        self.nres = 0
        self.all_dma = []
        self.bar = {e: None for e in ENGS}

    def res(self, name=None):
        self.nres += 1
        return Res((name or "r") + str(self.nres))

    def _track(self, op, reads, writes):
        deps = []
        for r in reads:
            if r.last_w is not None:
                deps.append(r.last_w)
        for w in writes:
            if w.last_w is not None:
                deps.append(w.last_w)
            deps.extend(w.readers.values())
            deps.extend(w.dreaders)
        b = self.bar[op.eng]
        if b is not None:
            deps.extend(b)
            self.bar[op.eng] = None
        for r in reads:
            if op.is_dma:
                r.dreaders.append(op)
            else:
                r.readers[op.eng] = op
        for w in writes:
            w.last_w = op
            w.readers = {}
            w.dreaders = []
        best = {}
        out = []
        for d in deps:
            if d is op:
                continue
            if d.is_dma:
                out.append(d)
                continue
            if (not op.is_dma) and d.eng == 'pe' and op.eng == 'pe':
                continue
            if (not op.is_dma) and d.eng == op.eng and not SAME_ENGINE_SYNC:
                continue
            if d.eng not in best or best[d.eng].idx < d.idx:
                best[d.eng] = d
        out.extend(best.values())
        op.deps = out

    def op(self, eng, fn, reads=(), writes=()):
        if any(r.excl for r in reads):
            writes = list(writes) + [r for r in reads if r.excl]
            reads = [r for r in reads if not r.excl]
        o = OpRec()
        o.eng = eng
        o.fn = fn
        o.is_dma = False
        o.flagged = False
        o.count = None
        o.dres = None
        o.dval = None
        o.idx = len(self.q[eng])
        self._track(o, reads, writes)
        self.q[eng].append(o)
        return o

    def dma(self, eng, fn, sres, reads=(), writes=()):
        o = OpRec()
        o.eng = eng
        o.fn = fn
        o.is_dma = True
        o.flagged = False
        o.count = None
        o.dres = sres
        sres.dma_cnt += 16
        o.dval = sres.dma_cnt
        o.idx = len(self.q[eng])
        self._track(o, reads, writes)
        self.q[eng].append(o)
        self.all_dma.append(o)
        return o

    def barrier(self):
        deps = list(self.all_dma)
        self.all_dma = []
        for e in ENGS:
            for o in reversed(self.q[e]):
                if not o.is_dma:
                    deps.append(o)
                    break
        for e in ENGS:
            prev = self.bar[e]
            self.bar[e] = (prev or []) + deps

    def finalize(self, final_waits=()):
        nc = self.nc
        es = self.es
        for e in ENGS:
            for o in self.q[e]:
                for d in o.deps:
                    if not d.is_dma:
                        d.flagged = True
        for o in final_waits:
            if not o.is_dma:
                o.flagged = True
        esem = {}
        for e in ENGS:
            c = 0
            for o in self.q[e]:
                if o.is_dma:
                    if o.dres.sem is None:
                        o.dres.sem = es.enter_context(nc.semaphore("d_" + o.dres.name))
                elif o.flagged:
                    c += 1
                    o.count = c
            esem[e] = es.enter_context(nc.semaphore("e_" + e))

        def tok(d):
            if d.is_dma:
                return d.dres.sem, d.dval
            return esem[d.eng], d.count

        def emit(ename, eng):
            known = {}
            for o in self.q[ename]:
                need = {}
                for d in o.deps:
                    s, v = tok(d)
                    k = id(s)
                    if known.get(k, 0) >= v:
                        continue
                    if k not in need or need[k][1] < v:
                        need[k] = (s, v)
                for k, (s, v) in need.items():
                    eng.wait_ge(s, v)
                    known[k] = v
                ins = o.fn(eng)
                if o.is_dma:
                    ins.then_inc(o.dres.sem, 16)
                elif o.flagged:
                    ins.then_inc(esem[ename], 1)
            if ename == 'sp':
                for d in final_waits:
                    s, v = tok(d)
                    eng.wait_ge(s, v)

        with nc.Block() as block:
            @block.tensor
            def _(eng):
                emit('pe', eng)

            @block.scalar
            def _(eng):
                emit('act', eng)

            @block.vector
            def _(eng):
                emit('dve', eng)

            @block.gpsimd
            def _(eng):
                emit('pool', eng)

            @block.sync
            def _(eng):
                emit('sp', eng)


class CutHere(Exception):
    pass


import os
RW_CUT = int(os.environ.get("RW_CUT", "0"))


def cut(n):
    if RW_CUT == n:
        raise CutHere()


class Tile:
    __slots__ = ('ap', 'res')

    def __init__(self, ap, res):
        if not isinstance(ap, bass.AP):
            ap = ap[:]
        self.ap = ap
        self.res = res


AX = mybir.AxisListType
RW_ORDER = [18, 19] + [c for ch in range(6) for c in (ch, 6 + ch, 12 + ch)] + [20, 21]
PC_MU, PC_W0, PC_A0, PC_KK, PC_KA, PC_RK, PC_LG, PC_LB = 0, 20, 26, 32, 38, 44, 50, 56


class Builder:
    STAGES = ["l0_x1", "l0_x2", "l0_x3", "l1_x1", "l1_x2", "l1_x3"]

    def __init__(self, stop_after=None, start_at=0, dbg=(), rw_blocks=8):
        self.stop_after = stop_after
        self.start_at = start_at
        self.rw_blocks = rw_blocks
        last = self.STAGES.index(stop_after) if stop_after else 5
        self.run_stages = self.STAGES[start_at:last + 1]
        self.dbg_names = set(dbg)
        self.dbg_outs = []
        self.nc = bass.Bass("TRN2", target_bir_lowering=False)
        self.es = ExitStack()
        self.bank_rr = 0

    def sb(self, name, shape, dt):
        return self.es.enter_context(self.nc.sbuf_tensor(name, shape, dt))

    def dram_in(self, name, shape):
        return self.nc.dram_tensor(name, list(shape), F32, kind="ExternalInput").ap()

    def mm(self, out, lhsT, rhs, start, stop, reads, writes, skip=False):
        return self.P.op('pe', lambda e: e.matmul(out, lhsT=lhsT, rhs=rhs, start=start, stop=stop, skip_group_check=skip),
                         reads=reads, writes=writes)

    def tr(self, out, in_, reads, writes, start=True, stop=True):
        ident = self.IDB16.ap
        return self.P.op('pe', lambda e: e.matmul(out, lhsT=in_, rhs=ident, start=start, stop=stop),
                         reads=list(reads) + [self.IDB16.res], writes=writes)

    def act(self, out, in_, func, reads, writes, scale=None, bias=None):
        kw = {}
        if scale is not None:
            kw['scale'] = scale
        if bias is not None:
            kw['bias'] = bias
        return self.P.op('act', lambda e: e.activation(out=out, in_=in_, func=func, **kw),
                         reads=reads, writes=writes)

    def asc(self, out, in_, scale, bias, reads, writes):
        if bias is None:
            return self.P.op('act', lambda e: e.activation(out=out, in_=in_, func=AF.Copy, scale=scale),
                             reads=reads, writes=writes)
        return self.P.op('act', lambda e: e.activation(out=out, in_=in_, func=AF.Identity, scale=scale, bias=bias),
                         reads=reads, writes=writes)

    def tt(self, eng, out, in0, in1, op, reads, writes):
        return self.P.op(eng, lambda e: e.tensor_tensor(out=out, in0=in0, in1=in1, op=op),
                         reads=reads, writes=writes)

    def ts(self, eng, out, in0, s1, s2, op0, op1, reads, writes):
        if op1 is None:
            return self.P.op(eng, lambda e: e.tensor_scalar(out=out, in0=in0, scalar1=s1, scalar2=None, op0=op0),
                             reads=reads, writes=writes)
        return self.P.op(eng, lambda e: e.tensor_scalar(out=out, in0=in0, scalar1=s1, scalar2=s2, op0=op0, op1=op1),
                         reads=reads, writes=writes)

    def stt(self, out, in0, scalar, in1, op0, op1, reads, writes):
        return self.P.op('dve', lambda e: e.scalar_tensor_tensor(out=out, in0=in0, scalar=scalar, in1=in1,
                                                                 op0=op0, op1=op1),
                         reads=reads, writes=writes)

    def rsqrt(self, out, in_, eps, reads, writes, eng='dve'):
        self.P.op('act', lambda e: e.activation(out=out, in_=in_, func=AF.Sqrt, bias=self.cst(eps), scale=1.0),
                  reads=list(reads) + [self.CONST.res], writes=writes)
        return self.P.op(eng, lambda e: e.reciprocal(out=out, in_=out), reads=writes, writes=writes)

    def cst(self, v, n=128, base=0):
        c = self.const_cols[v]
        return self.CONST.ap[base:base + n, c:c + 1]

    def copy(self, eng, out, in_, reads, writes):
        if eng == 'act':
            return self.P.op('act', lambda e: e.copy(out=out, in_=in_), reads=reads, writes=writes)
        return self.P.op(eng, lambda e: e.tensor_copy(out=out, in_=in_), reads=reads, writes=writes)

    def memset(self, eng, ap, v, res):
        return self.P.op(eng, lambda e: e.memset(ap, v), writes=[res])

    def dma_in(self, eng, dst, src, res):
        return self.P.dma(eng, lambda e: e.dma_start(out=dst, in_=src), res, writes=[res])

    def bank(self):
        b = self.ps[3 + self.bank_rr % 5]
        self.bank_rr += 1
        return b

    def dbg(self, name, ap, shape, res_list, dt=F32):
        if name not in self.dbg_names:
            return
        t = self.nc.dram_tensor("dbg_" + name, list(shape), dt, kind="ExternalOutput").ap()
        r = self.P.res("dbg_" + name)
        o = self.P.dma('sp', lambda e: e.dma_start(out=t, in_=ap), r, reads=res_list)
        self.dbg_outs.append(o)

    def xr(self, kind, kc, t0, T):
        rr = self.XFr if kind == 'f' else self.XBr
        return [rr[kc][b] for b in range(t0 // 256, (t0 + T + 255) // 256)]

    def w_schedule(self):
        sched = []
        for i in range(4):
            sched.append(('A', self.wmem_d[i]))
        for st in self.run_stages:
            L = int(st[1])
            if st.endswith("x2"):
                sched.extend(self.attn_w_schedule(L))
                continue
            which = 1 if st.endswith("x1") else 2
            g, u, d = self.wd[f"g{which}"], self.wd[f"u{which}"], self.wd[f"d{which}"]
            for half in range(2):
                for fc in range(NFC):
                    sched.append(('A', g[L, fc]))
                    sched.append(('A', u[L, fc]))
                for dc in range(NKC):
                    sched.append(('D', d[L, dc]))
        return sched

    def attn_w_schedule(self, L):
        s = []
        if L == 0:
            for blk in range(self.rw_blocks):
                for cc in RW_ORDER:
                    s.append(('A', self.rwin_d[cc]))
                for dc in range(NKC):
                    s.append(('A', self.wout_d[0, dc]))
        else:
            for cc in range(20):
                s.append(('A', self.sbin_d[cc]))
            for tt in range(4):
                for dc in range(NKC):
                    s.append(('A', self.wout_d[1, dc]))
        return s

    def w_init(self):
        self.NA = 6
        self.ND = 2
        self.wslots = {'A': [], 'D': []}
        for i in range(self.NA):
            t = self.sb(f"wa{i}", [128, NKC, 128], BF16)
            self.wslots['A'].append(Tile(t, self.P.res(f"wa{i}")))
        for i in range(self.ND):
            o = 23552 + i * 1408
            t = self.carve(o, 1408, BF16, "p (f d) -> p f d", f=NFC)
            self.wslots['D'].append(Tile(t, self.P.res(f"wd{i}")))
        self.wsched = self.w_schedule()
        self.w_issued = 0
        self.w_next = 0
        self.w_kcount = {'A': 0, 'D': 0}
        self.w_tile_slot = []
        self.w_ahead = {'A': 3, 'D': 1}

    def _w_issue_one(self):
        kind, src = self.wsched[self.w_issued]
        n = self.w_kcount[kind]
        self.w_kcount[kind] = n + 1
        slots = self.wslots[kind]
        t = slots[n % len(slots)]
        self.w_tile_slot.append(t)
        if kind == 'A':
            dst = t.ap[:].rearrange("p a b -> p (a b)")
            self.P.dma('pool', lambda e: e.dma_start(out=dst, in_=src), t.res, writes=[t.res])
        else:
            dst = t.ap.rearrange("p a b -> p (a b)").rearrange("p (h x) -> p h x", h=2)
            s2 = src.rearrange("p (h x) -> p h x", h=2)
            self.P.dma('pool', lambda e: e.dma_start(out=dst, in_=s2), t.res, writes=[t.res])
        self.w_issued += 1

    def w_get(self, kind):
        i = self.w_next
        assert self.wsched[i][0] == kind, (i, self.wsched[i][0], kind)
        while self.w_issued <= i:
            self._w_issue_one()
        ahead_cnt = {'A': 0, 'D': 0}
        for j in range(i + 1, self.w_issued):
            ahead_cnt[self.wsched[j][0]] += 1
        while self.w_issued < len(self.wsched):
            k = self.wsched[self.w_issued][0]
            if ahead_cnt[k] >= self.w_ahead[k]:
                break
            self._w_issue_one()
            ahead_cnt[k] += 1
        self.w_next += 1
        return self.w_tile_slot[i]

    def build(self):
        nc = self.nc
        es = self.es
        with es:
            self.P = P = Prog(nc, es)
            self.xT = self.dram_in("xT", [D, S])
            self.memT = self.dram_in("memT", [D, NMEM])
            self.outT = nc.dram_tensor("outT", [D, S], F32, kind="ExternalOutput").ap()
            self.wd = {}
            for which in (1, 2):
                self.wd[f"g{which}"] = self.dram_in(f"g{which}", [2, NFC, 128, NKC * 128])
                self.wd[f"u{which}"] = self.dram_in(f"u{which}", [2, NFC, 128, NKC * 128])
                self.wd[f"d{which}"] = self.dram_in(f"d{which}", [2, NKC, 128, NFC * 128])
            self.lng_d = self.dram_in("lng", [128, 48])
            self.lnb_d = self.dram_in("lnb", [128, 48])
            self.wout_d = self.dram_in("wout", [2, NKC, 128, NKC * 128])
            self.wmem_d = self.dram_in("wmem", [4, 128, NKC * 128])
            self.rwin_d = self.dram_in("rwin", [22, 128, NKC * 128])
            self.sbin_d = self.dram_in("sbin", [20, 128, NKC * 128])
            self.rpar_d = self.dram_in("rpar", [128, 64])
            self.lw_d = self.dram_in("lw", [128, 768])
            self.gw_d = self.dram_in("gw", [128, 768])

            self.XF = self.sb("XF", [128, NKC, S], F32)
            self.XFr = [[P.res(f"xf{k}_{t}") for t in range(8)] for k in range(NKC)]
            self.XBr = [[P.res(f"xb{k}_{t}") for t in range(8)] for k in range(NKC)]
            self.LNG = Tile(self.sb("LNG", [128, 48], F32), P.res("lng"))
            self.LNB = Tile(self.sb("LNB", [128, 48], F32), P.res("lnb"))
            self.ones_f = Tile(self.sb("ones_f", [128, 128], F32), P.res("ones_f"))
            self.ones_b = Tile(self.sb("ones_b", [128, 128], BF16), P.res("ones_b"))
            self.IDF = Tile(self.sb("IDF", [128, 128], F32), P.res("idf"))
            self.BDF = Tile(self.sb("BDF", [128, 128], F32), P.res("bdf"))
            self.IDB16 = Tile(self.sb("IDB16", [128, 128], BF16), P.res("idb16"))
            self.MK = Tile(self.sb("MK", [128, 2, NMEM], BF16), P.res("mk"))
            self.MV = Tile(self.sb("MV", [128, 2, 2, 128], BF16), P.res("mv"))
            self.ARENA_F32 = 32600
            self.arena = self.sb("arena", [128, self.ARENA_F32], F32)
            self.w_init()
            self.ps = []
            for i in range(8):
                t = es.enter_context(nc.psum_tensor(f"ps{i}", [128, 512], F32))
                self.ps.append(Tile(t, P.res(f"ps{i}")))
                self.ps[-1].res.excl = True

            self.memset('dve', self.ones_f.ap[:], 1.0, self.ones_f.res)
            self.memset('dve', self.ones_b.ap[:], 1.0, self.ones_b.res)
            self.CONST = Tile(self.sb("CONST", [128, 8], F32), P.res("const"))
            self.const_cols = {}
            for ci, cv in enumerate([4.0 * LN_EPS, LN_EPS, LNX_EPS, 1.0, 0.0]):
                self.const_cols[cv] = ci
                self.memset('dve', self.CONST.ap[:, ci:ci + 1], cv, self.CONST.res)
            P.op('pool', lambda e: e.affine_select(out=self.IDF.ap[:], in_=self.ones_f.ap[:], pattern=[[-1, 128]],
                                                   compare_op=ALU.is_equal, fill=0.0, base=0, channel_multiplier=1),
                 reads=[self.ones_f.res], writes=[self.IDF.res])
            self.copy('dve', self.IDB16.ap, self.IDF.ap, [self.IDF.res], [self.IDB16.res])
            self.memset('dve', self.BDF.ap[:], 0.0, self.BDF.res)
            self.memset('dve', self.BDF.ap[0:64, 0:64], 1.0, self.BDF.res)
            self.memset('dve', self.BDF.ap[64:128, 64:128], 1.0, self.BDF.res)

            self.dma_in('sp', self.LNG.ap[:], self.lng_d, self.LNG.res)
            self.dma_in('sp', self.LNB.ap[:], self.lnb_d, self.LNB.res)
            for kc in range(NKC):
                for t in range(8):
                    r = self.XFr[kc][t]
                    self.dma_in('sp', self.XF[:, kc, t * 256:(t + 1) * 256],
                                self.xT[kc * 128:(kc + 1) * 128, t * 256:(t + 1) * 256], r)

            self.arena_ffn()
            self.mem_setup()
            for kc in range(NKC):
                for t in range(4):
                    sl = slice(t * 512, (t + 1) * 512)
                    self.copy('dve' if (kc + t) % 2 == 0 else 'act', self.XB[:, kc, sl], self.XF[:, kc, sl],
                              reads=self.xr('f', kc, t * 512, 512), writes=self.xr('b', kc, t * 512, 512))

            for st in self.run_stages:
                L = int(st[1])
                if st.endswith("x1"):
                    self.ffn(L, 1)
                elif st.endswith("x3"):
                    self.ffn(L, 2)
                else:
                    P.barrier()
                    if L == 0:
                        try:
                            self.rwkv_stage()
                        except CutHere:
                            pass
                    else:
                        self.sb_stage()
                    P.barrier()
                    self.arena_ffn()
                    if L == 1:
                        for kc in range(NKC):
                            for t in range(4):
                                sl = slice(t * 512, (t + 1) * 512)
                                self.copy('dve' if (kc + t) % 2 == 0 else 'act', self.XB[:, kc, sl], self.XF[:, kc, sl],
                                          reads=self.xr('f', kc, t * 512, 512), writes=self.xr('b', kc, t * 512, 512))

            finals = list(self.dbg_outs)
            for kc in range(NKC):
                for t in range(4):
                    rl = self.xr('f', kc, t * 512, 512)
                    src = self.XF[:, kc, t * 512:(t + 1) * 512]
                    dst = self.outT[kc * 128:(kc + 1) * 128, t * 512:(t + 1) * 512]
                    o = P.dma('sp', (lambda dst, src: lambda e: e.dma_start(out=dst, in_=src))(dst, src), rl[0], reads=rl)
                    finals.append(o)
            P.finalize(final_waits=finals)
        return nc

    def carve(self, off_words, nwords, dt, pattern=None, **kw):
        ap = self.arena[:, off_words:off_words + nwords]
        if dt == BF16:
            ap = ap.bitcast(BF16)
        if pattern:
            ap = ap.rearrange(pattern, **kw)
        return ap

    def alloc(self, nelem, dt, name, pattern=None, **kw):
        nwords = nelem if dt == F32 else (nelem + 1) // 2
        ap = self.carve(self.aoff, nwords, dt, pattern, **kw)
        self.aoff += nwords
        assert self.aoff <= self.ARENA_F32, (name, self.aoff)
        return Tile(ap, self.P.res(name))

    def arena_ffn(self):
        P = self.P
        o = 0
        self.XB = self.carve(o, NKC * S // 2, BF16, "p (k t) -> p k t", k=NKC)
        o += NKC * S // 2
        self.HT = self.carve(o, NFC * 1024 // 2, BF16, "p (f t) -> p f t", f=NFC)
        o += NFC * 1024 // 2
        self.HTr = [[P.res(f"ht{f}_{t}") for t in range(2)] for f in range(NFC)]
        self.tmpf = []
        for i in range(8):
            self.tmpf.append(Tile(self.carve(o, 512, F32), P.res(f"tmpf{i}")))
            o += 512
        assert o <= self.ARENA_F32, o

    def mem_setup(self):
        P = self.P
        MT = self.HT[:, 0:4, :].rearrange("p a b -> p (a b)")[:, 0:NKC * NMEM].rearrange("p (k m) -> p k m", k=NKC)
        mtr = P.res("mt")
        for kc in range(NKC):
            P.dma('pool', (lambda kc: lambda e: e.dma_start(out=MT[:, kc, :], in_=self.memT[kc * 128:(kc + 1) * 128, :]))(kc),
                  mtr, writes=[mtr])
        for c in range(2):
            w = self.w_get('A')
            b = self.bank()
            for kc in range(NKC):
                self.mm(b.ap[:, 0:NMEM], w.ap[:, kc, :], MT[:, kc, :], kc == 0, kc == NKC - 1, [w.res, mtr], [b.res])
            self.copy('act', self.MK.ap[:, c, :], b.ap[:, 0:NMEM], [b.res], [self.MK.res])
        for c in range(2):
            w = self.w_get('A')
            b = self.bank()
            for mt in range(2):
                for kc in range(NKC):
                    self.mm(b.ap[:, mt * 128:(mt + 1) * 128], MT[:, kc, mt * 128:(mt + 1) * 128], w.ap[:, kc, :],
                            kc == 0, kc == NKC - 1, [w.res, mtr], [b.res])
            for mt in range(2):
                self.copy('act', self.MV.ap[:, mt, c, :], b.ap[:, mt * 128:(mt + 1) * 128], [b.res], [self.MV.res])

    def ffn(self, L, which):
        XF, XB, HT, ps = self.XF, self.XB, self.HT, self.ps
        lncol = (L * 3 + (0 if which == 1 else 2)) * 8
        it = 0
        for half in range(2):
            t0 = half * 1024
            for fc in range(NFC):
                wg = self.w_get('A')
                wu = self.w_get('A')
                for tt in range(2):
                    ts0 = t0 + tt * 512
                    tok = slice(ts0, ts0 + 512)
                    bg = ps[(it % 2) * 2]
                    bu = ps[(it % 2) * 2 + 1]
                    for kc in range(NKC):
                        self.mm(bg.ap[:], wg.ap[:, kc, :], XB[:, kc, tok], kc == 0, kc == NKC - 1,
                                [wg.res] + self.xr('b', kc, ts0, 512), [bg.res])
                    for kc in range(NKC):
                        self.mm(bu.ap[:], wu.ap[:, kc, :], XB[:, kc, tok], kc == 0, kc == NKC - 1,
                                [wu.res] + self.xr('b', kc, ts0, 512), [bu.res])
                    st = self.tmpf[it % 2]
                    self.act(st.ap, bg.ap[:], AF.Silu, [bg.res], [st.res])
                    self.tt('dve', HT[:, fc, tt * 512:(tt + 1) * 512], st.ap, bu.ap[:], ALU.mult,
                            [st.res, bu.res], [self.HTr[fc][tt]])
                    it += 1
            sbk = [(ps[2], ps[3]), (ps[0], ps[1])]
            for dc in range(NKC):
                wd = self.w_get('D')
                for tt in range(2):
                    ts0 = t0 + tt * 512
                    tok = slice(ts0, ts0 + 512)
                    by = ps[4 + (it % 2)]
                    for fc in range(NFC):
                        self.mm(by.ap[:], wd.ap[:, fc, :], HT[:, fc, tt * 512:(tt + 1) * 512], fc == 0, fc == NFC - 1,
                                [wd.res, self.HTr[fc][tt]], [by.res])
                    s1, s2 = sbk[tt]
                    self.resid_stats(dc, ts0, 512, by, 2.0 * ALPHA, s1, s2, self.tmpf[2 + it % 2])
                    it += 1
            for tt in range(2):
                s1, s2 = sbk[tt]
                self.ln_finish(t0 + tt * 512, 512, s1, s2, 4.0 * LN_EPS, lncol, self.tmpf)

    def resid_stats(self, dc, t0, T, by, xscale, s1, s2, sq):
        XF = self.XF
        tok = slice(t0, t0 + T)
        xr = self.xr('f', dc, t0, T)
        self.stt(XF[:, dc, tok], XF[:, dc, tok], xscale, by.ap[:, 0:T], ALU.mult, ALU.add, xr + [by.res], xr)
        self.act(sq.ap[:, 0:T], XF[:, dc, tok], AF.Square, xr, [sq.res])
        self.mm(s1.ap[:, 0:T], self.ones_f.ap[:], XF[:, dc, tok], dc == 0, dc == NKC - 1,
                [self.ones_f.res] + xr, [s1.res])
        self.mm(s2.ap[:, 0:T], self.ones_f.ap[:], sq.ap[:, 0:T], dc == 0, dc == NKC - 1,
                [self.ones_f.res, sq.res], [s2.res])

    def ln_finish(self, t0, T, s1, s2, eps, lncol, tmps, write_xb=True):
        XF, XB = self.XF, self.XB
        tok = slice(t0, t0 + T)
        mean, msq, rstd, mr = tmps[4], tmps[5], tmps[6], tmps[7]
        w = slice(0, T)
        self.act(mean.ap[:, w], s1.ap[:, w], AF.Copy, [s1.res], [mean.res], scale=1.0 / D)
        self.tt('dve', msq.ap[:, w], mean.ap[:, w], mean.ap[:, w], ALU.mult, [mean.res], [msq.res])
        self.stt(msq.ap[:, w], s2.ap[:, w], 1.0 / D, msq.ap[:, w], ALU.mult, ALU.subtract, [s2.res, msq.res], [msq.res])
        self.rsqrt(rstd.ap[:, w], msq.ap[:, w], eps, [msq.res], [rstd.res])
        self.tt('dve', mr.ap[:, w], mean.ap[:, w], rstd.ap[:, w], ALU.mult, [mean.res, rstd.res], [mr.res])
        for dc in range(NKC):
            xr = self.xr('f', dc, t0, T)
            u = tmps[dc % 2]
            self.tt('dve', u.ap[:, w], XF[:, dc, tok], rstd.ap[:, w], ALU.mult, xr + [rstd.res], [u.res])
            self.tt('dve', u.ap[:, w], u.ap[:, w], mr.ap[:, w], ALU.subtract, [u.res, mr.res], [u.res])
            g = self.LNG.ap[:, lncol + dc:lncol + dc + 1]
            b = self.LNB.ap[:, lncol + dc:lncol + dc + 1]
            self.act(XF[:, dc, tok], u.ap[:, w], AF.Identity, [u.res, self.LNG.res, self.LNB.res], xr, scale=g, bias=b)
            if write_xb:
                self.act(XB[:, dc, tok], u.ap[:, w], AF.Identity, [u.res, self.LNG.res, self.LNB.res],
                         self.xr('b', dc, t0, T), scale=g, bias=b)

    def mem_attn(self, MQ, mq_res, HD, hd_res, t0q, T, t0h, E):
        for h in range(4):
            cq, hp = h // 2, h % 2
            pr = slice(hp * 64, (hp + 1) * 64)
            for mt in range(2):
                b = self.bank()
                self.mm(b.ap[:, 0:T], self.MK.ap[pr, cq, mt * 128:(mt + 1) * 128], MQ[pr, cq, t0q:t0q + T], True, True,
                        [self.MK.res, mq_res], [b.res])
                self.act(E[mt].ap[:, 0:T], b.ap[:, 0:T], AF.Exp, [b.res], [E[mt].res], scale=0.125)
            bn = self.bank()
            bd = self.bank()
            for mt in range(2):
                self.mm(bn.ap[pr, 0:T], self.MV.ap[:, mt, cq, hp * 64:(hp + 1) * 64], E[mt].ap[:, 0:T], mt == 0, mt == 1,
                        [self.MV.res, E[mt].res], [bn.res])
            for mt in range(2):
                self.mm(bd.ap[pr, 0:T], self.ones_b.ap[:, 0:64], E[mt].ap[:, 0:T], mt == 0, mt == 1,
                        [self.ones_b.res, E[mt].res], [bd.res])
            rd = E[2]
            self.P.op('dve', (lambda o, i: lambda e: e.reciprocal(out=o, in_=i))(rd.ap[pr, 0:T], bd.ap[pr, 0:T]),
                      reads=[bd.res], writes=[rd.res])
            self.tt('dve', HD[pr, 6 + cq, t0h:t0h + T], bn.ap[pr, 0:T], rd.ap[pr, 0:T], ALU.mult,
                    [bn.res, rd.res], [hd_res])

    def out_proj_ln(self, L, HD, hd_res, t0h, t0, T, tmps):
        lncol = (L * 3 + 1) * 8
        s1, s2 = self.ps[0], self.ps[1]
        for dc in range(NKC):
            wo = self.w_get('A')
            by = self.bank()
            for cch in range(NKC):
                self.mm(by.ap[:, 0:T], wo.ap[:, cch, :], HD[:, cch, t0h:t0h + T], cch == 0, cch == NKC - 1,
                        [wo.res, hd_res], [by.res])
            self.resid_stats(dc, t0, T, by, ALPHA, s1, s2, tmps[2 + dc % 2])
        self.ln_finish(t0, T, s1, s2, LN_EPS, lncol, tmps)
    def rwkv_stage(self):
        P = self.P
        XB = self.XB
        T = 256
        self.aoff = NKC * S // 2
        A = self.alloc
        RP = A(64, F32, "rpar")
        OM = A(32, F32, "om")
        LW = A(768, BF16, "lw")
        GW = A(768, BF16, "gw")
        SM = A(T, F32, "scanmask")
        MLT = A(64, F32, "mlt")
        MLE = A(64, F32, "mle")
        MGT = A(64, F32, "mgt")
        IDB = A(64, F32, "idb")
        CAR = A(20, F32, "carry")
        HF = A(6 * 64, F32, "hf", "p (c i) -> p c i", c=6)
        HB = A(6 * 64, BF16, "hb", "p (c i) -> p c i", c=6)
        PR = [A(T + 1, F32, f"praw{i}") for i in range(3)]
        PL = PR[0:2]
        TD = A(T, BF16, "td")
        SDG = A(T, BF16, "sdg")
        NT = 14
        tm = [A(T, F32, f"rt{i}") for i in range(NT)]
        tb = [A(T, BF16, f"rtb{i}") for i in range(3)]
        ONH = A(768, BF16, "ONH")
        ONL = A(768, BF16, "ONL")
        At = A(6 * T, BF16, "At", "p (c t) -> p c t", c=6)
        Bt = A(6 * T, BF16, "Bt", "p (c t) -> p c t", c=6)
        Kt = A(6 * T, BF16, "Kt", "p (c t) -> p c t", c=6)
        Rt = A(6 * T, BF16, "Rt", "p (c t) -> p c t", c=6)
        Bh = A(2 * 768, BF16, "Bh", "p (r c) -> p r c", r=2)
        Kh = A(2 * 768, BF16, "Kh", "p (r c) -> p r c", r=2)
        Vt = A(2 * 768, BF16, "Vt", "p (r c) -> p r c", r=2)
        G = A(6 * T, F32, "G", "p (c t) -> p c t", c=6)
        BON = A(6 * T, F32, "BON", "p (c t) -> p c t", c=6)
        GC = A(6 * 4, F32, "gC", "p (c k) -> p c k", c=6)
        MX = [[A(6 * 64, BF16, f"X{b}{g}", "p (h t) -> p h t", h=6) for g in range(2)] for b in range(2)]
        MY = [[A(6 * 64, BF16, f"Y{b}{g}", "p (h t) -> p h t", h=6) for g in range(2)] for b in range(2)]
        ZT = [A(6 * 64, BF16, f"ZT{g}", "p (h t) -> p h t", h=6) for g in range(2)]
        AK = [A(6 * 64, BF16, f"AK{g}", "p (h t) -> p h t", h=6) for g in range(2)]
        RB = [A(6 * 64, BF16, f"RB{g}", "p (h t) -> p h t", h=6) for g in range(2)]
        RK = [A(6 * 64, BF16, f"RK{g}", "p (h t) -> p h t", h=6) for g in range(2)]
        W0 = [A(6 * 64, BF16, f"W0{g}", "p (h t) -> p h t", h=6) for g in range(2)]
        WB = [A(6 * 64, BF16, f"WB{g}", "p (h t) -> p h t", h=6) for g in range(2)]
        UB = [A(6 * 64, BF16, f"UB{g}", "p (h t) -> p h t", h=6) for g in range(2)]
        OT = [A(6 * 64, F32, f"OT{g}", "p (h t) -> p h t", h=6) for g in range(2)]
        OO = A(768, F32, "OO", "p (h t) -> p h t", h=12)
        ON = A(768, F32, "ON", "p (h t) -> p h t", h=12)
        SQ = ON
        ST = [A(12, F32, f"gst{i}") for i in range(4)]
        HD = A(NKC * T, BF16, "HD", "p (c t) -> p c t", c=NKC)
        MQ = A(2 * T, BF16, "MQ", "p (c t) -> p c t", c=2)
        E = [TD, SDG, tm[13]]
        lt = tm[0:8]

        self.dma_in('sp', RP.ap, self.rpar_d, RP.res)
        self.dma_in('pool', LW.ap, self.lw_d, LW.res)
        self.dma_in('pool', GW.ap, self.gw_d, GW.res)
        self.ts('dve', OM.ap[:, 0:20], RP.ap[:, PC_MU:PC_MU + 20], -1.0, 1.0, ALU.mult, ALU.add, [RP.res], [OM.res])
        self.ts('dve', OM.ap[:, 20:26], RP.ap[:, PC_KA:PC_KA + 6], -1.0, 1.0, ALU.mult, ALU.add, [RP.res], [OM.res])
        self.memset('dve', SM.ap, 1.0, SM.res)
        self.memset('dve', SM.ap[:, 0:T:64], 0.0, SM.res)
        self.memset('dve', CAR.ap, 0.0, CAR.res)
        self.memset('dve', HF.ap, 0.0, HF.res)
        self.memset('dve', HB.ap, 0.0, HB.res)
        ones = self.ones_f

        def asel(dst, pattern, cmul, op, base=0):
            for hp in range(2):
                pr = slice(hp * 64, (hp + 1) * 64)
                P.op('pool', (lambda pr: lambda e: e.affine_select(
                    out=dst.ap[pr, :], in_=ones.ap[pr, 0:64], pattern=pattern, compare_op=op, fill=0.0,
                    base=base, channel_multiplier=cmul))(pr), reads=[ones.res], writes=[dst.res])
        asel(MLT, [[1, 64]], -1, ALU.is_gt)
        asel(MLE, [[1, 64]], -1, ALU.is_ge)
        asel(MGT, [[-1, 64]], 1, ALU.is_gt)
        asel(IDB, [[-1, 64]], 1, ALU.is_equal)

        def bc6(m):
            return m.ap.unsqueeze(1).to_broadcast([128, 6, 64])
        cut(1)

        rpc = lambda c: RP.ap[:, c:c + 1]

        for blk in range(self.rw_blocks):
            t0 = blk * T
            tokb = slice(t0, t0 + T)
            wt = {}

            def inproj(cc, dst_tile, shifted):
                w = self.w_get('A')
                b = self.bank()
                for kc in range(NKC):
                    self.mm(b.ap[:, 0:T], w.ap[:, kc, :], XB[:, kc, tokb], kc == 0, kc == NKC - 1,
                            [w.res] + self.xr('b', kc, t0, T), [b.res])
                if shifted:
                    self.copy('dve', dst_tile.ap[:, 0:1], CAR.ap[:, cc:cc + 1], [CAR.res], [dst_tile.res])
                    self.copy('act', dst_tile.ap[:, 1:T + 1], b.ap[:, 0:T], [b.res], [dst_tile.res])
                    self.copy('act', CAR.ap[:, cc:cc + 1], dst_tile.ap[:, T:T + 1], [dst_tile.res], [CAR.res])

            def shift(praw, cc, out):
                self.asc(out.ap, praw.ap[:, 0:T], rpc(PC_MU + cc), None, [praw.res, RP.res], [out.res])
                self.stt(out.ap, praw.ap[:, 1:T + 1], OM.ap[:, cc:cc + 1], out.ap, ALU.mult, ALU.add,
                         [praw.res, OM.res, out.res], [out.res])

            inproj(18, PL[0], True)
            inproj(19, PL[1], True)
            zl0, zl1 = tm[0], tm[1]
            shift(PL[0], 18, zl0)
            shift(PL[1], 19, zl1)
            self.act(TD.ap[0:64, :], zl0.ap[0:64, :], AF.Tanh, [zl0.res], [TD.res])
            self.copy('dve', TD.ap[64:128, :], zl0.ap[64:128, :], [zl0.res], [TD.res])
            self.act(SDG.ap, zl1.ap, AF.Sigmoid, [zl1.res], [SDG.res])
            cut(2)

            for ch in range(6):
                cs = slice(ch * 128, (ch + 1) * 128)
                inproj(ch, PR[0], True)
                inproj(6 + ch, PR[1], True)
                inproj(12 + ch, PR[2], True)
                zr, zk, zv = tm[2], tm[3], tm[4]
                shift(PR[0], ch, zr)
                shift(PR[1], 6 + ch, zk)
                shift(PR[2], 12 + ch, zv)
                bw, ba, bg_ = self.bank(), self.bank(), self.bank()
                self.mm(bw.ap[:, 0:T], LW.ap[0:64, cs], TD.ap[0:64, :], True, True, [LW.res, TD.res], [bw.res])
                self.mm(ba.ap[:, 0:T], LW.ap[64:128, cs], TD.ap[64:128, :], True, True, [LW.res, TD.res], [ba.res])
                self.mm(bg_.ap[:, 0:T], GW.ap[:, cs], SDG.ap, True, True, [GW.res, SDG.res], [bg_.res])
                sg, a = tm[5], tm[6]
                self.act(sg.ap, bw.ap[:, 0:T], AF.Sigmoid, [bw.res, RP.res], [sg.res], bias=rpc(PC_W0 + ch), scale=1.0)
                self.act(a.ap, ba.ap[:, 0:T], AF.Sigmoid, [ba.res, RP.res], [a.res], bias=rpc(PC_A0 + ch), scale=1.0)
                self.copy('act', G.ap[:, ch, :], bg_.ap[:, 0:T], [bg_.res], [G.res])
                cut(31)
                kk, t1 = tm[7], tm[8]
                self.asc(kk.ap, zk.ap, rpc(PC_KK + ch), None, [zk.res, RP.res], [kk.res])
                self.act(t1.ap, kk.ap, AF.Square, [kk.res], [t1.res])
                bs = self.bank()
                self.mm(bs.ap[:, 0:T], self.BDF.ap, t1.ap, True, True, [self.BDF.res, t1.res], [bs.res])
                self.ts('dve', t1.ap, bs.ap[:, 0:T], 1e-24, None, ALU.max, None, [bs.res], [t1.res])
                self.rsqrt(t1.ap, t1.ap, 0.0, [t1.res], [t1.res])
                self.tt('dve', kk.ap, kk.ap, t1.ap, ALU.mult, [kk.res, t1.res], [kk.res])
                cut(32)
                k, bb = tm[9], tm[10]
                self.asc(t1.ap, a.ap, rpc(PC_KA + ch), OM.ap[:, 20 + ch:21 + ch], [a.res, RP.res, OM.res], [t1.res])
                self.tt('dve', k.ap, zk.ap, t1.ap, ALU.mult, [zk.res, t1.res], [k.res])
                self.tt('dve', bb.ap, kk.ap, a.ap, ALU.mult, [kk.res, a.res], [bb.res])
                Lp, e, t2 = tm[11], tm[12], tm[13]
                P.op('dve', lambda e_: e_.tensor_tensor_scan(out=Lp.ap, data0=SM.ap, data1=sg.ap, initial=0.0,
                                                             op0=ALU.mult, op1=ALU.add),
                     reads=[SM.res, sg.res], writes=[Lp.res])
                Lp4 = Lp.ap.rearrange("p (c t) -> p c t", c=4)
                cut(33)
                self.act(e.ap, Lp.ap, AF.Exp, [Lp.res], [e.res], scale=DECAY_C)
                self.copy('dve', GC.ap[:, ch, :], e.ap[:, 63:T:64], [e.res], [GC.res])
                self.tt('dve', Rt.ap[:, ch, :], zr.ap, e.ap, ALU.mult, [zr.res, e.res], [Rt.res])
                self.act(e.ap, Lp.ap, AF.Exp, [Lp.res], [e.res], scale=-DECAY_C)
                self.tt('dve', Bt.ap[:, ch, :], bb.ap, e.ap, ALU.mult, [bb.res, e.res], [Bt.res])
                self.tt('dve', Kt.ap[:, ch, :], k.ap, e.ap, ALU.mult, [k.res, e.res], [Kt.res])
                self.tt('dve', t2.ap, Lp.ap, sg.ap, ALU.subtract, [Lp.res, sg.res], [t2.res])
                self.act(e.ap, t2.ap, AF.Exp, [t2.res], [e.res], scale=DECAY_C)
                self.stt(At.ap[:, ch, :], kk.ap, -1.0, e.ap, ALU.mult, ALU.mult, [kk.res, e.res], [At.res])
                t24 = t2.ap.rearrange("p (c t) -> p c t", c=4)
                self.tt('dve', t24, Lp4[:, :, 63:64].to_broadcast([128, 4, 64]), Lp4, ALU.subtract, [Lp.res], [t2.res])
                self.act(e.ap, t2.ap, AF.Exp, [t2.res], [e.res], scale=DECAY_C)
                self.tt('dve', tb[0].ap, bb.ap, e.ap, ALU.mult, [bb.res, e.res], [tb[0].res])
                self.tt('dve', tb[1].ap, k.ap, e.ap, ALU.mult, [k.res, e.res], [tb[1].res])
                self.copy('act', tb[2].ap, zv.ap, [zv.res], [tb[2].res])
                cut(34)
                for pr_ in range(2):
                    ps_ = slice(pr_ * 128, (pr_ + 1) * 128)
                    bt = self.bank()
                    self.tr(bt.ap[:, 0:128], tb[0].ap[:, ps_], [tb[0].res], [bt.res])
                    self.tr(bt.ap[:, 128:256], tb[1].ap[:, ps_], [tb[1].res], [bt.res])
                    self.tr(bt.ap[:, 256:384], tb[2].ap[:, ps_], [tb[2].res], [bt.res])
                    cut(341)
                    self.copy('act', Bh.ap[:, pr_, cs], bt.ap[:, 0:128], [bt.res], [Bh.res])
                    cut(342)
                    self.copy('dve', Kh.ap[:, pr_, cs], bt.ap[:, 128:256], [bt.res], [Kh.res])
                    cut(343)
                    self.copy('act', Vt.ap[:, pr_, cs], bt.ap[:, 256:384], [bt.res], [Vt.res])
                cut(35)
                self.tt('dve', t1.ap, zr.ap, k.ap, ALU.mult, [zr.res, k.res], [t1.res])
                self.asc(t1.ap, t1.ap, rpc(PC_RK + ch), None, [t1.res, RP.res], [t1.res])
                bs2 = self.bank()
                self.mm(bs2.ap[:, 0:T], self.BDF.ap, t1.ap, True, True, [self.BDF.res, t1.res], [bs2.res])
                self.tt('dve', BON.ap[:, ch, :], bs2.ap[:, 0:T], zv.ap, ALU.mult, [bs2.res, zv.res], [BON.res])
                cut(3)

            for c2 in range(2):
                w = self.w_get('A')
                b = self.bank()
                for kc in range(NKC):
                    self.mm(b.ap[:, 0:T], w.ap[:, kc, :], XB[:, kc, tokb], kc == 0, kc == NKC - 1,
                            [w.res] + self.xr('b', kc, t0, T), [b.res])
                self.copy('act', MQ.ap[:, c2, :], b.ap[:, 0:T], [b.res], [MQ.res])

            for pr_ in range(2):
                def hidx(hg, hh):
                    h = 2 * hh + hg
                    return h, hh, slice(hg * 64, hg * 64 + 64)

                def tsl(c2):
                    o = pr_ * 128 + c2 * 64
                    return slice(o, o + 64)

                def grp(dst, lhs, rhs, mask, extra=None):
                    for hg in range(2):
                        b = self.bank()
                        for c2 in range(2):
                            for hh in range(6):
                                h, ch, hp = hidx(hg, hh)
                                self.mm(b.ap[c2 * 64:(c2 + 1) * 64, hh * 64:(hh + 1) * 64],
                                        lhs.ap[hp, ch, tsl(c2)], rhs.ap[hp, ch, tsl(c2)], True, True,
                                        [lhs.res, rhs.res], [b.res])
                        bv = b.ap[:, 0:384].rearrange("p (h t) -> p h t", h=6)
                        self.tt('dve', dst[hg].ap, bv, bc6(mask), ALU.mult, [b.res, mask.res], [dst[hg].res])

                grp(MX[0], Bt, At, MLT)
                grp(MY[0], At, Bt, MGT)
                grp(AK, Kt, At, MLT)
                grp(RB, Bt, Rt, MLE)
                grp(RK, Kt, Rt, MLE)
                cut(4)
                for hg in range(2):
                    self.tt('dve', ZT[hg].ap, MX[0][hg].ap, bc6(IDB), ALU.add, [MX[0][hg].res, IDB.res], [ZT[hg].res])

                def grp2(dst, lhs, rhs, hg, accum_into=None, evac='act'):
                    for c2 in range(2):
                        b = self.bank()
                        pc = slice(c2 * 64, (c2 + 1) * 64)
                        for hh in range(6):
                            self.mm(b.ap[pc, hh * 64:(hh + 1) * 64], lhs.ap[pc, hh, :], rhs.ap[pc, hh, :], True, True,
                                    [lhs.res, rhs.res], [b.res])
                        bv = b.ap[pc, 0:384].rearrange("p (h t) -> p h t", h=6)
                        if accum_into is not None:
                            self.tt('dve', accum_into.ap[pc], bv, accum_into.ap[pc], ALU.add, [b.res, accum_into.res],
                                    [accum_into.res])
                        else:
                            self.copy(evac if c2 == 0 else ('dve' if evac == 'act' else 'act'), dst.ap[pc], bv, [b.res], [dst.res])

                cur = 0
                for kstep in range(6):
                    for hg in range(2):
                        Xc, Yc = MX[cur][hg], MY[cur][hg]
                        if kstep >= 1:
                            grp2(None, Yc, ZT[hg], hg, accum_into=ZT[hg])
                        if kstep <= 3:
                            grp2(MX[1 - cur][hg], Yc, Xc, hg, evac='act')
                        if kstep <= 4:
                            grp2(MY[1 - cur][hg], Xc, Yc, hg, evac='dve')
                    cur = 1 - cur
                for hg in range(2):
                    for c2 in range(2):
                        b = self.bank()
                        pc = slice(c2 * 64, (c2 + 1) * 64)
                        for hh in range(6):
                            h = 2 * hh + hg
                            self.mm(b.ap[pc, hh * 64:(hh + 1) * 64], AK[hg].ap[pc, hh, :], Vt.ap[pc, pr_, h * 64:(h + 1) * 64],
                                    True, True, [AK[hg].res, Vt.res], [b.res])
                        self.copy('act' if c2 == 0 else 'dve', W0[hg].ap[pc],
                                  b.ap[pc, 0:384].rearrange("p (h t) -> p h t", h=6), [b.res], [W0[hg].res])

                cut(5)
                ps = self.ps
                for c2 in range(2):
                    pc = slice(c2 * 64, (c2 + 1) * 64)
                    cglob = pr_ * 2 + c2
                    for hg in range(2):
                        b = ps[hg]
                        for hh in range(6):
                            h, ch, hp = hidx(hg, hh)
                            self.mm(b.ap[pc, hh * 64:(hh + 1) * 64], At.ap[hp, ch, tsl(c2)], HB.ap[hp, ch, :], True, True,
                                    [At.res, HB.res], [b.res])
                        self.tt('dve', WB[hg].ap[pc], b.ap[pc, 0:384].rearrange("p (h t) -> p h t", h=6), W0[hg].ap[pc],
                                ALU.add, [b.res, W0[hg].res], [WB[hg].res])
                    for hg in range(2):
                        b = ps[3 + hg]
                        for hh in range(6):
                            h, ch, hp = hidx(hg, hh)
                            self.mm(b.ap[pc, hh * 64:(hh + 1) * 64], Rt.ap[hp, ch, tsl(c2)], HB.ap[hp, ch, :], True, True,
                                    [Rt.res, HB.res], [b.res])
                        self.copy('act', OT[hg].ap[pc], b.ap[pc, 0:384].rearrange("p (h t) -> p h t", h=6), [b.res], [OT[hg].res])
                    for hg in range(2):
                        b = ps[hg]
                        for hh in range(6):
                            self.mm(b.ap[pc, hh * 64:(hh + 1) * 64], ZT[hg].ap[pc, hh, :], WB[hg].ap[pc, hh, :], True, True,
                                    [ZT[hg].res, WB[hg].res], [b.res])
                        self.copy('act', UB[hg].ap[pc], b.ap[pc, 0:384].rearrange("p (h t) -> p h t", h=6), [b.res], [UB[hg].res])
                    bH = ps[2]
                    for hg in range(2):
                        for hh in range(6):
                            h, ch, hp = hidx(hg, hh)
                            hs = slice(h * 64, (h + 1) * 64)
                            self.mm(bH.ap[hp, ch * 64:(ch + 1) * 64], Bh.ap[pc, pr_, hs], UB[hg].ap[pc, hh, :], True, False,
                                    [Bh.res, UB[hg].res], [bH.res])
                            self.mm(bH.ap[hp, ch * 64:(ch + 1) * 64], Kh.ap[pc, pr_, hs], Vt.ap[pc, pr_, hs], False, True,
                                    [Kh.res, Vt.res], [bH.res])
                    for hg in range(2):
                        b = ps[5 + hg]
                        for hh in range(6):
                            h = 2 * hh + hg
                            hs = slice(h * 64, (h + 1) * 64)
                            self.mm(b.ap[pc, hh * 64:(hh + 1) * 64], RB[hg].ap[pc, hh, :], UB[hg].ap[pc, hh, :], True, False,
                                    [RB[hg].res, UB[hg].res], [b.res])
                            self.mm(b.ap[pc, hh * 64:(hh + 1) * 64], RK[hg].ap[pc, hh, :], Vt.ap[pc, pr_, hs], False, True,
                                    [RK[hg].res, Vt.res], [b.res])
                        self.tt('dve', OO.ap.rearrange("p (c g) t -> p c g t", g=2)[pc, :, hg, :],
                                b.ap[pc, 0:384].rearrange("p (h t) -> p h t", h=6),
                                OT[hg].ap[pc], ALU.add, [b.res, OT[hg].res], [OO.res])
                    gcb = GC.ap[:, :, cglob:cglob + 1].to_broadcast([128, 6, 64])
                    self.tt('dve', HF.ap, HF.ap, gcb, ALU.mult, [HF.res, GC.res], [HF.res])
                    self.tt('dve', HF.ap, HF.ap, bH.ap[:, 0:384].rearrange("p (c i) -> p c i", c=6), ALU.add,
                            [HF.res, bH.res], [HF.res])
                    self.copy('act', HB.ap, HF.ap, [HF.res], [HB.res])

                cut(6)
                s1, s2, mean, rstd = ST
                P.op('dve', lambda e_: e_.tensor_reduce(out=s1.ap, in_=OO.ap, axis=AX.X, op=ALU.add),
                     reads=[OO.res], writes=[s1.res])
                self.act(SQ.ap, OO.ap, AF.Square, [OO.res], [SQ.res])
                P.op('dve', lambda e_: e_.tensor_reduce(out=s2.ap, in_=SQ.ap, axis=AX.X, op=ALU.add),
                     reads=[SQ.res], writes=[s2.res])
                self.ts('dve', mean.ap, s1.ap, 1.0 / 64, None, ALU.mult, None, [s1.res], [mean.res])
                self.tt('dve', s1.ap, mean.ap, mean.ap, ALU.mult, [mean.res], [s1.res])
                self.stt(s2.ap, s2.ap, 1.0 / 64, s1.ap, ALU.mult, ALU.subtract, [s2.res, s1.res], [s2.res])
                self.rsqrt(rstd.ap, s2.ap, LNX_EPS, [s2.res], [rstd.res])
                self.tt('dve', ON.ap, OO.ap, mean.ap.unsqueeze(2).to_broadcast([128, 12, 64]), ALU.subtract,
                        [OO.res, mean.res], [ON.res])
                self.tt('dve', ON.ap, ON.ap, rstd.ap.unsqueeze(2).to_broadcast([128, 12, 64]), ALU.mult,
                        [ON.res, rstd.res], [ON.res])
                ONf = ON.ap.rearrange("p h t -> p (h t)")
                self.copy('act', ONH.ap, ONf, [ON.res], [ONH.res])
                self.tt('dve', ONL.ap, ONf, ONH.ap, ALU.subtract, [ON.res, ONH.res], [ONL.res])
                for half in range(2):
                    bt = self.bank()
                    for j in range(3):
                        ch = half * 3 + j
                        self.tr(bt.ap[:, j * 128:(j + 1) * 128], ONH.ap[:, ch * 128:(ch + 1) * 128], [ONH.res], [bt.res],
                                True, False)
                        self.tr(bt.ap[:, j * 128:(j + 1) * 128], ONL.ap[:, ch * 128:(ch + 1) * 128], [ONL.res], [bt.res],
                                False, True)
                    for j in range(3):
                        ch = half * 3 + j
                        y = tm[j]
                        bsl = slice(pr_ * 128, (pr_ + 1) * 128)
                        self.act(y.ap[:, 0:128], bt.ap[:, j * 128:(j + 1) * 128], AF.Identity, [bt.res, RP.res], [y.res],
                                 scale=rpc(PC_LG + ch), bias=rpc(PC_LB + ch))
                        self.tt('dve', y.ap[:, 0:128], y.ap[:, 0:128], BON.ap[:, ch, bsl], ALU.add, [y.res, BON.res], [y.res])
                        self.tt('dve', HD.ap[:, ch, bsl], y.ap[:, 0:128], G.ap[:, ch, bsl], ALU.mult, [y.res, G.res], [HD.res])

            cut(7)
            self.mem_attn(MQ.ap, MQ.res, HD.ap, HD.res, 0, T, 0, E)
            self.dbg(f"hd{blk}", HD.ap, [128, NKC, T], [HD.res], BF16)
            self.out_proj_ln(0, HD.ap, HD.res, 0, t0, T, lt)
    def sb_stage(self):
        P = self.P
        XB = self.XB
        self.aoff = NKC * S // 2
        A = self.alloc
        QT = A(6 * S, BF16, "QT", "p (c t) -> p c t", c=6)
        KT = A(6 * S, BF16, "KT", "p (c t) -> p c t", c=6)
        VT = A(16 * 768, BF16, "VTs", "p (s c) -> p s c", s=16)
        MQ = A(2 * S, BF16, "MQs", "p (c t) -> p c t", c=2)
        QTr = [[P.res(f"qt{c}_{t}") for t in range(4)] for c in range(6)]
        KTr = [P.res(f"kt{c}") for c in range(6)]

        for cc in range(20):
            w = self.w_get('A')
            if 12 <= cc < 18:
                for g4 in range(4):
                    b = self.bank()
                    for j in range(4):
                        st = g4 * 4 + j
                        for kc in range(NKC):
                            self.mm(b.ap[:, j * 128:(j + 1) * 128], XB[:, kc, st * 128:(st + 1) * 128], w.ap[:, kc, :],
                                    kc == 0, kc == NKC - 1, [w.res] + self.xr('b', kc, st * 128, 128), [b.res])
                    dst = VT.ap[:, g4 * 4:(g4 + 1) * 4, (cc - 12) * 128:(cc - 11) * 128]
                    self.copy('act' if g4 % 2 == 0 else 'dve', dst, b.ap[:].rearrange("p (j c) -> p j c", j=4), [b.res], [VT.res])
                continue
            for tt in range(4):
                tok = slice(tt * 512, (tt + 1) * 512)
                b = self.bank()
                for kc in range(NKC):
                    self.mm(b.ap[:], w.ap[:, kc, :], XB[:, kc, tok], kc == 0, kc == NKC - 1,
                            [w.res] + self.xr('b', kc, tt * 512, 512), [b.res])
                eng = 'act' if tt % 2 == 0 else 'dve'
                if cc < 6:
                    self.copy(eng, QT.ap[:, cc, tok], b.ap[:], [b.res], [QTr[cc][tt]])
                elif cc < 12:
                    self.copy(eng, KT.ap[:, cc - 6, tok], b.ap[:], [b.res], [KTr[cc - 6]])
                else:
                    self.copy(eng, MQ.ap[:, cc - 18, tok], b.ap[:], [b.res], [MQ.res])
        P.barrier()

        save = self.aoff
        self.aoff = 0
        TRI = A(128, BF16, "tri")
        TRC = A(128, BF16, "trc")
        MSK = [A(512, BF16, f"dmask{o}") for o in range(4)]
        EX = [A(512, F32, f"ex{i}") for i in range(2)]
        SP = [A(512, F32, f"sp{i}") for i in range(3)]
        SPH = [A(512, BF16, f"sph{i}") for i in range(5)]
        SPL = [A(512, BF16, f"spl{i}") for i in range(5)]
        ARG = [A(512, F32, f"arg{i}") for i in range(2)]
        ATT = [A(512, BF16, f"att{i}") for i in range(3)]
        assert self.aoff <= NKC * S // 2, self.aoff
        ones_b = self.ones_b
        P.op('pool', lambda e: e.affine_select(out=TRI.ap, in_=ones_b.ap[:], pattern=[[-1, 128]], compare_op=ALU.is_gt,
                                               fill=0.0, base=0, channel_multiplier=1), reads=[ones_b.res], writes=[TRI.res])
        P.op('pool', lambda e: e.affine_select(out=TRC.ap, in_=ones_b.ap[:], pattern=[[1, 128]], compare_op=ALU.is_ge,
                                               fill=0.0, base=0, channel_multiplier=-1), reads=[ones_b.res], writes=[TRC.res])
        self.memset('dve', EX[0].ap, 1.0, EX[0].res)
        for o in range(4):
            P.op('pool', (lambda o: lambda e: e.affine_select(out=MSK[o].ap, in_=EX[0].ap, pattern=[[1, 512]],
                                                              compare_op=ALU.is_gt, fill=0.0, base=-128 * o,
                                                              channel_multiplier=-1))(o),
                 reads=[EX[0].res], writes=[MSK[o].res])

        ps = self.ps
        pairs = []
        it = 0
        for hpair in range(6):
            for qt in range(4):
                nkt = 4 * (qt + 1)
                for idx, kt in enumerate(range(nkt - 1, -1, -1)):
                    for j in range(2):
                        pairs.append(dict(h=2 * hpair + j, qt=qt, kt=kt, idx=idx, nkt=nkt, it=it + j, g=len(pairs)))
                it += 2

        def bufs(p):
            g = p['g']
            return dict(zb=ps[2 + g % 3], ex=EX[g % 2], sp=SP[g % 3], sph=SPH[g % 5], spl=SPL[g % 5], arg=ARG[g % 2],
                        att=ATT[g % 3], xs=ps[5 + p['it'] % 2], ob=ps[p['it'] % 2])

        def stage0(p):
            b = bufs(p)
            h, qt, kt = p['h'], p['qt'], p['kt']
            ch, hp = h // 2, slice((h % 2) * 64, (h % 2) * 64 + 64)
            qtok = slice(qt * 512, (qt + 1) * 512)
            diag = kt - 4 * qt
            zb, ex, sp, sph, spl = b['zb'], b['ex'], b['sp'], b['sph'], b['spl']
            self.mm(zb.ap[:], KT.ap[hp, ch, kt * 128:(kt + 1) * 128], QT.ap[hp, ch, qtok], True, True,
                    [KTr[ch], QTr[ch][qt]], [zb.res])
            self.act(ex.ap, zb.ap[:], AF.Exp, [zb.res], [ex.res], scale=0.125)
            if diag >= 0:
                self.tt('pool', ex.ap, ex.ap, MSK[diag].ap, ALU.mult, [ex.res, MSK[diag].res], [ex.res])
            self.act(sp.ap, ex.ap, AF.Ln, [ex.res], [sp.res], bias=self.cst(1.0), scale=1.0)
            self.act(sph.ap, ex.ap, AF.Ln, [ex.res], [sph.res], bias=self.cst(1.0), scale=1.0)
            self.tt('dve', spl.ap, sp.ap, sph.ap, ALU.subtract, [sp.res, sph.res], [spl.res])

        def stage1(p, prev):
            b = bufs(p)
            diag = p['kt'] - 4 * p['qt']
            zb, sp, sph, spl, arg, att, xs = b['zb'], b['sp'], b['sph'], b['spl'], b['arg'], b['att'], b['xs']
            first = p['idx'] == 0
            if not first:
                pb = bufs(prev)
                self.mm(xs.ap[:], TRC.ap, pb['sph'].ap, False, False, [TRC.res, pb['sph'].res], [xs.res], skip=True)
                self.mm(xs.ap[:], TRC.ap, pb['spl'].ap, False, False, [TRC.res, pb['spl'].res], [xs.res], skip=True)
            self.mm(xs.ap[:], TRI.ap, sph.ap, first, False, [TRI.res, sph.res], [xs.res], skip=not first)
            self.mm(xs.ap[:], TRI.ap, spl.ap, False, True, [TRI.res, spl.res], [xs.res], skip=not first)
            self.stt(arg.ap, zb.ap[:], 0.125, sp.ap, ALU.mult, ALU.subtract, [zb.res, sp.res], [arg.res])
            self.tt('dve', arg.ap, arg.ap, xs.ap[:], ALU.subtract, [arg.res, xs.res], [arg.res])

        def stage1b(p):
            b = bufs(p)
            diag = p['kt'] - 4 * p['qt']
            arg, att = b['arg'], b['att']
            self.act(att.ap, arg.ap, AF.Exp, [arg.res], [att.res])
            if diag >= 0:
                self.tt('pool', att.ap, att.ap, MSK[diag].ap, ALU.mult, [att.res, MSK[diag].res], [att.res])

        def stage2(p):
            b = bufs(p)
            h, qt, kt = p['h'], p['qt'], p['kt']
            ch, hp = h // 2, slice((h % 2) * 64, (h % 2) * 64 + 64)
            qtok = slice(qt * 512, (qt + 1) * 512)
            ob, att = b['ob'], b['att']
            self.mm(ob.ap[hp, :], VT.ap[:, kt, h * 64:(h + 1) * 64], att.ap, p['idx'] == 0, p['idx'] == p['nkt'] - 1,
                    [VT.res, att.res], [ob.res])
            if p['idx'] == p['nkt'] - 1:
                self.copy('act', QT.ap[hp, ch, qtok], ob.ap[hp, :], [ob.res], [QTr[ch][qt]])

        N = len(pairs)
        for step in range(N + 5):
            if 0 <= step - 2 < N:
                p = pairs[step - 2]
                stage1(p, pairs[step - 4] if p['idx'] > 0 else None)
            if 0 <= step - 3 < N:
                stage1b(pairs[step - 3])
            if step < N:
                stage0(pairs[step])
            if 0 <= step - 5 < N:
                stage2(pairs[step - 5])
        P.barrier()
        self.aoff = 0
        E = [A(512, BF16, "E0s"), A(512, BF16, "E1s"), A(512, F32, "E2s")]
        lt = [A(512, F32, f"lnts{i}") for i in range(8)]
        assert self.aoff <= NKC * S // 2, self.aoff
        self.aoff = save

        for tt in range(4):
            t0 = tt * 512
            HDv = None
            self.mem_attn_sb(MQ, t0, E)
            self.out_proj_ln_sb(QT, QTr, MQ, t0, lt)

    def mem_attn_sb(self, MQ, t0, E):
        T = 512
        for h in range(4):
            cq, hp = h // 2, h % 2
            pr = slice(hp * 64, (hp + 1) * 64)
            for mt in range(2):
                b = self.bank()
                self.mm(b.ap[:, 0:T], self.MK.ap[pr, cq, mt * 128:(mt + 1) * 128], MQ.ap[pr, cq, t0:t0 + T], True, True,
                        [self.MK.res, MQ.res], [b.res])
                self.act(E[mt].ap[:, 0:T], b.ap[:, 0:T], AF.Exp, [b.res], [E[mt].res], scale=0.125)
            bn = self.bank()
            bd = self.bank()
            for mt in range(2):
                self.mm(bn.ap[pr, 0:T], self.MV.ap[:, mt, cq, hp * 64:(hp + 1) * 64], E[mt].ap[:, 0:T], mt == 0, mt == 1,
                        [self.MV.res, E[mt].res], [bn.res])
            for mt in range(2):
                self.mm(bd.ap[pr, 0:T], self.ones_b.ap[:, 0:64], E[mt].ap[:, 0:T], mt == 0, mt == 1,
                        [self.ones_b.res, E[mt].res], [bd.res])
            rd = E[2]
            self.P.op('dve', (lambda o, i: lambda e: e.reciprocal(out=o, in_=i))(rd.ap[pr, 0:T], bd.ap[pr, 0:T]),
                      reads=[bd.res], writes=[rd.res])
            self.tt('dve', MQ.ap[pr, cq, t0:t0 + T], bn.ap[pr, 0:T], rd.ap[pr, 0:T], ALU.mult,
                    [bn.res, rd.res], [MQ.res])

    def out_proj_ln_sb(self, QT, QTr, MQ, t0, tmps):
        T = 512
        tt = t0 // 512
        lncol = (1 * 3 + 1) * 8
        s1, s2 = self.ps[0], self.ps[1]
        for dc in range(NKC):
            wo = self.w_get('A')
            by = self.bank()
            for cch in range(NKC):
                if cch < 6:
                    rhs, rr = QT.ap[:, cch, t0:t0 + T], QTr[cch][tt]
                else:
                    rhs, rr = MQ.ap[:, cch - 6, t0:t0 + T], MQ.res
                self.mm(by.ap[:, 0:T], wo.ap[:, cch, :], rhs, cch == 0, cch == NKC - 1, [wo.res, rr], [by.res])
            self.resid_stats(dc, t0, T, by, ALPHA, s1, s2, tmps[2 + dc % 2])
        self.ln_finish(t0, T, s1, s2, LN_EPS, lncol, tmps, write_xb=False)


def tile_a(w):
    lead = w.shape[:-2]
    n = w.shape[-1] // 128
    w = w.reshape(lead + (NKC, 128, n, 128))
    nd = len(lead)
    w = np.transpose(w, tuple(range(nd)) + (nd + 2, nd + 1, nd + 0, nd + 3))
    return np.ascontiguousarray(w).reshape(lead + (n, 128, NKC * 128))


def tile_d(w):
    lead = w.shape[:-2]
    w = w.reshape(lead + (NFC, 128, NKC, 128))
    nd = len(lead)
    w = np.transpose(w, tuple(range(nd)) + (nd + 2, nd + 1, nd + 0, nd + 3))
    return np.ascontiguousarray(w).reshape(lead + (NKC, 128, NFC * 128))


def cols(v, n):
    return np.asarray(v, dtype=np.float32).reshape(n, 128).T


def prep_shared(inputs):
    f = lambda a: np.asarray(a, dtype=np.float32)
    sh = {}
    for which in (1, 2):
        sh[f"g{which}"] = tile_a(f(inputs[f"ffn{which}_w_gate"]))
        sh[f"u{which}"] = tile_a(f(inputs[f"ffn{which}_w_up"]))
        sh[f"d{which}"] = tile_d(f(inputs[f"ffn{which}_w_down"]))
    sh["lng"] = np.ascontiguousarray(f(inputs["ln_g"]).reshape(6, NKC, 128).transpose(2, 0, 1).reshape(128, 48))
    sh["lnb"] = np.ascontiguousarray(f(inputs["ln_b"]).reshape(6, NKC, 128).transpose(2, 0, 1).reshape(128, 48))
    sh["wout"] = tile_a(f(inputs["w_out"]))
    sh["wmem"] = tile_a(f(inputs["w_mem_kv"]))
    sh["rwin"] = tile_a(f(inputs["rwkv_w_in"])[0])
    sh["sbin"] = tile_a(f(inputs["sb_w_in"])[0])
    rp = np.zeros((128, 64), np.float32)
    rp[:, PC_MU:PC_MU + 20] = cols(f(inputs["rwkv_mu"])[0], 20)
    for off, key in ((PC_W0, "rwkv_w0"), (PC_A0, "rwkv_a0"), (PC_KK, "rwkv_k_k"), (PC_KA, "rwkv_k_a"),
                     (PC_RK, "rwkv_r_k"), (PC_LG, "rwkv_lnx_g"), (PC_LB, "rwkv_lnx_b")):
        rp[:, off:off + 6] = cols(f(inputs[key])[0].reshape(-1), 6)
    sh["rpar"] = rp
    sh["lw"] = np.ascontiguousarray(np.concatenate([f(inputs["rwkv_w_up"])[0], f(inputs["rwkv_a_up"])[0]], axis=0))
    sh["gw"] = np.ascontiguousarray(f(inputs["rwkv_g_up"])[0])
    return sh


_NC_CACHE = {}


def get_nc(stop_after=None, start_at=0, dbg=(), rw_blocks=8):
    key = (stop_after, start_at, tuple(dbg), rw_blocks)
    if key not in _NC_CACHE:
        _NC_CACHE[key] = Builder(stop_after, start_at, dbg, rw_blocks).build()
    return _NC_CACHE[key]


def make_in_maps(inputs, cores=8, x_override=None):
    sh = prep_shared(inputs)
    x = np.asarray(inputs["x"], dtype=np.float32) if x_override is None else x_override
    mem = np.asarray(inputs["mem"], dtype=np.float32)
    in_maps = []
    for b in range(cores):
        m = dict(sh)
        m["xT"] = np.ascontiguousarray(x[b].T)
        m["memT"] = np.ascontiguousarray(mem[b].T)
        in_maps.append(m)
    return in_maps


def run(inputs, cores=8, stop_after=None, start_at=0, dbg=(), trace=False, x_override=None, rw_blocks=8):
    sh = prep_shared(inputs)
    x = np.asarray(inputs["x"], dtype=np.float32) if x_override is None else x_override
    mem = np.asarray(inputs["mem"], dtype=np.float32)
    in_maps = []
    for b in range(cores):
        m = dict(sh)
        m["xT"] = np.ascontiguousarray(x[b].T)
        m["memT"] = np.ascontiguousarray(mem[b].T)
        in_maps.append(m)
    nc = get_nc(stop_after, start_at, dbg, rw_blocks)
    res = run_bass_kernel_spmd(nc, in_maps, core_ids=list(range(cores)), trace=trace)
    out = np.stack([np.ascontiguousarray(r["outT"].T) for r in res.results], axis=0)
    return out, res


def kernel(**inputs):
    out, _ = run(inputs, cores=8)
    return out.astype(np.float32)
```

```python
import os
import numpy as np
from contextlib import ExitStack

import concourse.bass as bass
import concourse.mybir as mybir
from concourse.bass_utils import run_bass_kernel_spmd

F32 = mybir.dt.float32
BF16 = mybir.dt.bfloat16
AF = mybir.ActivationFunctionType
ALU = mybir.AluOpType

D = 1024
S = 2048
DFF = 2816
NKC = 8
NFC = 22
NMEM = 256
ALPHA = 4.0 ** 0.25
LN_EPS = 1e-5
LNX_EPS = 64e-5
DECAY_C = -float(np.exp(-0.5))

ENGS = ['pe', 'act', 'dve', 'pool', 'sp']
SAME_ENGINE_SYNC = os.environ.get("SES", "1") == "1"


class Res:
    __slots__ = ('name', 'last_w', 'readers', 'dreaders', 'sem', 'dma_cnt', 'excl')

    def __init__(self, name):
        self.name = name
        self.excl = False
        self.last_w = None
        self.readers = {}
        self.dreaders = []
        self.sem = None
        self.dma_cnt = 0


class OpRec:
    __slots__ = ('eng', 'fn', 'is_dma', 'deps', 'flagged', 'count', 'dres', 'dval', 'idx')


class Prog:
    def __init__(self, nc, es):
        self.nc = nc
        self.es = es
        self.q = {e: [] for e in ENGS}
        self.nres = 0
        self.all_dma = []
        self.bar = {e: None for e in ENGS}

    def res(self, name=None):
        self.nres += 1
        return Res((name or "r") + str(self.nres))

    def _track(self, op, reads, writes):
        deps = []
        for r in reads:
            if r.last_w is not None:
                deps.append(r.last_w)
        for w in writes:
            if w.last_w is not None:
                deps.append(w.last_w)
            deps.extend(w.readers.values())
            deps.extend(w.dreaders)
        b = self.bar[op.eng]
        if b is not None:
            deps.extend(b)
            self.bar[op.eng] = None
        for r in reads:
            if op.is_dma:
                r.dreaders.append(op)
            else:
                r.readers[op.eng] = op
        for w in writes:
            w.last_w = op
            w.readers = {}
            w.dreaders = []
        best = {}
        out = []
        for d in deps:
            if d is op:
                continue
            if d.is_dma:
                out.append(d)
                continue
            if (not op.is_dma) and d.eng == 'pe' and op.eng == 'pe':
                continue
            if (not op.is_dma) and d.eng == op.eng and not SAME_ENGINE_SYNC:
                continue
            if d.eng not in best or best[d.eng].idx < d.idx:
                best[d.eng] = d
        out.extend(best.values())
        op.deps = out

    def op(self, eng, fn, reads=(), writes=()):
        if any(r.excl for r in reads):
            writes = list(writes) + [r for r in reads if r.excl]
            reads = [r for r in reads if not r.excl]
        o = OpRec()
        o.eng = eng
        o.fn = fn
        o.is_dma = False
        o.flagged = False
        o.count = None
        o.dres = None
        o.dval = None
        o.idx = len(self.q[eng])
        self._track(o, reads, writes)
        self.q[eng].append(o)
        return o

    def dma(self, eng, fn, sres, reads=(), writes=()):
        o = OpRec()
        o.eng = eng
        o.fn = fn
        o.is_dma = True
        o.flagged = False
        o.count = None
        o.dres = sres
        sres.dma_cnt += 16
        o.dval = sres.dma_cnt
        o.idx = len(self.q[eng])
        self._track(o, reads, writes)
        self.q[eng].append(o)
        self.all_dma.append(o)
        return o

    def barrier(self):
        deps = list(self.all_dma)
        self.all_dma = []
        for e in ENGS:
            for o in reversed(self.q[e]):
                if not o.is_dma:
                    deps.append(o)
                    break
        for e in ENGS:
            prev = self.bar[e]
            self.bar[e] = (prev or []) + deps

    def finalize(self, final_waits=()):
        nc = self.nc
        es = self.es
        for e in ENGS:
            for o in self.q[e]:
                for d in o.deps:
                    if not d.is_dma:
                        d.flagged = True
        for o in final_waits:
            if not o.is_dma:
                o.flagged = True
        esem = {}
        for e in ENGS:
            c = 0
            for o in self.q[e]:
                if o.is_dma:
                    if o.dres.sem is None:
                        o.dres.sem = es.enter_context(nc.semaphore("d_" + o.dres.name))
                elif o.flagged:
                    c += 1
                    o.count = c
            esem[e] = es.enter_context(nc.semaphore("e_" + e))

        def tok(d):
            if d.is_dma:
                return d.dres.sem, d.dval
            return esem[d.eng], d.count

        def emit(ename, eng):
            known = {}
            for o in self.q[ename]:
                need = {}
                for d in o.deps:
                    s, v = tok(d)
                    k = id(s)
                    if known.get(k, 0) >= v:
                        continue
                    if k not in need or need[k][1] < v:
                        need[k] = (s, v)
                for k, (s, v) in need.items():
                    eng.wait_ge(s, v)
                    known[k] = v
                ins = o.fn(eng)
                if o.is_dma:
                    ins.then_inc(o.dres.sem, 16)
                elif o.flagged:
                    ins.then_inc(esem[ename], 1)
            if ename == 'sp':
                for d in final_waits:
                    s, v = tok(d)
                    eng.wait_ge(s, v)

        with nc.Block() as block:
            @block.tensor
            def _(eng):
                emit('pe', eng)

            @block.scalar
            def _(eng):
                emit('act', eng)

            @block.vector
            def _(eng):
                emit('dve', eng)

            @block.gpsimd
            def _(eng):
                emit('pool', eng)

            @block.sync
            def _(eng):
                emit('sp', eng)


class CutHere(Exception):
    pass


import os
RW_CUT = int(os.environ.get("RW_CUT", "0"))


def cut(n):
    if RW_CUT == n:
        raise CutHere()


class Tile:
    __slots__ = ('ap', 'res')

    def __init__(self, ap, res):
        if not isinstance(ap, bass.AP):
            ap = ap[:]
        self.ap = ap
        self.res = res


AX = mybir.AxisListType
RW_ORDER = [18, 19] + [c for ch in range(6) for c in (ch, 6 + ch, 12 + ch)] + [20, 21]
PC_MU, PC_W0, PC_A0, PC_KK, PC_KA, PC_RK, PC_LG, PC_LB = 0, 20, 26, 32, 38, 44, 50, 56


class Builder:
    STAGES = ["l0_x1", "l0_x2", "l0_x3", "l1_x1", "l1_x2", "l1_x3"]

    def __init__(self, stop_after=None, start_at=0, dbg=(), rw_blocks=8):
        self.stop_after = stop_after
        self.start_at = start_at
        self.rw_blocks = rw_blocks
        last = self.STAGES.index(stop_after) if stop_after else 5
        self.run_stages = self.STAGES[start_at:last + 1]
        self.dbg_names = set(dbg)
        self.dbg_outs = []
        self.nc = bass.Bass("TRN2", target_bir_lowering=False)
        self.es = ExitStack()
        self.bank_rr = 0
        self.ring = list(range(8))

    def sb(self, name, shape, dt):
        return self.es.enter_context(self.nc.sbuf_tensor(name, shape, dt))

    def dram_in(self, name, shape):
        return self.nc.dram_tensor(name, list(shape), F32, kind="ExternalInput").ap()

    def mm(self, out, lhsT, rhs, start, stop, reads, writes, skip=False):
        return self.P.op('pe', lambda e: e.matmul(out, lhsT=lhsT, rhs=rhs, start=start, stop=stop, skip_group_check=skip),
                         reads=reads, writes=writes)

    def tr(self, out, in_, reads, writes, start=True, stop=True):
        ident = self.IDB16.ap
        return self.P.op('pe', lambda e: e.matmul(out, lhsT=in_, rhs=ident, start=start, stop=stop),
                         reads=list(reads) + [self.IDB16.res], writes=writes)

    def act(self, out, in_, func, reads, writes, scale=None, bias=None):
        kw = {}
        if scale is not None:
            kw['scale'] = scale
        if bias is not None:
            kw['bias'] = bias
        return self.P.op('act', lambda e: e.activation(out=out, in_=in_, func=func, **kw),
                         reads=reads, writes=writes)

    def asc(self, out, in_, scale, bias, reads, writes):
        if bias is None:
            return self.P.op('act', lambda e: e.activation(out=out, in_=in_, func=AF.Copy, scale=scale),
                             reads=reads, writes=writes)
        return self.P.op('act', lambda e: e.activation(out=out, in_=in_, func=AF.Identity, scale=scale, bias=bias),
                         reads=reads, writes=writes)

    def tt(self, eng, out, in0, in1, op, reads, writes):
        return self.P.op(eng, lambda e: e.tensor_tensor(out=out, in0=in0, in1=in1, op=op),
                         reads=reads, writes=writes)

    def ts(self, eng, out, in0, s1, s2, op0, op1, reads, writes):
        if op1 is None:
            return self.P.op(eng, lambda e: e.tensor_scalar(out=out, in0=in0, scalar1=s1, scalar2=None, op0=op0),
                             reads=reads, writes=writes)
        return self.P.op(eng, lambda e: e.tensor_scalar(out=out, in0=in0, scalar1=s1, scalar2=s2, op0=op0, op1=op1),
                         reads=reads, writes=writes)

    def stt(self, out, in0, scalar, in1, op0, op1, reads, writes):
        return self.P.op('dve', lambda e: e.scalar_tensor_tensor(out=out, in0=in0, scalar=scalar, in1=in1,
                                                                 op0=op0, op1=op1),
                         reads=reads, writes=writes)

    def rsqrt(self, out, in_, eps, reads, writes, eng='dve'):
        self.P.op('act', lambda e: e.activation(out=out, in_=in_, func=AF.Sqrt, bias=self.cst(eps), scale=1.0),
                  reads=list(reads) + [self.CONST.res], writes=writes)
        return self.P.op(eng, lambda e: e.reciprocal(out=out, in_=out), reads=writes, writes=writes)

    def cst(self, v, n=128, base=0):
        c = self.const_cols[v]
        return self.CONST.ap[base:base + n, c:c + 1]

    def copy(self, eng, out, in_, reads, writes):
        if eng == 'act':
            return self.P.op('act', lambda e: e.copy(out=out, in_=in_), reads=reads, writes=writes)
        return self.P.op(eng, lambda e: e.tensor_copy(out=out, in_=in_), reads=reads, writes=writes)

    def memset(self, eng, ap, v, res):
        return self.P.op(eng, lambda e: e.memset(ap, v), writes=[res])

    def dma_in(self, eng, dst, src, res):
        return self.P.dma(eng, lambda e: e.dma_start(out=dst, in_=src), res, writes=[res])

    def bank(self):
        ring = self.ring
        b = self.ps[ring[self.bank_rr % len(ring)]]
        self.bank_rr += 1
        return b

    def dbg(self, name, ap, shape, res_list, dt=F32):
        if name not in self.dbg_names:
            return
        t = self.nc.dram_tensor("dbg_" + name, list(shape), dt, kind="ExternalOutput").ap()
        r = self.P.res("dbg_" + name)
        o = self.P.dma('sp', lambda e: e.dma_start(out=t, in_=ap), r, reads=res_list)
        self.dbg_outs.append(o)

    def xr(self, kind, kc, t0, T):
        rr = self.XFr if kind == 'f' else self.XBr
        return [rr[kc][b] for b in range(t0 // 256, (t0 + T + 255) // 256)]

    def w_schedule(self):
        sched = []
        for i in range(4):
            sched.append(('A', self.wmem_d[i]))
        for st in self.run_stages:
            L = int(st[1])
            if st.endswith("x2"):
                sched.extend(self.attn_w_schedule(L))
                continue
            which = 1 if st.endswith("x1") else 2
            g, u, d = self.wd[f"g{which}"], self.wd[f"u{which}"], self.wd[f"d{which}"]
            for half in range(2):
                for fc in range(NFC):
                    sched.append(('A', g[L, fc]))
                    sched.append(('A', u[L, fc]))
                for dc in range(NKC):
                    sched.append(('D', d[L, dc]))
        return sched

    def attn_w_schedule(self, L):
        s = []
        if L == 0:
            for blk in range(self.rw_blocks):
                for cc in RW_ORDER:
                    s.append(('A', self.rwin_d[cc]))
                for dc in range(NKC):
                    s.append(('A', self.wout_d[0, dc]))
        else:
            for cc in range(20):
                s.append(('A', self.sbin_d[cc]))
            for tt in range(4):
                for dc in range(NKC):
                    s.append(('A', self.wout_d[1, dc]))
        return s

    def w_init(self):
        self.NA = 6
        self.ND = 2
        self.wslots = {'A': [], 'D': []}
        for i in range(self.NA):
            t = self.sb(f"wa{i}", [128, NKC, 128], BF16)
            self.wslots['A'].append(Tile(t, self.P.res(f"wa{i}")))
        for i in range(self.ND):
            o = 23552 + i * 1408
            t = self.carve(o, 1408, BF16, "p (f d) -> p f d", f=NFC)
            self.wslots['D'].append(Tile(t, self.P.res(f"wd{i}")))
        self.wsched = self.w_schedule()
        self.w_issued = 0
        self.w_next = 0
        self.w_kcount = {'A': 0, 'D': 0}
        self.w_tile_slot = []
        self.w_ahead = {'A': 3, 'D': 1}

    def _w_issue_one(self):
        kind, src = self.wsched[self.w_issued]
        n = self.w_kcount[kind]
        self.w_kcount[kind] = n + 1
        slots = self.wslots[kind]
        t = slots[n % len(slots)]
        self.w_tile_slot.append(t)
        if kind == 'A':
            dst = t.ap[:].rearrange("p a b -> p (a b)")
            self.P.dma('pool', lambda e: e.dma_start(out=dst, in_=src), t.res, writes=[t.res])
        else:
            dst = t.ap.rearrange("p a b -> p (a b)").rearrange("p (h x) -> p h x", h=2)
            s2 = src.rearrange("p (h x) -> p h x", h=2)
            self.P.dma('pool', lambda e: e.dma_start(out=dst, in_=s2), t.res, writes=[t.res])
        self.w_issued += 1

    def w_get(self, kind):
        i = self.w_next
        assert self.wsched[i][0] == kind, (i, self.wsched[i][0], kind)
        while self.w_issued <= i:
            self._w_issue_one()
        ahead_cnt = {'A': 0, 'D': 0}
        for j in range(i + 1, self.w_issued):
            ahead_cnt[self.wsched[j][0]] += 1
        while self.w_issued < len(self.wsched):
            k = self.wsched[self.w_issued][0]
            if ahead_cnt[k] >= self.w_ahead[k]:
                break
            self._w_issue_one()
            ahead_cnt[k] += 1
        self.w_next += 1
        return self.w_tile_slot[i]

    def build(self):
        nc = self.nc
        es = self.es
        with es:
            self.P = P = Prog(nc, es)
            self.xT = self.dram_in("xT", [D, S])
            self.memT = self.dram_in("memT", [D, NMEM])
            self.outT = nc.dram_tensor("outT", [D, S], F32, kind="ExternalOutput").ap()
            self.wd = {}
            for which in (1, 2):
                self.wd[f"g{which}"] = self.dram_in(f"g{which}", [2, NFC, 128, NKC * 128])
                self.wd[f"u{which}"] = self.dram_in(f"u{which}", [2, NFC, 128, NKC * 128])
                self.wd[f"d{which}"] = self.dram_in(f"d{which}", [2, NKC, 128, NFC * 128])
            self.lng_d = self.dram_in("lng", [128, 48])
            self.lnb_d = self.dram_in("lnb", [128, 48])
            self.wout_d = self.dram_in("wout", [2, NKC, 128, NKC * 128])
            self.wmem_d = self.dram_in("wmem", [4, 128, NKC * 128])
            self.rwin_d = self.dram_in("rwin", [22, 128, NKC * 128])
            self.sbin_d = self.dram_in("sbin", [20, 128, NKC * 128])
            self.rpar_d = self.dram_in("rpar", [128, 64])
            self.lw_d = self.dram_in("lw", [128, 768])
            self.gw_d = self.dram_in("gw", [128, 768])

            self.XF = self.sb("XF", [128, NKC, S], F32)
            self.XFr = [[P.res(f"xf{k}_{t}") for t in range(8)] for k in range(NKC)]
            self.XBr = [[P.res(f"xb{k}_{t}") for t in range(8)] for k in range(NKC)]
            self.LNG = Tile(self.sb("LNG", [128, 48], F32), P.res("lng"))
            self.LNB = Tile(self.sb("LNB", [128, 48], F32), P.res("lnb"))
            self.ones_f = Tile(self.sb("ones_f", [128, 128], F32), P.res("ones_f"))
            self.ones_b = Tile(self.sb("ones_b", [128, 128], BF16), P.res("ones_b"))
            self.IDF = Tile(self.sb("IDF", [128, 128], F32), P.res("idf"))
            self.BDF = Tile(self.sb("BDF", [128, 128], F32), P.res("bdf"))
            self.IDB16 = Tile(self.sb("IDB16", [128, 128], BF16), P.res("idb16"))
            self.MK = Tile(self.sb("MK", [128, 2, NMEM], BF16), P.res("mk"))
            self.MV = Tile(self.sb("MV", [128, 2, 2, 128], BF16), P.res("mv"))
            self.ARENA_F32 = 32600
            self.arena = self.sb("arena", [128, self.ARENA_F32], F32)
            self.w_init()
            self.ps = []
            for i in range(8):
                t = es.enter_context(nc.psum_tensor(f"ps{i}", [128, 512], F32))
                self.ps.append(Tile(t, P.res(f"ps{i}")))
                self.ps[-1].res.excl = True

            self.memset('dve', self.ones_f.ap[:], 1.0, self.ones_f.res)
            self.memset('dve', self.ones_b.ap[:], 1.0, self.ones_b.res)
            self.CONST = Tile(self.sb("CONST", [128, 8], F32), P.res("const"))
            self.const_cols = {}
            for ci, cv in enumerate([4.0 * LN_EPS, LN_EPS, LNX_EPS, 1.0, 0.0]):
                self.const_cols[cv] = ci
                self.memset('dve', self.CONST.ap[:, ci:ci + 1], cv, self.CONST.res)
            P.op('pool', lambda e: e.affine_select(out=self.IDF.ap[:], in_=self.ones_f.ap[:], pattern=[[-1, 128]],
                                                   compare_op=ALU.is_equal, fill=0.0, base=0, channel_multiplier=1),
                 reads=[self.ones_f.res], writes=[self.IDF.res])
            self.copy('dve', self.IDB16.ap, self.IDF.ap, [self.IDF.res], [self.IDB16.res])
            self.memset('dve', self.BDF.ap[:], 0.0, self.BDF.res)
            self.memset('dve', self.BDF.ap[0:64, 0:64], 1.0, self.BDF.res)
            self.memset('dve', self.BDF.ap[64:128, 64:128], 1.0, self.BDF.res)

            self.dma_in('sp', self.LNG.ap[:], self.lng_d, self.LNG.res)
            self.dma_in('sp', self.LNB.ap[:], self.lnb_d, self.LNB.res)
            for kc in range(NKC):
                for t in range(8):
                    r = self.XFr[kc][t]
                    self.dma_in('sp', self.XF[:, kc, t * 256:(t + 1) * 256],
                                self.xT[kc * 128:(kc + 1) * 128, t * 256:(t + 1) * 256], r)

            self.arena_ffn()
            self.mem_setup()
            for kc in range(NKC):
                for t in range(4):
                    sl = slice(t * 512, (t + 1) * 512)
                    self.copy('dve' if (kc + t) % 2 == 0 else 'act', self.XB[:, kc, sl], self.XF[:, kc, sl],
                              reads=self.xr('f', kc, t * 512, 512), writes=self.xr('b', kc, t * 512, 512))

            for st in self.run_stages:
                L = int(st[1])
                if st.endswith("x1"):
                    self.ffn(L, 1)
                elif st.endswith("x3"):
                    self.ffn(L, 2)
                else:
                    P.barrier()
                    if L == 0:
                        try:
                            self.rwkv_stage()
                        except CutHere:
                            pass
                    else:
                        self.sb_stage()
                    P.barrier()
                    self.arena_ffn()
                    if L == 1:
                        for kc in range(NKC):
                            for t in range(4):
                                sl = slice(t * 512, (t + 1) * 512)
                                self.copy('dve' if (kc + t) % 2 == 0 else 'act', self.XB[:, kc, sl], self.XF[:, kc, sl],
                                          reads=self.xr('f', kc, t * 512, 512), writes=self.xr('b', kc, t * 512, 512))

            finals = list(self.dbg_outs)
            for kc in range(NKC):
                for t in range(4):
                    rl = self.xr('f', kc, t * 512, 512)
                    src = self.XF[:, kc, t * 512:(t + 1) * 512]
                    dst = self.outT[kc * 128:(kc + 1) * 128, t * 512:(t + 1) * 512]
                    o = P.dma('sp', (lambda dst, src: lambda e: e.dma_start(out=dst, in_=src))(dst, src), rl[0], reads=rl)
                    finals.append(o)
            P.finalize(final_waits=finals)
        return nc

    def carve(self, off_words, nwords, dt, pattern=None, **kw):
        ap = self.arena[:, off_words:off_words + nwords]
        if dt == BF16:
            ap = ap.bitcast(BF16)
        if pattern:
            ap = ap.rearrange(pattern, **kw)
        return ap

    def alloc(self, nelem, dt, name, pattern=None, **kw):
        nwords = nelem if dt == F32 else (nelem + 1) // 2
        ap = self.carve(self.aoff, nwords, dt, pattern, **kw)
        self.aoff += nwords
        assert self.aoff <= self.ARENA_F32, (name, self.aoff)
        return Tile(ap, self.P.res(name))

    def arena_ffn(self):
        P = self.P
        o = 0
        self.XB = self.carve(o, NKC * S // 2, BF16, "p (k t) -> p k t", k=NKC)
        o += NKC * S // 2
        self.HT = self.carve(o, NFC * 1024 // 2, BF16, "p (f t) -> p f t", f=NFC)
        o += NFC * 1024 // 2
        self.HTr = [[P.res(f"ht{f}_{t}") for t in range(2)] for f in range(NFC)]
        self.tmpf = []
        for i in range(8):
            self.tmpf.append(Tile(self.carve(o, 512, F32), P.res(f"tmpf{i}")))
            o += 512
        assert o == 23552
        o = 23552 + 2 * 1408
        for i in range(8, 12):
            self.tmpf.append(Tile(self.carve(o, 512, F32), P.res(f"tmpf{i}")))
            o += 512
        assert o <= self.ARENA_F32, o

    def mem_setup(self):
        P = self.P
        MT = self.HT[:, 0:4, :].rearrange("p a b -> p (a b)")[:, 0:NKC * NMEM].rearrange("p (k m) -> p k m", k=NKC)
        mtr = P.res("mt")
        for kc in range(NKC):
            P.dma('pool', (lambda kc: lambda e: e.dma_start(out=MT[:, kc, :], in_=self.memT[kc * 128:(kc + 1) * 128, :]))(kc),
                  mtr, writes=[mtr])
        for c in range(2):
            w = self.w_get('A')
            b = self.bank()
            for kc in range(NKC):
                self.mm(b.ap[:, 0:NMEM], w.ap[:, kc, :], MT[:, kc, :], kc == 0, kc == NKC - 1, [w.res, mtr], [b.res])
            self.copy('act', self.MK.ap[:, c, :], b.ap[:, 0:NMEM], [b.res], [self.MK.res])
        for c in range(2):
            w = self.w_get('A')
            b = self.bank()
            for mt in range(2):
                for kc in range(NKC):
                    self.mm(b.ap[:, mt * 128:(mt + 1) * 128], MT[:, kc, mt * 128:(mt + 1) * 128], w.ap[:, kc, :],
                            kc == 0, kc == NKC - 1, [w.res, mtr], [b.res])
            for mt in range(2):
                self.copy('act', self.MV.ap[:, mt, c, :], b.ap[:, mt * 128:(mt + 1) * 128], [b.res], [self.MV.res])

    def ffn(self, L, which):
        XF, XB, HT, ps = self.XF, self.XB, self.HT, self.ps
        lncol = (L * 3 + (0 if which == 1 else 2)) * 8
        it = 0
        deferred = []
        tf = self.tmpf
        for half in range(2):
            t0 = half * 1024
            for fc in range(NFC):
                if deferred and fc >= 1:
                    dt0, drstd, dmr, ddc = deferred.pop(0)
                    self.ln_apply(dt0, 512, drstd, dmr, lncol, tf[10:12], True, dcs=[ddc])
                wg = self.w_get('A')
                wu = self.w_get('A')
                for tt in range(2):
                    ts0 = t0 + tt * 512
                    tok = slice(ts0, ts0 + 512)
                    bg = ps[(it % 2) * 2]
                    bu = ps[(it % 2) * 2 + 1]
                    for kc in range(NKC):
                        self.mm(bg.ap[:], wg.ap[:, kc, :], XB[:, kc, tok], kc == 0, kc == NKC - 1,
                                [wg.res] + self.xr('b', kc, ts0, 512), [bg.res])
                    for kc in range(NKC):
                        self.mm(bu.ap[:], wu.ap[:, kc, :], XB[:, kc, tok], kc == 0, kc == NKC - 1,
                                [wu.res] + self.xr('b', kc, ts0, 512), [bu.res])
                    st = self.tmpf[it % 2]
                    self.act(st.ap, bg.ap[:], AF.Silu, [bg.res], [st.res])
                    self.tt('dve', HT[:, fc, tt * 512:(tt + 1) * 512], st.ap, bu.ap[:], ALU.mult,
                            [st.res, bu.res], [self.HTr[fc][tt]])
                    it += 1
            sbk = [(ps[2], ps[3]), (ps[0], ps[1])]
            for dc in range(NKC):
                wd = self.w_get('D')
                for tt in range(2):
                    ts0 = t0 + tt * 512
                    tok = slice(ts0, ts0 + 512)
                    by = ps[4 + (it % 2)]
                    for fc in range(NFC):
                        self.mm(by.ap[:], wd.ap[:, fc, :], HT[:, fc, tt * 512:(tt + 1) * 512], fc == 0, fc == NFC - 1,
                                [wd.res, self.HTr[fc][tt]], [by.res])
                    s1, s2 = sbk[tt]
                    self.resid_stats(dc, ts0, 512, by, 2.0 * ALPHA, s1, s2, self.tmpf[2 + it % 2])
                    it += 1
            for tt in range(2):
                s1, s2 = sbk[tt]
                rstd, mr = (tf[6], tf[7]) if tt == 0 else (tf[8], tf[9])
                self.ln_stats(512, s1, s2, 4.0 * LN_EPS, tf[4], tf[5], rstd, mr)
                if half == 0:
                    for dc in range(NKC):
                        deferred.append((t0 + tt * 512, rstd, mr, dc))
                else:
                    self.ln_apply(t0 + tt * 512, 512, rstd, mr, lncol, tf[10:12], True)

    def resid_stats(self, dc, t0, T, by, xscale, s1, s2, sq):
        XF = self.XF
        tok = slice(t0, t0 + T)
        xr = self.xr('f', dc, t0, T)
        self.stt(XF[:, dc, tok], XF[:, dc, tok], xscale, by.ap[:, 0:T], ALU.mult, ALU.add, xr + [by.res], xr)
        self.act(sq.ap[:, 0:T], XF[:, dc, tok], AF.Square, xr, [sq.res])
        self.mm(s1.ap[:, 0:T], self.ones_f.ap[:], XF[:, dc, tok], dc == 0, dc == NKC - 1,
                [self.ones_f.res] + xr, [s1.res])
        self.mm(s2.ap[:, 0:T], self.ones_f.ap[:], sq.ap[:, 0:T], dc == 0, dc == NKC - 1,
                [self.ones_f.res, sq.res], [s2.res])

    def ln_finish(self, t0, T, s1, s2, eps, lncol, tmps, write_xb=True):
        self.ln_stats(T, s1, s2, eps, tmps[4], tmps[5], tmps[6], tmps[7])
        self.ln_apply(t0, T, tmps[6], tmps[7], lncol, tmps[0:2], write_xb)

    def ln_stats(self, T, s1, s2, eps, mean, msq, rstd, mr):
        w = slice(0, T)
        self.act(mean.ap[:, w], s1.ap[:, w], AF.Copy, [s1.res], [mean.res], scale=1.0 / D)
        self.tt('dve', msq.ap[:, w], mean.ap[:, w], mean.ap[:, w], ALU.mult, [mean.res], [msq.res])
        self.stt(msq.ap[:, w], s2.ap[:, w], 1.0 / D, msq.ap[:, w], ALU.mult, ALU.subtract, [s2.res, msq.res], [msq.res])
        self.rsqrt(rstd.ap[:, w], msq.ap[:, w], eps, [msq.res], [rstd.res])
        self.tt('dve', mr.ap[:, w], mean.ap[:, w], rstd.ap[:, w], ALU.mult, [mean.res, rstd.res], [mr.res])

    def ln_apply(self, t0, T, rstd, mr, lncol, us, write_xb=True, dcs=range(NKC)):
        XF, XB = self.XF, self.XB
        tok = slice(t0, t0 + T)
        w = slice(0, T)
        for dc in dcs:
            xr = self.xr('f', dc, t0, T)
            u = us[dc % 2]
            self.tt('dve', u.ap[:, w], XF[:, dc, tok], rstd.ap[:, w], ALU.mult, xr + [rstd.res], [u.res])
            self.tt('dve', u.ap[:, w], u.ap[:, w], mr.ap[:, w], ALU.subtract, [u.res, mr.res], [u.res])
            g = self.LNG.ap[:, lncol + dc:lncol + dc + 1]
            b = self.LNB.ap[:, lncol + dc:lncol + dc + 1]
            self.act(XF[:, dc, tok], u.ap[:, w], AF.Identity, [u.res, self.LNG.res, self.LNB.res], xr, scale=g, bias=b)
            if write_xb:
                self.act(XB[:, dc, tok], u.ap[:, w], AF.Identity, [u.res, self.LNG.res, self.LNB.res],
                         self.xr('b', dc, t0, T), scale=g, bias=b)

    def mem_attn(self, MQ, mq_res, HD, hd_res, t0q, T, t0h, E):
        for h in range(4):
            cq, hp = h // 2, h % 2
            pr = slice(hp * 64, (hp + 1) * 64)
            for mt in range(2):
                b = self.bank()
                self.mm(b.ap[:, 0:T], self.MK.ap[pr, cq, mt * 128:(mt + 1) * 128], MQ[pr, cq, t0q:t0q + T], True, True,
                        [self.MK.res, mq_res], [b.res])
                self.act(E[mt].ap[:, 0:T], b.ap[:, 0:T], AF.Exp, [b.res], [E[mt].res], scale=0.125)
            bn = self.bank()
            bd = self.bank()
            for mt in range(2):
                self.mm(bn.ap[pr, 0:T], self.MV.ap[:, mt, cq, hp * 64:(hp + 1) * 64], E[mt].ap[:, 0:T], mt == 0, mt == 1,
                        [self.MV.res, E[mt].res], [bn.res])
            for mt in range(2):
                self.mm(bd.ap[pr, 0:T], self.ones_b.ap[:, 0:64], E[mt].ap[:, 0:T], mt == 0, mt == 1,
                        [self.ones_b.res, E[mt].res], [bd.res])
            rd = E[2]
            self.P.op('dve', (lambda o, i: lambda e: e.reciprocal(out=o, in_=i))(rd.ap[pr, 0:T], bd.ap[pr, 0:T]),
                      reads=[bd.res], writes=[rd.res])
            self.tt('dve', HD[pr, 6 + cq, t0h:t0h + T], bn.ap[pr, 0:T], rd.ap[pr, 0:T], ALU.mult,
                    [bn.res, rd.res], [hd_res])

    def out_proj_ln(self, L, HD, hd_res, t0h, t0, T, tmps):
        lncol = (L * 3 + 1) * 8
        s1, s2 = self.ps[0], self.ps[1]
        self.ring = list(range(2, 8))
        for dc in range(NKC):
            wo = self.w_get('A')
            by = self.bank()
            for cch in range(NKC):
                self.mm(by.ap[:, 0:T], wo.ap[:, cch, :], HD[:, cch, t0h:t0h + T], cch == 0, cch == NKC - 1,
                        [wo.res, hd_res], [by.res])
            self.resid_stats(dc, t0, T, by, ALPHA, s1, s2, tmps[2 + dc % 2])
        self.ring = list(range(8))
        self.ln_finish(t0, T, s1, s2, LN_EPS, lncol, tmps)
    def rwkv_stage(self):
        P = self.P
        XB = self.XB
        T = 256
        self.aoff = NKC * S // 2
        A = self.alloc
        RP = A(64, F32, "rpar")
        OM = A(32, F32, "om")
        LW = A(768, BF16, "lw")
        GW = A(768, BF16, "gw")
        SM = A(T, F32, "scanmask")
        MLT = A(64, F32, "mlt")
        MLE = A(64, F32, "mle")
        MGT = A(64, F32, "mgt")
        IDB = A(64, F32, "idb")
        CAR = A(20, F32, "carry")
        HF = A(6 * 64, F32, "hf", "p (c i) -> p c i", c=6)
        HB = A(6 * 64, BF16, "hb", "p (c i) -> p c i", c=6)
        PR = [A(T + 1, F32, f"praw{i}") for i in range(3)]
        PL = PR[0:2]
        TD = A(T, BF16, "td")
        SDG = A(T, BF16, "sdg")
        NT = 14
        tm = [A(T, F32, f"rt{i}") for i in range(NT)]
        tb = [A(T, BF16, f"rtb{i}") for i in range(3)]
        ONH = A(768, BF16, "ONH")
        ONL = A(768, BF16, "ONL")
        At = A(6 * T, BF16, "At", "p (c t) -> p c t", c=6)
        Bt = A(6 * T, BF16, "Bt", "p (c t) -> p c t", c=6)
        Kt = A(6 * T, BF16, "Kt", "p (c t) -> p c t", c=6)
        Rt = A(6 * T, BF16, "Rt", "p (c t) -> p c t", c=6)
        Bh = A(2 * 768, BF16, "Bh", "p (r c) -> p r c", r=2)
        Kh = A(2 * 768, BF16, "Kh", "p (r c) -> p r c", r=2)
        Vt = A(2 * 768, BF16, "Vt", "p (r c) -> p r c", r=2)
        G = A(6 * T, BF16, "G", "p (c t) -> p c t", c=6)
        BON = A(6 * T, BF16, "BON", "p (c t) -> p c t", c=6)
        tx = [A(T, F32, f"rtx{i}") for i in range(6)]
        GC = A(6 * 4, F32, "gC", "p (c k) -> p c k", c=6)
        MX = [[A(6 * 64, BF16, f"X{b}{g}", "p (h t) -> p h t", h=6) for g in range(2)] for b in range(2)]
        MY = [[A(6 * 64, BF16, f"Y{b}{g}", "p (h t) -> p h t", h=6) for g in range(2)] for b in range(2)]
        ZT = [A(6 * 64, BF16, f"ZT{g}", "p (h t) -> p h t", h=6) for g in range(2)]
        AK = [A(6 * 64, BF16, f"AK{g}", "p (h t) -> p h t", h=6) for g in range(2)]
        RB = [A(6 * 64, BF16, f"RB{g}", "p (h t) -> p h t", h=6) for g in range(2)]
        RK = [A(6 * 64, BF16, f"RK{g}", "p (h t) -> p h t", h=6) for g in range(2)]
        W0 = [A(6 * 64, BF16, f"W0{g}", "p (h t) -> p h t", h=6) for g in range(2)]
        WB = [A(6 * 64, BF16, f"WB{g}", "p (h t) -> p h t", h=6) for g in range(2)]
        UB = [A(6 * 64, BF16, f"UB{g}", "p (h t) -> p h t", h=6) for g in range(2)]
        OT = [A(6 * 64, F32, f"OT{g}", "p (h t) -> p h t", h=6) for g in range(2)]
        OO = A(768, F32, "OO", "p (h t) -> p h t", h=12)
        ON = A(768, F32, "ON", "p (h t) -> p h t", h=12)
        SQ = ON
        ST = [A(12, F32, f"gst{i}") for i in range(4)]
        HD = A(NKC * T, BF16, "HD", "p (c t) -> p c t", c=NKC)
        MQ = A(2 * T, BF16, "MQ", "p (c t) -> p c t", c=2)
        E = [TD, SDG, tm[13]]
        lt = tm[0:8]

        self.dma_in('sp', RP.ap, self.rpar_d, RP.res)
        self.dma_in('pool', LW.ap, self.lw_d, LW.res)
        self.dma_in('pool', GW.ap, self.gw_d, GW.res)
        self.ts('dve', OM.ap[:, 0:20], RP.ap[:, PC_MU:PC_MU + 20], -1.0, 1.0, ALU.mult, ALU.add, [RP.res], [OM.res])
        self.ts('dve', OM.ap[:, 20:26], RP.ap[:, PC_KA:PC_KA + 6], -1.0, 1.0, ALU.mult, ALU.add, [RP.res], [OM.res])
        self.memset('dve', SM.ap, 1.0, SM.res)
        self.memset('dve', SM.ap[:, 0:T:64], 0.0, SM.res)
        self.memset('dve', CAR.ap, 0.0, CAR.res)
        self.memset('dve', HF.ap, 0.0, HF.res)
        self.memset('dve', HB.ap, 0.0, HB.res)
        ones = self.ones_f

        def asel(dst, pattern, cmul, op, base=0):
            for hp in range(2):
                pr = slice(hp * 64, (hp + 1) * 64)
                P.op('pool', (lambda pr: lambda e: e.affine_select(
                    out=dst.ap[pr, :], in_=ones.ap[pr, 0:64], pattern=pattern, compare_op=op, fill=0.0,
                    base=base, channel_multiplier=cmul))(pr), reads=[ones.res], writes=[dst.res])
        asel(MLT, [[1, 64]], -1, ALU.is_gt)
        asel(MLE, [[1, 64]], -1, ALU.is_ge)
        asel(MGT, [[-1, 64]], 1, ALU.is_gt)
        asel(IDB, [[-1, 64]], 1, ALU.is_equal)

        def bc6(m):
            return m.ap.unsqueeze(1).to_broadcast([128, 6, 64])
        cut(1)

        rpc = lambda c: RP.ap[:, c:c + 1]

        for blk in range(self.rw_blocks):
            t0 = blk * T
            tokb = slice(t0, t0 + T)
            wt = {}

            def inproj(cc, dst_tile, shifted):
                w = self.w_get('A')
                b = self.bank()
                for kc in range(NKC):
                    self.mm(b.ap[:, 0:T], w.ap[:, kc, :], XB[:, kc, tokb], kc == 0, kc == NKC - 1,
                            [w.res] + self.xr('b', kc, t0, T), [b.res])
                if shifted:
                    self.copy('dve', dst_tile.ap[:, 0:1], CAR.ap[:, cc:cc + 1], [CAR.res], [dst_tile.res])
                    self.copy('act', dst_tile.ap[:, 1:T + 1], b.ap[:, 0:T], [b.res], [dst_tile.res])
                    self.copy('act', CAR.ap[:, cc:cc + 1], dst_tile.ap[:, T:T + 1], [dst_tile.res], [CAR.res])

            def shift(praw, cc, out):
                self.asc(out.ap, praw.ap[:, 0:T], rpc(PC_MU + cc), None, [praw.res, RP.res], [out.res])
                self.stt(out.ap, praw.ap[:, 1:T + 1], OM.ap[:, cc:cc + 1], out.ap, ALU.mult, ALU.add,
                         [praw.res, OM.res, out.res], [out.res])

            inproj(18, PL[0], True)
            inproj(19, PL[1], True)
            zl0, zl1 = tm[0], tm[1]
            shift(PL[0], 18, zl0)
            shift(PL[1], 19, zl1)
            self.act(TD.ap[0:64, :], zl0.ap[0:64, :], AF.Tanh, [zl0.res], [TD.res])
            self.copy('dve', TD.ap[64:128, :], zl0.ap[64:128, :], [zl0.res], [TD.res])
            self.act(SDG.ap, zl1.ap, AF.Sigmoid, [zl1.res], [SDG.res])
            cut(2)

            for ch in range(6):
                cs = slice(ch * 128, (ch + 1) * 128)
                inproj(ch, PR[0], True)
                inproj(6 + ch, PR[1], True)
                inproj(12 + ch, PR[2], True)
                zr, zk, zv = tm[2], tm[3], tm[4]
                shift(PR[0], ch, zr)
                shift(PR[1], 6 + ch, zk)
                shift(PR[2], 12 + ch, zv)
                bw, ba, bg_ = self.bank(), self.bank(), self.bank()
                self.mm(bw.ap[:, 0:T], LW.ap[0:64, cs], TD.ap[0:64, :], True, True, [LW.res, TD.res], [bw.res])
                self.mm(ba.ap[:, 0:T], LW.ap[64:128, cs], TD.ap[64:128, :], True, True, [LW.res, TD.res], [ba.res])
                self.mm(bg_.ap[:, 0:T], GW.ap[:, cs], SDG.ap, True, True, [GW.res, SDG.res], [bg_.res])
                sg, a = tm[5], tm[6]
                self.act(sg.ap, bw.ap[:, 0:T], AF.Sigmoid, [bw.res, RP.res], [sg.res], bias=rpc(PC_W0 + ch), scale=1.0)
                self.act(a.ap, ba.ap[:, 0:T], AF.Sigmoid, [ba.res, RP.res], [a.res], bias=rpc(PC_A0 + ch), scale=1.0)
                self.copy('act', G.ap[:, ch, :], bg_.ap[:, 0:T], [bg_.res], [G.res])
                cut(31)
                kk, t1 = tm[7], tm[8]
                t1b, t1c, e2, e3, e4, t2b = tx
                self.asc(kk.ap, zk.ap, rpc(PC_KK + ch), None, [zk.res, RP.res], [kk.res])
                self.act(t1.ap, kk.ap, AF.Square, [kk.res], [t1.res])
                bs = self.bank()
                self.mm(bs.ap[:, 0:T], self.BDF.ap, t1.ap, True, True, [self.BDF.res, t1.res], [bs.res])
                self.ts('dve', t1.ap, bs.ap[:, 0:T], 1e-24, None, ALU.max, None, [bs.res], [t1.res])
                self.rsqrt(t1.ap, t1.ap, 0.0, [t1.res], [t1.res])
                self.tt('dve', kk.ap, kk.ap, t1.ap, ALU.mult, [kk.res, t1.res], [kk.res])
                cut(32)
                k, bb = tm[9], tm[10]
                self.asc(t1b.ap, a.ap, rpc(PC_KA + ch), OM.ap[:, 20 + ch:21 + ch], [a.res, RP.res, OM.res], [t1b.res])
                self.tt('dve', k.ap, zk.ap, t1b.ap, ALU.mult, [zk.res, t1b.res], [k.res])
                self.tt('dve', bb.ap, kk.ap, a.ap, ALU.mult, [kk.res, a.res], [bb.res])
                Lp, e, t2 = tm[11], tm[12], tm[13]
                P.op('dve', lambda e_: e_.tensor_tensor_scan(out=Lp.ap, data0=SM.ap, data1=sg.ap, initial=0.0,
                                                             op0=ALU.mult, op1=ALU.add),
                     reads=[SM.res, sg.res], writes=[Lp.res])
                Lp4 = Lp.ap.rearrange("p (c t) -> p c t", c=4)
                cut(33)
                self.act(e.ap, Lp.ap, AF.Exp, [Lp.res], [e.res], scale=DECAY_C)
                self.copy('dve', GC.ap[:, ch, :], e.ap[:, 63:T:64], [e.res], [GC.res])
                self.tt('dve', Rt.ap[:, ch, :], zr.ap, e.ap, ALU.mult, [zr.res, e.res], [Rt.res])
                self.act(e2.ap, Lp.ap, AF.Exp, [Lp.res], [e2.res], scale=-DECAY_C)
                self.tt('dve', Bt.ap[:, ch, :], bb.ap, e2.ap, ALU.mult, [bb.res, e2.res], [Bt.res])
                self.tt('dve', Kt.ap[:, ch, :], k.ap, e2.ap, ALU.mult, [k.res, e2.res], [Kt.res])
                self.tt('dve', t2.ap, Lp.ap, sg.ap, ALU.subtract, [Lp.res, sg.res], [t2.res])
                self.act(e3.ap, t2.ap, AF.Exp, [t2.res], [e3.res], scale=DECAY_C)
                self.stt(At.ap[:, ch, :], kk.ap, -1.0, e3.ap, ALU.mult, ALU.mult, [kk.res, e3.res], [At.res])
                t24 = t2b.ap.rearrange("p (c t) -> p c t", c=4)
                self.tt('dve', t24, Lp4[:, :, 63:64].to_broadcast([128, 4, 64]), Lp4, ALU.subtract, [Lp.res], [t2b.res])
                self.act(e4.ap, t2b.ap, AF.Exp, [t2b.res], [e4.res], scale=DECAY_C)
                self.tt('dve', tb[0].ap, bb.ap, e4.ap, ALU.mult, [bb.res, e4.res], [tb[0].res])
                self.tt('dve', tb[1].ap, k.ap, e4.ap, ALU.mult, [k.res, e4.res], [tb[1].res])
                self.copy('act', tb[2].ap, zv.ap, [zv.res], [tb[2].res])
                cut(34)
                for pr_ in range(2):
                    ps_ = slice(pr_ * 128, (pr_ + 1) * 128)
                    bt = self.bank()
                    self.tr(bt.ap[:, 0:128], tb[0].ap[:, ps_], [tb[0].res], [bt.res])
                    self.tr(bt.ap[:, 128:256], tb[1].ap[:, ps_], [tb[1].res], [bt.res])
                    self.tr(bt.ap[:, 256:384], tb[2].ap[:, ps_], [tb[2].res], [bt.res])
                    cut(341)
                    self.copy('act', Bh.ap[:, pr_, cs], bt.ap[:, 0:128], [bt.res], [Bh.res])
                    cut(342)
                    self.copy('dve', Kh.ap[:, pr_, cs], bt.ap[:, 128:256], [bt.res], [Kh.res])
                    cut(343)
                    self.copy('act', Vt.ap[:, pr_, cs], bt.ap[:, 256:384], [bt.res], [Vt.res])
                cut(35)
                self.tt('dve', t1c.ap, zr.ap, k.ap, ALU.mult, [zr.res, k.res], [t1c.res])
                self.asc(t1c.ap, t1c.ap, rpc(PC_RK + ch), None, [t1c.res, RP.res], [t1c.res])
                bs2 = self.bank()
                self.mm(bs2.ap[:, 0:T], self.BDF.ap, t1c.ap, True, True, [self.BDF.res, t1c.res], [bs2.res])
                self.tt('dve', BON.ap[:, ch, :], bs2.ap[:, 0:T], zv.ap, ALU.mult, [bs2.res, zv.res], [BON.res])
                cut(3)

            for c2 in range(2):
                w = self.w_get('A')
                b = self.bank()
                for kc in range(NKC):
                    self.mm(b.ap[:, 0:T], w.ap[:, kc, :], XB[:, kc, tokb], kc == 0, kc == NKC - 1,
                            [w.res] + self.xr('b', kc, t0, T), [b.res])
                self.copy('act', MQ.ap[:, c2, :], b.ap[:, 0:T], [b.res], [MQ.res])

            for pr_ in range(2):
                def hidx(hg, hh):
                    h = 2 * hh + hg
                    return h, hh, slice(hg * 64, hg * 64 + 64)

                def tsl(c2):
                    o = pr_ * 128 + c2 * 64
                    return slice(o, o + 64)

                def grp(dst, lhs, rhs, mask, extra=None):
                    for hg in range(2):
                        b = self.bank()
                        for c2 in range(2):
                            for hh in range(6):
                                h, ch, hp = hidx(hg, hh)
                                self.mm(b.ap[c2 * 64:(c2 + 1) * 64, hh * 64:(hh + 1) * 64],
                                        lhs.ap[hp, ch, tsl(c2)], rhs.ap[hp, ch, tsl(c2)], True, True,
                                        [lhs.res, rhs.res], [b.res])
                        bv = b.ap[:, 0:384].rearrange("p (h t) -> p h t", h=6)
                        self.tt('dve', dst[hg].ap, bv, bc6(mask), ALU.mult, [b.res, mask.res], [dst[hg].res])

                grp(MX[0], Bt, At, MLT)
                grp(MY[0], At, Bt, MGT)
                grp(AK, Kt, At, MLT)
                grp(RB, Bt, Rt, MLE)
                grp(RK, Kt, Rt, MLE)
                cut(4)
                for hg in range(2):
                    self.tt('dve', ZT[hg].ap, MX[0][hg].ap, bc6(IDB), ALU.add, [MX[0][hg].res, IDB.res], [ZT[hg].res])

                def grp2(dst, lhs, rhs, hg, accum_into=None, evac='act'):
                    for c2 in range(2):
                        b = self.bank()
                        pc = slice(c2 * 64, (c2 + 1) * 64)
                        for hh in range(6):
                            self.mm(b.ap[pc, hh * 64:(hh + 1) * 64], lhs.ap[pc, hh, :], rhs.ap[pc, hh, :], True, True,
                                    [lhs.res, rhs.res], [b.res])
                        bv = b.ap[pc, 0:384].rearrange("p (h t) -> p h t", h=6)
                        if accum_into is not None:
                            self.tt('dve', accum_into.ap[pc], bv, accum_into.ap[pc], ALU.add, [b.res, accum_into.res],
                                    [accum_into.res])
                        else:
                            self.copy('act' if (evac == 'act' or c2 == 0) else 'dve', dst.ap[pc], bv, [b.res], [dst.res])

                cur = 0
                for kstep in range(6):
                    for hg in range(2):
                        Xc, Yc = MX[cur][hg], MY[cur][hg]
                        if kstep >= 1:
                            grp2(None, Yc, ZT[hg], hg, accum_into=ZT[hg])
                        if kstep <= 3:
                            grp2(MX[1 - cur][hg], Yc, Xc, hg, evac='act')
                        if kstep <= 4:
                            grp2(MY[1 - cur][hg], Xc, Yc, hg, evac='dve')
                    cur = 1 - cur
                for hg in range(2):
                    for c2 in range(2):
                        b = self.bank()
                        pc = slice(c2 * 64, (c2 + 1) * 64)
                        for hh in range(6):
                            h = 2 * hh + hg
                            self.mm(b.ap[pc, hh * 64:(hh + 1) * 64], AK[hg].ap[pc, hh, :], Vt.ap[pc, pr_, h * 64:(h + 1) * 64],
                                    True, True, [AK[hg].res, Vt.res], [b.res])
                        self.copy('act' if c2 == 0 else 'dve', W0[hg].ap[pc],
                                  b.ap[pc, 0:384].rearrange("p (h t) -> p h t", h=6), [b.res], [W0[hg].res])

                cut(5)
                ps = self.ps
                for c2 in range(2):
                    pc = slice(c2 * 64, (c2 + 1) * 64)
                    cglob = pr_ * 2 + c2
                    for hg in range(2):
                        b = ps[hg]
                        for hh in range(6):
                            h, ch, hp = hidx(hg, hh)
                            self.mm(b.ap[pc, hh * 64:(hh + 1) * 64], At.ap[hp, ch, tsl(c2)], HB.ap[hp, ch, :], True, True,
                                    [At.res, HB.res], [b.res])
                        self.tt('dve', WB[hg].ap[pc], b.ap[pc, 0:384].rearrange("p (h t) -> p h t", h=6), W0[hg].ap[pc],
                                ALU.add, [b.res, W0[hg].res], [WB[hg].res])
                    for hg in range(2):
                        b = ps[3 + hg]
                        for hh in range(6):
                            h, ch, hp = hidx(hg, hh)
                            self.mm(b.ap[pc, hh * 64:(hh + 1) * 64], Rt.ap[hp, ch, tsl(c2)], HB.ap[hp, ch, :], True, True,
                                    [Rt.res, HB.res], [b.res])
                        self.copy('act', OT[hg].ap[pc], b.ap[pc, 0:384].rearrange("p (h t) -> p h t", h=6), [b.res], [OT[hg].res])
                    for hg in range(2):
                        b = ps[hg]
                        for hh in range(6):
                            self.mm(b.ap[pc, hh * 64:(hh + 1) * 64], ZT[hg].ap[pc, hh, :], WB[hg].ap[pc, hh, :], True, True,
                                    [ZT[hg].res, WB[hg].res], [b.res])
                        self.copy('act', UB[hg].ap[pc], b.ap[pc, 0:384].rearrange("p (h t) -> p h t", h=6), [b.res], [UB[hg].res])
                    bH = ps[2]
                    for hg in range(2):
                        for hh in range(6):
                            h, ch, hp = hidx(hg, hh)
                            hs = slice(h * 64, (h + 1) * 64)
                            self.mm(bH.ap[hp, ch * 64:(ch + 1) * 64], Bh.ap[pc, pr_, hs], UB[hg].ap[pc, hh, :], True, False,
                                    [Bh.res, UB[hg].res], [bH.res])
                            self.mm(bH.ap[hp, ch * 64:(ch + 1) * 64], Kh.ap[pc, pr_, hs], Vt.ap[pc, pr_, hs], False, True,
                                    [Kh.res, Vt.res], [bH.res])
                    for hg in range(2):
                        b = ps[5 + hg]
                        for hh in range(6):
                            h = 2 * hh + hg
                            hs = slice(h * 64, (h + 1) * 64)
                            self.mm(b.ap[pc, hh * 64:(hh + 1) * 64], RB[hg].ap[pc, hh, :], UB[hg].ap[pc, hh, :], True, False,
                                    [RB[hg].res, UB[hg].res], [b.res])
                            self.mm(b.ap[pc, hh * 64:(hh + 1) * 64], RK[hg].ap[pc, hh, :], Vt.ap[pc, pr_, hs], False, True,
                                    [RK[hg].res, Vt.res], [b.res])
                        self.tt('dve', OO.ap.rearrange("p (c g) t -> p c g t", g=2)[pc, :, hg, :],
                                b.ap[pc, 0:384].rearrange("p (h t) -> p h t", h=6),
                                OT[hg].ap[pc], ALU.add, [b.res, OT[hg].res], [OO.res])
                    gcb = GC.ap[:, :, cglob:cglob + 1].to_broadcast([128, 6, 64])
                    self.tt('dve', HF.ap, HF.ap, gcb, ALU.mult, [HF.res, GC.res], [HF.res])
                    self.tt('dve', HF.ap, HF.ap, bH.ap[:, 0:384].rearrange("p (c i) -> p c i", c=6), ALU.add,
                            [HF.res, bH.res], [HF.res])
                    self.copy('act', HB.ap, HF.ap, [HF.res], [HB.res])

                cut(6)
                s1, s2, mean, rstd = ST
                P.op('dve', lambda e_: e_.tensor_reduce(out=s1.ap, in_=OO.ap, axis=AX.X, op=ALU.add),
                     reads=[OO.res], writes=[s1.res])
                self.act(SQ.ap, OO.ap, AF.Square, [OO.res], [SQ.res])
                P.op('dve', lambda e_: e_.tensor_reduce(out=s2.ap, in_=SQ.ap, axis=AX.X, op=ALU.add),
                     reads=[SQ.res], writes=[s2.res])
                self.ts('dve', mean.ap, s1.ap, 1.0 / 64, None, ALU.mult, None, [s1.res], [mean.res])
                self.tt('dve', s1.ap, mean.ap, mean.ap, ALU.mult, [mean.res], [s1.res])
                self.stt(s2.ap, s2.ap, 1.0 / 64, s1.ap, ALU.mult, ALU.subtract, [s2.res, s1.res], [s2.res])
                self.rsqrt(rstd.ap, s2.ap, LNX_EPS, [s2.res], [rstd.res])
                self.tt('dve', ON.ap, OO.ap, mean.ap.unsqueeze(2).to_broadcast([128, 12, 64]), ALU.subtract,
                        [OO.res, mean.res], [ON.res])
                self.tt('dve', ON.ap, ON.ap, rstd.ap.unsqueeze(2).to_broadcast([128, 12, 64]), ALU.mult,
                        [ON.res, rstd.res], [ON.res])
                ONf = ON.ap.rearrange("p h t -> p (h t)")
                self.copy('act', ONH.ap, ONf, [ON.res], [ONH.res])
                self.tt('dve', ONL.ap, ONf, ONH.ap, ALU.subtract, [ON.res, ONH.res], [ONL.res])
                for half in range(2):
                    bt = self.bank()
                    for j in range(3):
                        ch = half * 3 + j
                        self.tr(bt.ap[:, j * 128:(j + 1) * 128], ONH.ap[:, ch * 128:(ch + 1) * 128], [ONH.res], [bt.res],
                                True, False)
                        self.tr(bt.ap[:, j * 128:(j + 1) * 128], ONL.ap[:, ch * 128:(ch + 1) * 128], [ONL.res], [bt.res],
                                False, True)
                    for j in range(3):
                        ch = half * 3 + j
                        y = tm[j]
                        bsl = slice(pr_ * 128, (pr_ + 1) * 128)
                        self.act(y.ap[:, 0:128], bt.ap[:, j * 128:(j + 1) * 128], AF.Identity, [bt.res, RP.res], [y.res],
                                 scale=rpc(PC_LG + ch), bias=rpc(PC_LB + ch))
                        self.tt('dve', y.ap[:, 0:128], y.ap[:, 0:128], BON.ap[:, ch, bsl], ALU.add, [y.res, BON.res], [y.res])
                        self.tt('dve', HD.ap[:, ch, bsl], y.ap[:, 0:128], G.ap[:, ch, bsl], ALU.mult, [y.res, G.res], [HD.res])

            cut(7)
            self.mem_attn(MQ.ap, MQ.res, HD.ap, HD.res, 0, T, 0, E)
            self.dbg(f"hd{blk}", HD.ap, [128, NKC, T], [HD.res], BF16)
            self.out_proj_ln(0, HD.ap, HD.res, 0, t0, T, lt)
    def sb_stage(self):
        P = self.P
        XB = self.XB
        self.aoff = NKC * S // 2
        A = self.alloc
        QT = A(6 * S, BF16, "QT", "p (c t) -> p c t", c=6)
        KT = A(6 * S, BF16, "KT", "p (c t) -> p c t", c=6)
        VT = A(16 * 768, BF16, "VTs", "p (s c) -> p s c", s=16)
        MQ = A(2 * S, BF16, "MQs", "p (c t) -> p c t", c=2)
        QTr = [[P.res(f"qt{c}_{t}") for t in range(4)] for c in range(6)]
        KTr = [P.res(f"kt{c}") for c in range(6)]

        for cc in range(20):
            w = self.w_get('A')
            if 12 <= cc < 18:
                for g4 in range(4):
                    b = self.bank()
                    for j in range(4):
                        st = g4 * 4 + j
                        for kc in range(NKC):
                            self.mm(b.ap[:, j * 128:(j + 1) * 128], XB[:, kc, st * 128:(st + 1) * 128], w.ap[:, kc, :],
                                    kc == 0, kc == NKC - 1, [w.res] + self.xr('b', kc, st * 128, 128), [b.res])
                    dst = VT.ap[:, g4 * 4:(g4 + 1) * 4, (cc - 12) * 128:(cc - 11) * 128]
                    self.copy('act' if g4 % 2 == 0 else 'dve', dst, b.ap[:].rearrange("p (j c) -> p j c", j=4), [b.res], [VT.res])
                continue
            for tt in range(4):
                tok = slice(tt * 512, (tt + 1) * 512)
                b = self.bank()
                for kc in range(NKC):
                    self.mm(b.ap[:], w.ap[:, kc, :], XB[:, kc, tok], kc == 0, kc == NKC - 1,
                            [w.res] + self.xr('b', kc, tt * 512, 512), [b.res])
                eng = 'act' if tt % 2 == 0 else 'dve'
                if cc < 6:
                    self.copy(eng, QT.ap[:, cc, tok], b.ap[:], [b.res], [QTr[cc][tt]])
                elif cc < 12:
                    self.copy(eng, KT.ap[:, cc - 6, tok], b.ap[:], [b.res], [KTr[cc - 6]])
                else:
                    self.copy(eng, MQ.ap[:, cc - 18, tok], b.ap[:], [b.res], [MQ.res])
        P.barrier()

        save = self.aoff
        self.aoff = 0
        TRI = A(128, BF16, "tri")
        TRC = A(128, BF16, "trc")
        MSK = [A(512, BF16, f"dmask{o}") for o in range(4)]
        EX = [A(512, F32, f"ex{i}") for i in range(2)]
        SP = [A(512, F32, f"sp{i}") for i in range(3)]
        SPH = [A(512, BF16, f"sph{i}") for i in range(5)]
        SPL = [A(512, BF16, f"spl{i}") for i in range(5)]
        ARG = [A(512, F32, f"arg{i}") for i in range(2)]
        ATT = [A(512, BF16, f"att{i}") for i in range(3)]
        assert self.aoff <= NKC * S // 2, self.aoff
        ones_b = self.ones_b
        P.op('pool', lambda e: e.affine_select(out=TRI.ap, in_=ones_b.ap[:], pattern=[[-1, 128]], compare_op=ALU.is_gt,
                                               fill=0.0, base=0, channel_multiplier=1), reads=[ones_b.res], writes=[TRI.res])
        P.op('pool', lambda e: e.affine_select(out=TRC.ap, in_=ones_b.ap[:], pattern=[[1, 128]], compare_op=ALU.is_ge,
                                               fill=0.0, base=0, channel_multiplier=-1), reads=[ones_b.res], writes=[TRC.res])
        self.memset('dve', EX[0].ap, 1.0, EX[0].res)
        for o in range(4):
            P.op('pool', (lambda o: lambda e: e.affine_select(out=MSK[o].ap, in_=EX[0].ap, pattern=[[1, 512]],
                                                              compare_op=ALU.is_gt, fill=0.0, base=-128 * o,
                                                              channel_multiplier=-1))(o),
                 reads=[EX[0].res], writes=[MSK[o].res])

        ps = self.ps
        pairs = []
        it = 0
        for hpair in range(6):
            for qt in range(4):
                nkt = 4 * (qt + 1)
                for idx, kt in enumerate(range(nkt - 1, -1, -1)):
                    for j in range(2):
                        pairs.append(dict(h=2 * hpair + j, qt=qt, kt=kt, idx=idx, nkt=nkt, it=it + j, g=len(pairs)))
                it += 2

        def bufs(p):
            g = p['g']
            return dict(zb=ps[2 + g % 3], ex=EX[g % 2], sp=SP[g % 3], sph=SPH[g % 5], spl=SPL[g % 5], arg=ARG[g % 2],
                        att=ATT[g % 3], xs=ps[5 + p['it'] % 2], ob=ps[p['it'] % 2])

        def stage0(p):
            b = bufs(p)
            h, qt, kt = p['h'], p['qt'], p['kt']
            ch, hp = h // 2, slice((h % 2) * 64, (h % 2) * 64 + 64)
            qtok = slice(qt * 512, (qt + 1) * 512)
            diag = kt - 4 * qt
            zb, ex, sp, sph, spl = b['zb'], b['ex'], b['sp'], b['sph'], b['spl']
            self.mm(zb.ap[:], KT.ap[hp, ch, kt * 128:(kt + 1) * 128], QT.ap[hp, ch, qtok], True, True,
                    [KTr[ch], QTr[ch][qt]], [zb.res])
            self.act(ex.ap, zb.ap[:], AF.Exp, [zb.res], [ex.res], scale=0.125)
            if diag >= 0:
                self.tt('pool', ex.ap, ex.ap, MSK[diag].ap, ALU.mult, [ex.res, MSK[diag].res], [ex.res])
            self.act(sp.ap, ex.ap, AF.Ln, [ex.res], [sp.res], bias=self.cst(1.0), scale=1.0)
            self.act(sph.ap, ex.ap, AF.Ln, [ex.res], [sph.res], bias=self.cst(1.0), scale=1.0)
            self.tt('dve', spl.ap, sp.ap, sph.ap, ALU.subtract, [sp.res, sph.res], [spl.res])

        def stage1(p, prev):
            b = bufs(p)
            diag = p['kt'] - 4 * p['qt']
            zb, sp, sph, spl, arg, att, xs = b['zb'], b['sp'], b['sph'], b['spl'], b['arg'], b['att'], b['xs']
            first = p['idx'] == 0
            if not first:
                pb = bufs(prev)
                self.mm(xs.ap[:], TRC.ap, pb['sph'].ap, False, False, [TRC.res, pb['sph'].res], [xs.res], skip=True)
                self.mm(xs.ap[:], TRC.ap, pb['spl'].ap, False, False, [TRC.res, pb['spl'].res], [xs.res], skip=True)
            self.mm(xs.ap[:], TRI.ap, sph.ap, first, False, [TRI.res, sph.res], [xs.res], skip=not first)
            self.mm(xs.ap[:], TRI.ap, spl.ap, False, True, [TRI.res, spl.res], [xs.res], skip=not first)
            self.stt(arg.ap, zb.ap[:], 0.125, sp.ap, ALU.mult, ALU.subtract, [zb.res, sp.res], [arg.res])
            self.tt('dve', arg.ap, arg.ap, xs.ap[:], ALU.subtract, [arg.res, xs.res], [arg.res])

        def stage1b(p):
            b = bufs(p)
            diag = p['kt'] - 4 * p['qt']
            arg, att = b['arg'], b['att']
            self.act(att.ap, arg.ap, AF.Exp, [arg.res], [att.res])
            if diag >= 0:
                self.tt('pool', att.ap, att.ap, MSK[diag].ap, ALU.mult, [att.res, MSK[diag].res], [att.res])

        def stage2(p):
            b = bufs(p)
            h, qt, kt = p['h'], p['qt'], p['kt']
            ch, hp = h // 2, slice((h % 2) * 64, (h % 2) * 64 + 64)
            qtok = slice(qt * 512, (qt + 1) * 512)
            ob, att = b['ob'], b['att']
            self.mm(ob.ap[hp, :], VT.ap[:, kt, h * 64:(h + 1) * 64], att.ap, p['idx'] == 0, p['idx'] == p['nkt'] - 1,
                    [VT.res, att.res], [ob.res])
            if p['idx'] == p['nkt'] - 1:
                self.copy('act', QT.ap[hp, ch, qtok], ob.ap[hp, :], [ob.res], [QTr[ch][qt]])

        N = len(pairs)
        for step in range(N + 5):
            if 0 <= step - 2 < N:
                p = pairs[step - 2]
                stage1(p, pairs[step - 4] if p['idx'] > 0 else None)
            if 0 <= step - 3 < N:
                stage1b(pairs[step - 3])
            if step < N:
                stage0(pairs[step])
            if 0 <= step - 5 < N:
                stage2(pairs[step - 5])
        P.barrier()
        self.aoff = 0
        E = [A(512, BF16, "E0s"), A(512, BF16, "E1s"), A(512, F32, "E2s")]
        lt = [A(512, F32, f"lnts{i}") for i in range(8)]
        assert self.aoff <= NKC * S // 2, self.aoff
        self.aoff = save

        for tt in range(4):
            t0 = tt * 512
            HDv = None
            self.mem_attn_sb(MQ, t0, E)
            self.out_proj_ln_sb(QT, QTr, MQ, t0, lt)

    def mem_attn_sb(self, MQ, t0, E):
        T = 512
        for h in range(4):
            cq, hp = h // 2, h % 2
            pr = slice(hp * 64, (hp + 1) * 64)
            for mt in range(2):
                b = self.bank()
                self.mm(b.ap[:, 0:T], self.MK.ap[pr, cq, mt * 128:(mt + 1) * 128], MQ.ap[pr, cq, t0:t0 + T], True, True,
                        [self.MK.res, MQ.res], [b.res])
                self.act(E[mt].ap[:, 0:T], b.ap[:, 0:T], AF.Exp, [b.res], [E[mt].res], scale=0.125)
            bn = self.bank()
            bd = self.bank()
            for mt in range(2):
                self.mm(bn.ap[pr, 0:T], self.MV.ap[:, mt, cq, hp * 64:(hp + 1) * 64], E[mt].ap[:, 0:T], mt == 0, mt == 1,
                        [self.MV.res, E[mt].res], [bn.res])
            for mt in range(2):
                self.mm(bd.ap[pr, 0:T], self.ones_b.ap[:, 0:64], E[mt].ap[:, 0:T], mt == 0, mt == 1,
                        [self.ones_b.res, E[mt].res], [bd.res])
            rd = E[2]
            self.P.op('dve', (lambda o, i: lambda e: e.reciprocal(out=o, in_=i))(rd.ap[pr, 0:T], bd.ap[pr, 0:T]),
                      reads=[bd.res], writes=[rd.res])
            self.tt('dve', MQ.ap[pr, cq, t0:t0 + T], bn.ap[pr, 0:T], rd.ap[pr, 0:T], ALU.mult,
                    [bn.res, rd.res], [MQ.res])

    def out_proj_ln_sb(self, QT, QTr, MQ, t0, tmps):
        T = 512
        tt = t0 // 512
        lncol = (1 * 3 + 1) * 8
        s1, s2 = self.ps[0], self.ps[1]
        self.ring = list(range(2, 8))
        for dc in range(NKC):
            wo = self.w_get('A')
            by = self.bank()
            for cch in range(NKC):
                if cch < 6:
                    rhs, rr = QT.ap[:, cch, t0:t0 + T], QTr[cch][tt]
                else:
                    rhs, rr = MQ.ap[:, cch - 6, t0:t0 + T], MQ.res
                self.mm(by.ap[:, 0:T], wo.ap[:, cch, :], rhs, cch == 0, cch == NKC - 1, [wo.res, rr], [by.res])
            self.resid_stats(dc, t0, T, by, ALPHA, s1, s2, tmps[2 + dc % 2])
        self.ring = list(range(8))
        self.ln_finish(t0, T, s1, s2, LN_EPS, lncol, tmps, write_xb=False)


def tile_a(w):
    lead = w.shape[:-2]
    n = w.shape[-1] // 128
    w = w.reshape(lead + (NKC, 128, n, 128))
    nd = len(lead)
    w = np.transpose(w, tuple(range(nd)) + (nd + 2, nd + 1, nd + 0, nd + 3))
    return np.ascontiguousarray(w).reshape(lead + (n, 128, NKC * 128))


def tile_d(w):
    lead = w.shape[:-2]
    w = w.reshape(lead + (NFC, 128, NKC, 128))
    nd = len(lead)
    w = np.transpose(w, tuple(range(nd)) + (nd + 2, nd + 1, nd + 0, nd + 3))
    return np.ascontiguousarray(w).reshape(lead + (NKC, 128, NFC * 128))


def cols(v, n):
    return np.asarray(v, dtype=np.float32).reshape(n, 128).T


def prep_shared(inputs):
    f = lambda a: np.asarray(a, dtype=np.float32)
    sh = {}
    for which in (1, 2):
        sh[f"g{which}"] = tile_a(f(inputs[f"ffn{which}_w_gate"]))
        sh[f"u{which}"] = tile_a(f(inputs[f"ffn{which}_w_up"]))
        sh[f"d{which}"] = tile_d(f(inputs[f"ffn{which}_w_down"]))
    sh["lng"] = np.ascontiguousarray(f(inputs["ln_g"]).reshape(6, NKC, 128).transpose(2, 0, 1).reshape(128, 48))
    sh["lnb"] = np.ascontiguousarray(f(inputs["ln_b"]).reshape(6, NKC, 128).transpose(2, 0, 1).reshape(128, 48))
    sh["wout"] = tile_a(f(inputs["w_out"]))
    sh["wmem"] = tile_a(f(inputs["w_mem_kv"]))
    sh["rwin"] = tile_a(f(inputs["rwkv_w_in"])[0])
    sh["sbin"] = tile_a(f(inputs["sb_w_in"])[0])
    rp = np.zeros((128, 64), np.float32)
    rp[:, PC_MU:PC_MU + 20] = cols(f(inputs["rwkv_mu"])[0], 20)
    for off, key in ((PC_W0, "rwkv_w0"), (PC_A0, "rwkv_a0"), (PC_KK, "rwkv_k_k"), (PC_KA, "rwkv_k_a"),
                     (PC_RK, "rwkv_r_k"), (PC_LG, "rwkv_lnx_g"), (PC_LB, "rwkv_lnx_b")):
        rp[:, off:off + 6] = cols(f(inputs[key])[0].reshape(-1), 6)
    sh["rpar"] = rp
    sh["lw"] = np.ascontiguousarray(np.concatenate([f(inputs["rwkv_w_up"])[0], f(inputs["rwkv_a_up"])[0]], axis=0))
    sh["gw"] = np.ascontiguousarray(f(inputs["rwkv_g_up"])[0])
    return sh


_NC_CACHE = {}


def get_nc(stop_after=None, start_at=0, dbg=(), rw_blocks=8):
    key = (stop_after, start_at, tuple(dbg), rw_blocks)
    if key not in _NC_CACHE:
        _NC_CACHE[key] = Builder(stop_after, start_at, dbg, rw_blocks).build()
    return _NC_CACHE[key]


def make_in_maps(inputs, cores=8, x_override=None):
    sh = prep_shared(inputs)
    x = np.asarray(inputs["x"], dtype=np.float32) if x_override is None else x_override
    mem = np.asarray(inputs["mem"], dtype=np.float32)
    in_maps = []
    for b in range(cores):
        m = dict(sh)
        m["xT"] = np.ascontiguousarray(x[b].T)
        m["memT"] = np.ascontiguousarray(mem[b].T)
        in_maps.append(m)
    return in_maps


def run(inputs, cores=8, stop_after=None, start_at=0, dbg=(), trace=False, x_override=None, rw_blocks=8):
    sh = prep_shared(inputs)
    x = np.asarray(inputs["x"], dtype=np.float32) if x_override is None else x_override
    mem = np.asarray(inputs["mem"], dtype=np.float32)
    in_maps = []
    for b in range(cores):
        m = dict(sh)
        m["xT"] = np.ascontiguousarray(x[b].T)
        m["memT"] = np.ascontiguousarray(mem[b].T)
        in_maps.append(m)
    nc = get_nc(stop_after, start_at, dbg, rw_blocks)
    res = run_bass_kernel_spmd(nc, in_maps, core_ids=list(range(cores)), trace=trace)
    out = np.stack([np.ascontiguousarray(r["outT"].T) for r in res.results], axis=0)
    return out, res


def kernel(**inputs):
    out, _ = run(inputs, cores=8)
    return out.astype(np.float32)
```

```python
import os
import numpy as np
from contextlib import ExitStack

import concourse.bass as bass
import concourse.mybir as mybir
from concourse.bass_utils import run_bass_kernel_spmd

F32 = mybir.dt.float32
BF16 = mybir.dt.bfloat16
AF = mybir.ActivationFunctionType
ALU = mybir.AluOpType

D = 1024
S = 2048
DFF = 2816
NKC = 8
NFC = 22
NMEM = 256
ALPHA = 4.0 ** 0.25
LN_EPS = 1e-5
LNX_EPS = 64e-5
DECAY_C = -float(np.exp(-0.5))

ENGS = ['pe', 'act', 'dve', 'pool', 'sp']
SAME_ENGINE_SYNC = os.environ.get("SES", "1") == "1"


class Res:
    __slots__ = ('name', 'last_w', 'readers', 'dreaders', 'sem', 'dma_cnt', 'excl')

    def __init__(self, name):
        self.name = name
        self.excl = False
        self.last_w = None
        self.readers = {}
        self.dreaders = []
        self.sem = None
        self.dma_cnt = 0


class OpRec:
    __slots__ = ('eng', 'fn', 'is_dma', 'deps', 'flagged', 'count', 'dres', 'dval', 'idx')


class Prog:
    def __init__(self, nc, es):
        self.nc = nc
        self.es = es
        self.q = {e: [] for e in ENGS}
        self.nres = 0
        self.all_dma = []
        self.bar = {e: None for e in ENGS}

    def res(self, name=None):
        self.nres += 1
        return Res((name or "r") + str(self.nres))

    def _track(self, op, reads, writes):
        deps = []
        for r in reads:
            if r.last_w is not None:
                deps.append(r.last_w)
        for w in writes:
            if w.last_w is not None:
                deps.append(w.last_w)
            deps.extend(w.readers.values())
            deps.extend(w.dreaders)
        b = self.bar[op.eng]
        if b is not None:
            deps.extend(b)
            self.bar[op.eng] = None
        for r in reads:
            if op.is_dma:
                r.dreaders.append(op)
            else:
                r.readers[op.eng] = op
        for w in writes:
            w.last_w = op
            w.readers = {}
            w.dreaders = []
        best = {}
        out = []
        for d in deps:
            if d is op:
                continue
            if d.is_dma:
                out.append(d)
                continue
            if (not op.is_dma) and d.eng == 'pe' and op.eng == 'pe':
                continue
            if (not op.is_dma) and d.eng == op.eng and not SAME_ENGINE_SYNC:
                continue
            if d.eng not in best or best[d.eng].idx < d.idx:
                best[d.eng] = d
        out.extend(best.values())
        op.deps = out

    def op(self, eng, fn, reads=(), writes=()):
        if any(r.excl for r in reads):
            writes = list(writes) + [r for r in reads if r.excl]
            reads = [r for r in reads if not r.excl]
        o = OpRec()
        o.eng = eng
        o.fn = fn
        o.is_dma = False
        o.flagged = False
        o.count = None
        o.dres = None
        o.dval = None
        o.idx = len(self.q[eng])
        self._track(o, reads, writes)
        self.q[eng].append(o)
        return o

    def dma(self, eng, fn, sres, reads=(), writes=()):
        o = OpRec()
        o.eng = eng
        o.fn = fn
        o.is_dma = True
        o.flagged = False
        o.count = None
        o.dres = sres
        sres.dma_cnt += 16
        o.dval = sres.dma_cnt
        o.idx = len(self.q[eng])
        self._track(o, reads, writes)
        self.q[eng].append(o)
        self.all_dma.append(o)
        return o

    def barrier(self):
        deps = list(self.all_dma)
        self.all_dma = []
        for e in ENGS:
            for o in reversed(self.q[e]):
                if not o.is_dma:
                    deps.append(o)
                    break
        for e in ENGS:
            prev = self.bar[e]
            self.bar[e] = (prev or []) + deps

    def finalize(self, final_waits=()):
        nc = self.nc
        es = self.es
        for e in ENGS:
            for o in self.q[e]:
                for d in o.deps:
                    if not d.is_dma:
                        d.flagged = True
        for o in final_waits:
            if not o.is_dma:
                o.flagged = True
        esem = {}
        for e in ENGS:
            c = 0
            for o in self.q[e]:
                if o.is_dma:
                    if o.dres.sem is None:
                        o.dres.sem = es.enter_context(nc.semaphore("d_" + o.dres.name))
                elif o.flagged:
                    c += 1
                    o.count = c
            esem[e] = es.enter_context(nc.semaphore("e_" + e))

        def tok(d):
            if d.is_dma:
                return d.dres.sem, d.dval
            return esem[d.eng], d.count

        def emit(ename, eng):
            known = {}
            for o in self.q[ename]:
                need = {}
                for d in o.deps:
                    s, v = tok(d)
                    k = id(s)
                    if known.get(k, 0) >= v:
                        continue
                    if k not in need or need[k][1] < v:
                        need[k] = (s, v)
                for k, (s, v) in need.items():
                    eng.wait_ge(s, v)
                    known[k] = v
                ins = o.fn(eng)
                if o.is_dma:
                    ins.then_inc(o.dres.sem, 16)
                elif o.flagged:
                    ins.then_inc(esem[ename], 1)
            if ename == 'sp':
                for d in final_waits:
                    s, v = tok(d)
                    eng.wait_ge(s, v)

        with nc.Block() as block:
            @block.tensor
            def _(eng):
                emit('pe', eng)

            @block.scalar
            def _(eng):
                emit('act', eng)

            @block.vector
            def _(eng):
                emit('dve', eng)

            @block.gpsimd
            def _(eng):
                emit('pool', eng)

            @block.sync
            def _(eng):
                emit('sp', eng)


class CutHere(Exception):
    pass


import os
RW_CUT = int(os.environ.get("RW_CUT", "0"))


def cut(n):
    if RW_CUT == n:
        raise CutHere()


class Tile:
    __slots__ = ('ap', 'res')

    def __init__(self, ap, res):
        if not isinstance(ap, bass.AP):
            ap = ap[:]
        self.ap = ap
        self.res = res


AX = mybir.AxisListType
RW_ORDER = [18, 19] + [c for ch in range(6) for c in (ch, 6 + ch, 12 + ch)] + [20, 21]
PC_MU, PC_W0, PC_A0, PC_KK, PC_KA, PC_RK, PC_LG, PC_LB = 0, 20, 26, 32, 38, 44, 50, 56


class Builder:
    STAGES = ["l0_x1", "l0_x2", "l0_x3", "l1_x1", "l1_x2", "l1_x3"]

    def __init__(self, stop_after=None, start_at=0, dbg=(), rw_blocks=8):
        self.stop_after = stop_after
        self.start_at = start_at
        self.rw_blocks = rw_blocks
        last = self.STAGES.index(stop_after) if stop_after else 5
        self.run_stages = self.STAGES[start_at:last + 1]
        self.dbg_names = set(dbg)
        self.dbg_outs = []
        self.nc = bass.Bass("TRN2", target_bir_lowering=False)
        self.es = ExitStack()
        self.bank_rr = 0
        self.ring = list(range(8))

    def sb(self, name, shape, dt):
        return self.es.enter_context(self.nc.sbuf_tensor(name, shape, dt))

    def dram_in(self, name, shape):
        return self.nc.dram_tensor(name, list(shape), F32, kind="ExternalInput").ap()

    def mm(self, out, lhsT, rhs, start, stop, reads, writes, skip=False):
        return self.P.op('pe', lambda e: e.matmul(out, lhsT=lhsT, rhs=rhs, start=start, stop=stop, skip_group_check=skip),
                         reads=reads, writes=writes)

    def tr(self, out, in_, reads, writes, start=True, stop=True):
        ident = self.IDB16.ap
        return self.P.op('pe', lambda e: e.matmul(out, lhsT=in_, rhs=ident, start=start, stop=stop),
                         reads=list(reads) + [self.IDB16.res], writes=writes)

    def act(self, out, in_, func, reads, writes, scale=None, bias=None):
        kw = {}
        if scale is not None:
            kw['scale'] = scale
        if bias is not None:
            kw['bias'] = bias
        return self.P.op('act', lambda e: e.activation(out=out, in_=in_, func=func, **kw),
                         reads=reads, writes=writes)

    def asc(self, out, in_, scale, bias, reads, writes):
        if bias is None:
            return self.P.op('act', lambda e: e.activation(out=out, in_=in_, func=AF.Copy, scale=scale),
                             reads=reads, writes=writes)
        return self.P.op('act', lambda e: e.activation(out=out, in_=in_, func=AF.Identity, scale=scale, bias=bias),
                         reads=reads, writes=writes)

    def tt(self, eng, out, in0, in1, op, reads, writes):
        return self.P.op(eng, lambda e: e.tensor_tensor(out=out, in0=in0, in1=in1, op=op),
                         reads=reads, writes=writes)

    def ts(self, eng, out, in0, s1, s2, op0, op1, reads, writes):
        if op1 is None:
            return self.P.op(eng, lambda e: e.tensor_scalar(out=out, in0=in0, scalar1=s1, scalar2=None, op0=op0),
                             reads=reads, writes=writes)
        return self.P.op(eng, lambda e: e.tensor_scalar(out=out, in0=in0, scalar1=s1, scalar2=s2, op0=op0, op1=op1),
                         reads=reads, writes=writes)

    def stt(self, out, in0, scalar, in1, op0, op1, reads, writes):
        return self.P.op('dve', lambda e: e.scalar_tensor_tensor(out=out, in0=in0, scalar=scalar, in1=in1,
                                                                 op0=op0, op1=op1),
                         reads=reads, writes=writes)

    def rsqrt(self, out, in_, eps, reads, writes, eng='dve'):
        self.P.op('act', lambda e: e.activation(out=out, in_=in_, func=AF.Sqrt, bias=self.cst(eps), scale=1.0),
                  reads=list(reads) + [self.CONST.res], writes=writes)
        return self.P.op(eng, lambda e: e.reciprocal(out=out, in_=out), reads=writes, writes=writes)

    def cst(self, v, n=128, base=0):
        c = self.const_cols[v]
        return self.CONST.ap[base:base + n, c:c + 1]

    def copy(self, eng, out, in_, reads, writes):
        if eng == 'act':
            return self.P.op('act', lambda e: e.copy(out=out, in_=in_), reads=reads, writes=writes)
        return self.P.op(eng, lambda e: e.tensor_copy(out=out, in_=in_), reads=reads, writes=writes)

    def memset(self, eng, ap, v, res):
        return self.P.op(eng, lambda e: e.memset(ap, v), writes=[res])

    def dma_in(self, eng, dst, src, res):
        return self.P.dma(eng, lambda e: e.dma_start(out=dst, in_=src), res, writes=[res])

    def bank(self):
        ring = self.ring
        b = self.ps[ring[self.bank_rr % len(ring)]]
        self.bank_rr += 1
        return b

    def dbg(self, name, ap, shape, res_list, dt=F32):
        if name not in self.dbg_names:
            return
        t = self.nc.dram_tensor("dbg_" + name, list(shape), dt, kind="ExternalOutput").ap()
        r = self.P.res("dbg_" + name)
        o = self.P.dma('sp', lambda e: e.dma_start(out=t, in_=ap), r, reads=res_list)
        self.dbg_outs.append(o)

    def xr(self, kind, kc, t0, T):
        rr = self.XFr if kind == 'f' else self.XBr
        return [rr[kc][b] for b in range(t0 // 256, (t0 + T + 255) // 256)]

    def w_schedule(self):
        sched = []
        for i in range(4):
            sched.append(('A', self.wmem_d[i]))
        for st in self.run_stages:
            L = int(st[1])
            if st.endswith("x2"):
                sched.extend(self.attn_w_schedule(L))
                continue
            which = 1 if st.endswith("x1") else 2
            g, u, d = self.wd[f"g{which}"], self.wd[f"u{which}"], self.wd[f"d{which}"]
            for half in range(2):
                for fc in range(NFC):
                    sched.append(('A', g[L, fc]))
                    sched.append(('A', u[L, fc]))
                for dc in range(NKC):
                    sched.append(('D', d[L, dc]))
        return sched

    def attn_w_schedule(self, L):
        s = []
        if L == 0:
            for blk in range(self.rw_blocks):
                for cc in RW_ORDER:
                    s.append(('A', self.rwin_d[cc]))
                for dc in range(NKC):
                    s.append(('A', self.wout_d[0, dc]))
        else:
            for cc in range(20):
                s.append(('A', self.sbin_d[cc]))
            for tt in range(4):
                for dc in range(NKC):
                    s.append(('A', self.wout_d[1, dc]))
        return s

    def w_init(self):
        self.NA = 6
        self.ND = 2
        self.wslots = {'A': [], 'D': []}
        for i in range(self.NA):
            t = self.sb(f"wa{i}", [128, NKC, 128], BF16)
            self.wslots['A'].append(Tile(t, self.P.res(f"wa{i}")))
        for i in range(self.ND):
            o = 23552 + i * 1408
            t = self.carve(o, 1408, BF16, "p (f d) -> p f d", f=NFC)
            self.wslots['D'].append(Tile(t, self.P.res(f"wd{i}")))
        self.wsched = self.w_schedule()
        self.w_issued = 0
        self.w_next = 0
        self.w_kcount = {'A': 0, 'D': 0}
        self.w_tile_slot = []
        self.w_ahead = {'A': 3, 'D': 1}

    def _w_issue_one(self):
        kind, src = self.wsched[self.w_issued]
        n = self.w_kcount[kind]
        self.w_kcount[kind] = n + 1
        slots = self.wslots[kind]
        t = slots[n % len(slots)]
        self.w_tile_slot.append(t)
        if kind == 'A':
            dst = t.ap[:].rearrange("p a b -> p (a b)")
            self.P.dma('pool', lambda e: e.dma_start(out=dst, in_=src), t.res, writes=[t.res])
        else:
            dst = t.ap.rearrange("p a b -> p (a b)").rearrange("p (h x) -> p h x", h=2)
            s2 = src.rearrange("p (h x) -> p h x", h=2)
            self.P.dma('pool', lambda e: e.dma_start(out=dst, in_=s2), t.res, writes=[t.res])
        self.w_issued += 1

    def w_get(self, kind):
        i = self.w_next
        assert self.wsched[i][0] == kind, (i, self.wsched[i][0], kind)
        while self.w_issued <= i:
            self._w_issue_one()
        ahead_cnt = {'A': 0, 'D': 0}
        for j in range(i + 1, self.w_issued):
            ahead_cnt[self.wsched[j][0]] += 1
        while self.w_issued < len(self.wsched):
            k = self.wsched[self.w_issued][0]
            if ahead_cnt[k] >= self.w_ahead[k]:
                break
            self._w_issue_one()
            ahead_cnt[k] += 1
        self.w_next += 1
        return self.w_tile_slot[i]

    def build(self):
        nc = self.nc
        es = self.es
        with es:
            self.P = P = Prog(nc, es)
            self.xT = self.dram_in("xT", [D, S])
            self.memT = self.dram_in("memT", [D, NMEM])
            self.outT = nc.dram_tensor("outT", [D, S], F32, kind="ExternalOutput").ap()
            self.wd = {}
            for which in (1, 2):
                self.wd[f"g{which}"] = self.dram_in(f"g{which}", [2, NFC, 128, NKC * 128])
                self.wd[f"u{which}"] = self.dram_in(f"u{which}", [2, NFC, 128, NKC * 128])
                self.wd[f"d{which}"] = self.dram_in(f"d{which}", [2, NKC, 128, NFC * 128])
            self.lng_d = self.dram_in("lng", [128, 48])
            self.lnb_d = self.dram_in("lnb", [128, 48])
            self.wout_d = self.dram_in("wout", [2, NKC, 128, NKC * 128])
            self.wmem_d = self.dram_in("wmem", [4, 128, NKC * 128])
            self.rwin_d = self.dram_in("rwin", [22, 128, NKC * 128])
            self.sbin_d = self.dram_in("sbin", [20, 128, NKC * 128])
            self.rpar_d = self.dram_in("rpar", [128, 64])
            self.lw_d = self.dram_in("lw", [128, 768])
            self.gw_d = self.dram_in("gw", [128, 768])

            self.XF = self.sb("XF", [128, NKC, S], F32)
            self.XFr = [[P.res(f"xf{k}_{t}") for t in range(8)] for k in range(NKC)]
            self.XBr = [[P.res(f"xb{k}_{t}") for t in range(8)] for k in range(NKC)]
            self.LNG = Tile(self.sb("LNG", [128, 48], F32), P.res("lng"))
            self.LNB = Tile(self.sb("LNB", [128, 48], F32), P.res("lnb"))
            self.ones_f = Tile(self.sb("ones_f", [128, 128], F32), P.res("ones_f"))
            self.ones_b = Tile(self.sb("ones_b", [128, 128], BF16), P.res("ones_b"))
            self.IDF = Tile(self.sb("IDF", [128, 128], F32), P.res("idf"))
            self.BDF = Tile(self.sb("BDF", [128, 128], F32), P.res("bdf"))
            self.IDB16 = Tile(self.sb("IDB16", [128, 128], BF16), P.res("idb16"))
            self.MK = Tile(self.sb("MK", [128, 2, NMEM], BF16), P.res("mk"))
            self.MV = Tile(self.sb("MV", [128, 2, 2, 128], BF16), P.res("mv"))
            self.ARENA_F32 = 32600
            self.arena = self.sb("arena", [128, self.ARENA_F32], F32)
            self.w_init()
            self.ps = []
            for i in range(8):
                t = es.enter_context(nc.psum_tensor(f"ps{i}", [128, 512], F32))
                self.ps.append(Tile(t, P.res(f"ps{i}")))
                self.ps[-1].res.excl = True

            self.memset('dve', self.ones_f.ap[:], 1.0, self.ones_f.res)
            self.memset('dve', self.ones_b.ap[:], 1.0, self.ones_b.res)
            self.CONST = Tile(self.sb("CONST", [128, 8], F32), P.res("const"))
            self.const_cols = {}
            for ci, cv in enumerate([4.0 * LN_EPS, LN_EPS, LNX_EPS, 1.0, 0.0]):
                self.const_cols[cv] = ci
                self.memset('dve', self.CONST.ap[:, ci:ci + 1], cv, self.CONST.res)
            P.op('pool', lambda e: e.affine_select(out=self.IDF.ap[:], in_=self.ones_f.ap[:], pattern=[[-1, 128]],
                                                   compare_op=ALU.is_equal, fill=0.0, base=0, channel_multiplier=1),
                 reads=[self.ones_f.res], writes=[self.IDF.res])
            self.copy('dve', self.IDB16.ap, self.IDF.ap, [self.IDF.res], [self.IDB16.res])
            self.memset('dve', self.BDF.ap[:], 0.0, self.BDF.res)
            self.memset('dve', self.BDF.ap[0:64, 0:64], 1.0, self.BDF.res)
            self.memset('dve', self.BDF.ap[64:128, 64:128], 1.0, self.BDF.res)

            self.dma_in('sp', self.LNG.ap[:], self.lng_d, self.LNG.res)
            self.dma_in('sp', self.LNB.ap[:], self.lnb_d, self.LNB.res)
            for kc in range(NKC):
                for t in range(8):
                    r = self.XFr[kc][t]
                    self.dma_in('sp', self.XF[:, kc, t * 256:(t + 1) * 256],
                                self.xT[kc * 128:(kc + 1) * 128, t * 256:(t + 1) * 256], r)

            self.arena_ffn()
            self.mem_setup()
            for kc in range(NKC):
                for t in range(4):
                    sl = slice(t * 512, (t + 1) * 512)
                    self.copy('dve' if (kc + t) % 2 == 0 else 'act', self.XB[:, kc, sl], self.XF[:, kc, sl],
                              reads=self.xr('f', kc, t * 512, 512), writes=self.xr('b', kc, t * 512, 512))

            for st in self.run_stages:
                L = int(st[1])
                if st.endswith("x1"):
                    self.ffn(L, 1)
                elif st.endswith("x3"):
                    self.ffn(L, 2)
                else:
                    P.barrier()
                    if L == 0:
                        try:
                            self.rwkv_stage()
                        except CutHere:
                            pass
                    else:
                        self.sb_stage()
                    P.barrier()
                    self.arena_ffn()
                    if L == 1:
                        for kc in range(NKC):
                            for t in range(4):
                                sl = slice(t * 512, (t + 1) * 512)
                                self.copy('dve' if (kc + t) % 2 == 0 else 'act', self.XB[:, kc, sl], self.XF[:, kc, sl],
                                          reads=self.xr('f', kc, t * 512, 512), writes=self.xr('b', kc, t * 512, 512))

            finals = list(self.dbg_outs)
            for kc in range(NKC):
                for t in range(4):
                    rl = self.xr('f', kc, t * 512, 512)
                    src = self.XF[:, kc, t * 512:(t + 1) * 512]
                    dst = self.outT[kc * 128:(kc + 1) * 128, t * 512:(t + 1) * 512]
                    o = P.dma('sp', (lambda dst, src: lambda e: e.dma_start(out=dst, in_=src))(dst, src), rl[0], reads=rl)
                    finals.append(o)
            P.finalize(final_waits=finals)
        return nc

    def carve(self, off_words, nwords, dt, pattern=None, **kw):
        ap = self.arena[:, off_words:off_words + nwords]
        if dt == BF16:
            ap = ap.bitcast(BF16)
        if pattern:
            ap = ap.rearrange(pattern, **kw)
        return ap

    def alloc(self, nelem, dt, name, pattern=None, **kw):
        nwords = nelem if dt == F32 else (nelem + 1) // 2
        ap = self.carve(self.aoff, nwords, dt, pattern, **kw)
        self.aoff += nwords
        assert self.aoff <= self.ARENA_F32, (name, self.aoff)
        return Tile(ap, self.P.res(name))

    def arena_ffn(self):
        P = self.P
        o = 0
        self.XB = self.carve(o, NKC * S // 2, BF16, "p (k t) -> p k t", k=NKC)
        o += NKC * S // 2
        self.HT = self.carve(o, NFC * 1024 // 2, BF16, "p (f t) -> p f t", f=NFC)
        o += NFC * 1024 // 2
        self.HTr = [[P.res(f"ht{f}_{t}") for t in range(2)] for f in range(NFC)]
        self.tmpf = []
        for i in range(8):
            self.tmpf.append(Tile(self.carve(o, 512, F32), P.res(f"tmpf{i}")))
            o += 512
        assert o == 23552
        o = 23552 + 2 * 1408
        for i in range(8, 12):
            self.tmpf.append(Tile(self.carve(o, 512, F32), P.res(f"tmpf{i}")))
            o += 512
        assert o <= self.ARENA_F32, o

    def mem_setup(self):
        P = self.P
        MT = self.HT[:, 0:4, :].rearrange("p a b -> p (a b)")[:, 0:NKC * NMEM].rearrange("p (k m) -> p k m", k=NKC)
        mtr = P.res("mt")
        for kc in range(NKC):
            P.dma('pool', (lambda kc: lambda e: e.dma_start(out=MT[:, kc, :], in_=self.memT[kc * 128:(kc + 1) * 128, :]))(kc),
                  mtr, writes=[mtr])
        for c in range(2):
            w = self.w_get('A')
            b = self.bank()
            for kc in range(NKC):
                self.mm(b.ap[:, 0:NMEM], w.ap[:, kc, :], MT[:, kc, :], kc == 0, kc == NKC - 1, [w.res, mtr], [b.res])
            self.copy('act', self.MK.ap[:, c, :], b.ap[:, 0:NMEM], [b.res], [self.MK.res])
        for c in range(2):
            w = self.w_get('A')
            b = self.bank()
            for mt in range(2):
                for kc in range(NKC):
                    self.mm(b.ap[:, mt * 128:(mt + 1) * 128], MT[:, kc, mt * 128:(mt + 1) * 128], w.ap[:, kc, :],
                            kc == 0, kc == NKC - 1, [w.res, mtr], [b.res])
            for mt in range(2):
                self.copy('act', self.MV.ap[:, mt, c, :], b.ap[:, mt * 128:(mt + 1) * 128], [b.res], [self.MV.res])

    def ffn(self, L, which):
        XF, XB, HT, ps = self.XF, self.XB, self.HT, self.ps
        lncol = (L * 3 + (0 if which == 1 else 2)) * 8
        it = 0
        deferred = []
        tf = self.tmpf
        for half in range(2):
            t0 = half * 1024
            for fc in range(NFC):
                if deferred and fc >= 1:
                    dt0, drstd, dmr, ddc = deferred.pop(0)
                    self.ln_apply(dt0, 512, drstd, dmr, lncol, tf[10:12], True, dcs=[ddc])
                wg = self.w_get('A')
                wu = self.w_get('A')
                for tt in range(2):
                    ts0 = t0 + tt * 512
                    tok = slice(ts0, ts0 + 512)
                    bg = ps[(it % 2) * 2]
                    bu = ps[(it % 2) * 2 + 1]
                    for kc in range(NKC):
                        self.mm(bg.ap[:], wg.ap[:, kc, :], XB[:, kc, tok], kc == 0, kc == NKC - 1,
                                [wg.res] + self.xr('b', kc, ts0, 512), [bg.res])
                    for kc in range(NKC):
                        self.mm(bu.ap[:], wu.ap[:, kc, :], XB[:, kc, tok], kc == 0, kc == NKC - 1,
                                [wu.res] + self.xr('b', kc, ts0, 512), [bu.res])
                    st = self.tmpf[it % 2]
                    self.act(st.ap, bg.ap[:], AF.Silu, [bg.res], [st.res])
                    self.tt('dve', HT[:, fc, tt * 512:(tt + 1) * 512], st.ap, bu.ap[:], ALU.mult,
                            [st.res, bu.res], [self.HTr[fc][tt]])
                    it += 1
            sbk = [(ps[2], ps[3]), (ps[0], ps[1])]
            for dc in range(NKC):
                wd = self.w_get('D')
                for tt in range(2):
                    ts0 = t0 + tt * 512
                    tok = slice(ts0, ts0 + 512)
                    by = ps[4 + (it % 2)]
                    for fc in range(NFC):
                        self.mm(by.ap[:], wd.ap[:, fc, :], HT[:, fc, tt * 512:(tt + 1) * 512], fc == 0, fc == NFC - 1,
                                [wd.res, self.HTr[fc][tt]], [by.res])
                    s1, s2 = sbk[tt]
                    self.resid_stats(dc, ts0, 512, by, 2.0 * ALPHA, s1, s2, self.tmpf[2 + it % 2])
                    it += 1
            for tt in range(2):
                s1, s2 = sbk[tt]
                rstd, mr = (tf[6], tf[7]) if tt == 0 else (tf[8], tf[9])
                self.ln_stats(512, s1, s2, 4.0 * LN_EPS, tf[4], tf[5], rstd, mr)
                if half == 0:
                    for dc in range(NKC):
                        deferred.append((t0 + tt * 512, rstd, mr, dc))
                else:
                    self.ln_apply(t0 + tt * 512, 512, rstd, mr, lncol, tf[10:12], True)

    def resid_stats(self, dc, t0, T, by, xscale, s1, s2, sq):
        XF = self.XF
        tok = slice(t0, t0 + T)
        xr = self.xr('f', dc, t0, T)
        self.stt(XF[:, dc, tok], XF[:, dc, tok], xscale, by.ap[:, 0:T], ALU.mult, ALU.add, xr + [by.res], xr)
        self.act(sq.ap[:, 0:T], XF[:, dc, tok], AF.Square, xr, [sq.res])
        self.mm(s1.ap[:, 0:T], self.ones_f.ap[:], XF[:, dc, tok], dc == 0, dc == NKC - 1,
                [self.ones_f.res] + xr, [s1.res])
        self.mm(s2.ap[:, 0:T], self.ones_f.ap[:], sq.ap[:, 0:T], dc == 0, dc == NKC - 1,
                [self.ones_f.res, sq.res], [s2.res])

    def ln_finish(self, t0, T, s1, s2, eps, lncol, tmps, write_xb=True):
        self.ln_stats(T, s1, s2, eps, tmps[4], tmps[5], tmps[6], tmps[7])
        self.ln_apply(t0, T, tmps[6], tmps[7], lncol, tmps[0:2], write_xb)

    def ln_stats(self, T, s1, s2, eps, mean, msq, rstd, mr):
        w = slice(0, T)
        self.act(mean.ap[:, w], s1.ap[:, w], AF.Copy, [s1.res], [mean.res], scale=1.0 / D)
        self.tt('dve', msq.ap[:, w], mean.ap[:, w], mean.ap[:, w], ALU.mult, [mean.res], [msq.res])
        self.stt(msq.ap[:, w], s2.ap[:, w], 1.0 / D, msq.ap[:, w], ALU.mult, ALU.subtract, [s2.res, msq.res], [msq.res])
        self.rsqrt(rstd.ap[:, w], msq.ap[:, w], eps, [msq.res], [rstd.res])
        self.tt('dve', mr.ap[:, w], mean.ap[:, w], rstd.ap[:, w], ALU.mult, [mean.res, rstd.res], [mr.res])

    def ln_apply(self, t0, T, rstd, mr, lncol, us, write_xb=True, dcs=range(NKC)):
        XF, XB = self.XF, self.XB
        tok = slice(t0, t0 + T)
        w = slice(0, T)
        for dc in dcs:
            xr = self.xr('f', dc, t0, T)
            u = us[dc % 2]
            self.tt('dve', u.ap[:, w], XF[:, dc, tok], rstd.ap[:, w], ALU.mult, xr + [rstd.res], [u.res])
            self.tt('dve', u.ap[:, w], u.ap[:, w], mr.ap[:, w], ALU.subtract, [u.res, mr.res], [u.res])
            g = self.LNG.ap[:, lncol + dc:lncol + dc + 1]
            b = self.LNB.ap[:, lncol + dc:lncol + dc + 1]
            self.act(XF[:, dc, tok], u.ap[:, w], AF.Identity, [u.res, self.LNG.res, self.LNB.res], xr, scale=g, bias=b)
            if write_xb:
                self.act(XB[:, dc, tok], u.ap[:, w], AF.Identity, [u.res, self.LNG.res, self.LNB.res],
                         self.xr('b', dc, t0, T), scale=g, bias=b)

    def mem_attn(self, MQ, mq_res, HD, hd_res, t0q, T, t0h, E):
        for h in range(4):
            cq, hp = h // 2, h % 2
            pr = slice(hp * 64, (hp + 1) * 64)
            for mt in range(2):
                b = self.bank()
                self.mm(b.ap[:, 0:T], self.MK.ap[pr, cq, mt * 128:(mt + 1) * 128], MQ[pr, cq, t0q:t0q + T], True, True,
                        [self.MK.res, mq_res], [b.res])
                self.act(E[mt].ap[:, 0:T], b.ap[:, 0:T], AF.Exp, [b.res], [E[mt].res], scale=0.125)
            bn = self.bank()
            bd = self.bank()
            for mt in range(2):
                self.mm(bn.ap[pr, 0:T], self.MV.ap[:, mt, cq, hp * 64:(hp + 1) * 64], E[mt].ap[:, 0:T], mt == 0, mt == 1,
                        [self.MV.res, E[mt].res], [bn.res])
            for mt in range(2):
                self.mm(bd.ap[pr, 0:T], self.ones_b.ap[:, 0:64], E[mt].ap[:, 0:T], mt == 0, mt == 1,
                        [self.ones_b.res, E[mt].res], [bd.res])
            rd = E[2]
            self.P.op('dve', (lambda o, i: lambda e: e.reciprocal(out=o, in_=i))(rd.ap[pr, 0:T], bd.ap[pr, 0:T]),
                      reads=[bd.res], writes=[rd.res])
            self.tt('dve', HD[pr, 6 + cq, t0h:t0h + T], bn.ap[pr, 0:T], rd.ap[pr, 0:T], ALU.mult,
                    [bn.res, rd.res], [hd_res])

    def out_proj_ln(self, L, HD, hd_res, t0h, t0, T, tmps):
        lncol = (L * 3 + 1) * 8
        s1, s2 = self.ps[0], self.ps[1]
        self.ring = list(range(2, 8))
        for dc in range(NKC):
            wo = self.w_get('A')
            by = self.bank()
            for cch in range(NKC):
                self.mm(by.ap[:, 0:T], wo.ap[:, cch, :], HD[:, cch, t0h:t0h + T], cch == 0, cch == NKC - 1,
                        [wo.res, hd_res], [by.res])
            self.resid_stats(dc, t0, T, by, ALPHA, s1, s2, tmps[2 + dc % 2])
        self.ring = list(range(8))
        self.ln_finish(t0, T, s1, s2, LN_EPS, lncol, tmps)
    def rwkv_stage(self):
        P = self.P
        XB = self.XB
        T = 256
        self.aoff = NKC * S // 2
        A = self.alloc
        RP = A(64, F32, "rpar")
        OM = A(32, F32, "om")
        LW = A(768, BF16, "lw")
        GW = A(768, BF16, "gw")
        SM = A(T, F32, "scanmask")
        MLT = A(64, F32, "mlt")
        MLE = A(64, F32, "mle")
        MGT = A(64, F32, "mgt")
        IDB = A(64, F32, "idb")
        IDBb = A(64, BF16, "idbb")
        CAR = A(20, F32, "carry")
        HF = A(6 * 64, F32, "hf", "p (c i) -> p c i", c=6)
        HB = A(6 * 64, BF16, "hb", "p (c i) -> p c i", c=6)
        TD = A(T, BF16, "td")
        SDG = A(T, BF16, "sdg")
        At = A(6 * T, BF16, "At", "p (c t) -> p c t", c=6)
        Bt = A(6 * T, BF16, "Bt", "p (c t) -> p c t", c=6)
        Kt = A(6 * T, BF16, "Kt", "p (c t) -> p c t", c=6)
        Rt = A(6 * T, BF16, "Rt", "p (c t) -> p c t", c=6)
        Bh = A(2 * 768, BF16, "Bh", "p (r c) -> p r c", r=2)
        Kh = A(2 * 768, BF16, "Kh", "p (r c) -> p r c", r=2)
        Vt = A(2 * 768, BF16, "Vt", "p (r c) -> p r c", r=2)
        G = A(6 * T, BF16, "G", "p (c t) -> p c t", c=6)
        BON = A(6 * T, BF16, "BON", "p (c t) -> p c t", c=6)
        GC = A(6 * 4, F32, "gC", "p (c k) -> p c k", c=6)
        HD = A(NKC * T, BF16, "HD", "p (c t) -> p c t", c=NKC)
        MQ = A(2 * T, BF16, "MQ", "p (c t) -> p c t", c=2)
        chres = [[P.res(f"chout{c}_{i}") for i in range(10)] for c in range(6)]
        ov = self.aoff
        SETS = []
        for si in range(2):
            SETS.append(dict(PR=[A(T + 1, F32, f"praw{si}_{i}") for i in range(3)],
                             tm=[A(T, F32, f"rt{si}_{i}") for i in range(14)],
                             tx=[A(T, F32, f"rtx{si}_{i}") for i in range(6)],
                             tb=[A(T, BF16, f"rtb{si}_{i}") for i in range(3)]))
        endA = self.aoff
        self.aoff = ov
        MSETS = []
        for pi in range(2):
            MSETS.append(dict(
                MX=[[A(6 * 64, BF16, f"X{pi}{b}{g}", "p (h t) -> p h t", h=6) for g in range(2)] for b in range(2)],
                MY=[[A(6 * 64, BF16, f"Y{pi}{b}{g}", "p (h t) -> p h t", h=6) for g in range(2)] for b in range(2)],
                ZT=[A(6 * 64, BF16, f"ZT{pi}{g}", "p (h t) -> p h t", h=6) for g in range(2)],
                AK=[A(6 * 64, BF16, f"AK{pi}{g}", "p (h t) -> p h t", h=6) for g in range(2)],
                RB=[A(6 * 64, BF16, f"RB{pi}{g}", "p (h t) -> p h t", h=6) for g in range(2)],
                RK=[A(6 * 64, BF16, f"RK{pi}{g}", "p (h t) -> p h t", h=6) for g in range(2)],
                W0=[A(6 * 64, BF16, f"W0{pi}{g}", "p (h t) -> p h t", h=6) for g in range(2)]))
        WB = [A(6 * 64, BF16, f"WB{g}", "p (h t) -> p h t", h=6) for g in range(2)]
        UB = [A(6 * 64, BF16, f"UB{g}", "p (h t) -> p h t", h=6) for g in range(2)]
        OT = [A(6 * 64, F32, f"OT{g}", "p (h t) -> p h t", h=6) for g in range(2)]
        OO = A(768, F32, "OO", "p (h t) -> p h t", h=12)
        ON = A(768, F32, "ON", "p (h t) -> p h t", h=12)
        SQ = ON
        ONH = A(768, BF16, "ONH")
        ONL = A(768, BF16, "ONL")
        ST = [A(12, F32, f"gst{i}") for i in range(4)]
        lt = [A(T, F32, f"lnt{i}") for i in range(8)]
        ytl = [A(128, F32, f"ytl{i}") for i in range(3)]
        E = [TD, SDG, A(T, F32, "E2")]
        endB = self.aoff
        self.aoff = max(endA, endB)
        tm = SETS[0]['tm']

        self.dma_in('sp', RP.ap, self.rpar_d, RP.res)
        self.dma_in('pool', LW.ap, self.lw_d, LW.res)
        self.dma_in('pool', GW.ap, self.gw_d, GW.res)
        self.ts('dve', OM.ap[:, 0:20], RP.ap[:, PC_MU:PC_MU + 20], -1.0, 1.0, ALU.mult, ALU.add, [RP.res], [OM.res])
        self.ts('dve', OM.ap[:, 20:26], RP.ap[:, PC_KA:PC_KA + 6], -1.0, 1.0, ALU.mult, ALU.add, [RP.res], [OM.res])
        self.memset('dve', SM.ap, 1.0, SM.res)
        self.memset('dve', SM.ap[:, 0:T:64], 0.0, SM.res)
        self.memset('dve', CAR.ap, 0.0, CAR.res)
        self.memset('dve', HF.ap, 0.0, HF.res)
        self.memset('dve', HB.ap, 0.0, HB.res)
        ones = self.ones_f

        def asel(dst, pattern, cmul, op, base=0):
            for hp in range(2):
                pr = slice(hp * 64, (hp + 1) * 64)
                P.op('pool', (lambda pr: lambda e: e.affine_select(
                    out=dst.ap[pr, :], in_=ones.ap[pr, 0:64], pattern=pattern, compare_op=op, fill=0.0,
                    base=base, channel_multiplier=cmul))(pr), reads=[ones.res], writes=[dst.res])
        asel(MLT, [[1, 64]], -1, ALU.is_gt)
        asel(MLE, [[1, 64]], -1, ALU.is_ge)
        asel(MGT, [[-1, 64]], 1, ALU.is_gt)
        asel(IDB, [[-1, 64]], 1, ALU.is_equal)
        self.copy('dve', IDBb.ap, IDB.ap, [IDB.res], [IDBb.res])
        evc = [0]

        def bc6(m):
            return m.ap.unsqueeze(1).to_broadcast([128, 6, 64])
        cut(1)

        rpc = lambda c: RP.ap[:, c:c + 1]

        for blk in range(self.rw_blocks):
            t0 = blk * T
            tokb = slice(t0, t0 + T)
            wt = {}

            def inproj(cc, dst_tile):
                w = self.w_get('A')
                b = self.bank()
                for kc in range(NKC):
                    self.mm(b.ap[:, 0:T], w.ap[:, kc, :], XB[:, kc, tokb], kc == 0, kc == NKC - 1,
                            [w.res] + self.xr('b', kc, t0, T), [b.res])
                self.copy('dve', dst_tile.ap[:, 0:1], CAR.ap[:, cc:cc + 1], [CAR.res], [dst_tile.res])
                self.copy('act', dst_tile.ap[:, 1:T + 1], b.ap[:, 0:T], [b.res], [dst_tile.res])
                self.copy('act', CAR.ap[:, cc:cc + 1], dst_tile.ap[:, T:T + 1], [dst_tile.res], [CAR.res])

            def shift(praw, cc, out):
                self.asc(out.ap, praw.ap[:, 0:T], rpc(PC_MU + cc), None, [praw.res, RP.res], [out.res])
                self.stt(out.ap, praw.ap[:, 1:T + 1], OM.ap[:, cc:cc + 1], out.ap, ALU.mult, ALU.add,
                         [praw.res, OM.res, out.res], [out.res])

            PL = SETS[0]['PR']
            inproj(18, PL[0])
            inproj(19, PL[1])
            zl0, zl1 = SETS[0]['tm'][0], SETS[0]['tm'][1]
            shift(PL[0], 18, zl0)
            shift(PL[1], 19, zl1)
            self.act(TD.ap[0:64, :], zl0.ap[0:64, :], AF.Tanh, [zl0.res], [TD.res])
            self.copy('dve', TD.ap[64:128, :], zl0.ap[64:128, :], [zl0.res], [TD.res])
            self.act(SDG.ap, zl1.ap, AF.Sigmoid, [zl1.res], [SDG.res])
            cut(2)

            def ch_gen(ch, S_):
                PR, tm, tx, tb = S_['PR'], S_['tm'], S_['tx'], S_['tb']
                cr = chres[ch]
                cs = slice(ch * 128, (ch + 1) * 128)
                inproj(ch, PR[0])
                inproj(6 + ch, PR[1])
                inproj(12 + ch, PR[2])
                yield
                zr, zk, zv = tm[2], tm[3], tm[4]
                shift(PR[0], ch, zr)
                shift(PR[1], 6 + ch, zk)
                shift(PR[2], 12 + ch, zv)
                yield
                bw, ba, bg_ = self.bank(), self.bank(), self.bank()
                self.mm(bw.ap[:, 0:T], LW.ap[0:64, cs], TD.ap[0:64, :], True, True, [LW.res, TD.res], [bw.res])
                self.mm(ba.ap[:, 0:T], LW.ap[64:128, cs], TD.ap[64:128, :], True, True, [LW.res, TD.res], [ba.res])
                self.mm(bg_.ap[:, 0:T], GW.ap[:, cs], SDG.ap, True, True, [GW.res, SDG.res], [bg_.res])
                sg, a = tm[5], tm[6]
                self.act(sg.ap, bw.ap[:, 0:T], AF.Sigmoid, [bw.res, RP.res], [sg.res], bias=rpc(PC_W0 + ch), scale=1.0)
                self.act(a.ap, ba.ap[:, 0:T], AF.Sigmoid, [ba.res, RP.res], [a.res], bias=rpc(PC_A0 + ch), scale=1.0)
                self.copy('act', G.ap[:, ch, :], bg_.ap[:, 0:T], [bg_.res], [cr[0]])
                yield
                kk, t1 = tm[7], tm[8]
                t1b, t1c, e2, e3, e4, t2b = tx
                self.asc(kk.ap, zk.ap, rpc(PC_KK + ch), None, [zk.res, RP.res], [kk.res])
                self.act(t1.ap, kk.ap, AF.Square, [kk.res], [t1.res])
                bs = self.bank()
                self.mm(bs.ap[:, 0:T], self.BDF.ap, t1.ap, True, True, [self.BDF.res, t1.res], [bs.res])
                self.ts('dve', t1.ap, bs.ap[:, 0:T], 1e-24, None, ALU.max, None, [bs.res], [t1.res])
                yield
                self.rsqrt(t1.ap, t1.ap, 0.0, [t1.res], [t1.res])
                self.tt('dve', kk.ap, kk.ap, t1.ap, ALU.mult, [kk.res, t1.res], [kk.res])
                yield
                k, bb = tm[9], tm[10]
                self.asc(t1b.ap, a.ap, rpc(PC_KA + ch), OM.ap[:, 20 + ch:21 + ch], [a.res, RP.res, OM.res], [t1b.res])
                self.tt('dve', k.ap, zk.ap, t1b.ap, ALU.mult, [zk.res, t1b.res], [k.res])
                self.tt('dve', bb.ap, kk.ap, a.ap, ALU.mult, [kk.res, a.res], [bb.res])
                yield
                Lp, e, t2 = tm[11], tm[12], tm[13]
                P.op('dve', lambda e_: e_.tensor_tensor_scan(out=Lp.ap, data0=SM.ap, data1=sg.ap, initial=0.0,
                                                             op0=ALU.mult, op1=ALU.add),
                     reads=[SM.res, sg.res], writes=[Lp.res])
                Lp4 = Lp.ap.rearrange("p (c t) -> p c t", c=4)
                yield
                self.act(e.ap, Lp.ap, AF.Exp, [Lp.res], [e.res], scale=DECAY_C)
                self.copy('dve', GC.ap[:, ch, :], e.ap[:, 63:T:64], [e.res], [cr[1]])
                self.tt('dve', Rt.ap[:, ch, :], zr.ap, e.ap, ALU.mult, [zr.res, e.res], [cr[2]])
                self.act(e2.ap, Lp.ap, AF.Exp, [Lp.res], [e2.res], scale=-DECAY_C)
                self.tt('dve', Bt.ap[:, ch, :], bb.ap, e2.ap, ALU.mult, [bb.res, e2.res], [cr[3]])
                self.tt('dve', Kt.ap[:, ch, :], k.ap, e2.ap, ALU.mult, [k.res, e2.res], [cr[4]])
                yield
                self.tt('dve', t2.ap, Lp.ap, sg.ap, ALU.subtract, [Lp.res, sg.res], [t2.res])
                self.act(e3.ap, t2.ap, AF.Exp, [t2.res], [e3.res], scale=DECAY_C)
                self.stt(At.ap[:, ch, :], kk.ap, -1.0, e3.ap, ALU.mult, ALU.mult, [kk.res, e3.res], [cr[5]])
                t24 = t2b.ap.rearrange("p (c t) -> p c t", c=4)
                self.tt('dve', t24, Lp4[:, :, 63:64].to_broadcast([128, 4, 64]), Lp4, ALU.subtract, [Lp.res], [t2b.res])
                self.act(e4.ap, t2b.ap, AF.Exp, [t2b.res], [e4.res], scale=DECAY_C)
                self.tt('dve', tb[0].ap, bb.ap, e4.ap, ALU.mult, [bb.res, e4.res], [tb[0].res])
                self.tt('dve', tb[1].ap, k.ap, e4.ap, ALU.mult, [k.res, e4.res], [tb[1].res])
                self.copy('act', tb[2].ap, zv.ap, [zv.res], [tb[2].res])
                yield
                for pr_ in range(2):
                    ps_ = slice(pr_ * 128, (pr_ + 1) * 128)
                    bt = self.bank()
                    self.tr(bt.ap[:, 0:128], tb[0].ap[:, ps_], [tb[0].res], [bt.res])
                    self.tr(bt.ap[:, 128:256], tb[1].ap[:, ps_], [tb[1].res], [bt.res])
                    self.tr(bt.ap[:, 256:384], tb[2].ap[:, ps_], [tb[2].res], [bt.res])
                    self.copy('act', Bh.ap[:, pr_, cs], bt.ap[:, 0:128], [bt.res], [cr[6]])
                    self.copy('dve', Kh.ap[:, pr_, cs], bt.ap[:, 128:256], [bt.res], [cr[7]])
                    self.copy('act', Vt.ap[:, pr_, cs], bt.ap[:, 256:384], [bt.res], [cr[8]])
                    yield
                self.tt('dve', t1c.ap, zr.ap, k.ap, ALU.mult, [zr.res, k.res], [t1c.res])
                self.asc(t1c.ap, t1c.ap, rpc(PC_RK + ch), None, [t1c.res, RP.res], [t1c.res])
                bs2 = self.bank()
                self.mm(bs2.ap[:, 0:T], self.BDF.ap, t1c.ap, True, True, [self.BDF.res, t1c.res], [bs2.res])
                self.tt('dve', BON.ap[:, ch, :], bs2.ap[:, 0:T], zv.ap, ALU.mult, [bs2.res, zv.res], [cr[9]])

            for c0 in (0, 2, 4):
                alive = [ch_gen(c0, SETS[0]), ch_gen(c0 + 1, SETS[1])]
                while alive:
                    for g_ in list(alive):
                        try:
                            next(g_)
                        except StopIteration:
                            alive.remove(g_)
            cut(3)

            for c2 in range(2):
                w = self.w_get('A')
                b = self.bank()
                for kc in range(NKC):
                    self.mm(b.ap[:, 0:T], w.ap[:, kc, :], XB[:, kc, tokb], kc == 0, kc == NKC - 1,
                            [w.res] + self.xr('b', kc, t0, T), [b.res])
                self.copy('act', MQ.ap[:, c2, :], b.ap[:, 0:T], [b.res], [MQ.res])

            P.barrier()
            def hidx(hg, hh):
                h = 2 * hh + hg
                return h, hh, slice(hg * 64, hg * 64 + 64)

            def tslp(pr_, c2):
                o = pr_ * 128 + c2 * 64
                return slice(o, o + 64)

            def grp(pr_, dst, lhs, rhs, mask):
                for hg in range(2):
                    b = self.bank()
                    for c2 in range(2):
                        for hh in range(6):
                            h, ch, hp = hidx(hg, hh)
                            self.mm(b.ap[c2 * 64:(c2 + 1) * 64, hh * 64:(hh + 1) * 64],
                                    lhs.ap[hp, ch, tslp(pr_, c2)], rhs.ap[hp, ch, tslp(pr_, c2)], True, True,
                                    [lhs.res, rhs.res], [b.res])
                    bv = b.ap[:, 0:384].rearrange("p (h t) -> p h t", h=6)
                    self.tt('dve', dst[hg].ap, bv, bc6(mask), ALU.mult, [b.res, mask.res], [dst[hg].res])

            def grp2(dst, lhs, rhs, accum_into=None, evac='act'):
                for c2 in range(2):
                    b = self.bank()
                    pc = slice(c2 * 64, (c2 + 1) * 64)
                    for hh in range(6):
                        if accum_into is not None:
                            self.mm(b.ap[pc, hh * 64:(hh + 1) * 64], lhs.ap[pc, hh, :], rhs.ap[pc, hh, :], True, False,
                                    [lhs.res, rhs.res], [b.res])
                            self.mm(b.ap[pc, hh * 64:(hh + 1) * 64], IDBb.ap[pc, :], rhs.ap[pc, hh, :], False, True,
                                    [IDBb.res, rhs.res], [b.res])
                        else:
                            self.mm(b.ap[pc, hh * 64:(hh + 1) * 64], lhs.ap[pc, hh, :], rhs.ap[pc, hh, :], True, True,
                                    [lhs.res, rhs.res], [b.res])
                    bv = b.ap[pc, 0:384].rearrange("p (h t) -> p h t", h=6)
                    d_ = accum_into if accum_into is not None else dst
                    eng = 'act' if evc[0] % 2 == 0 else 'dve'
                    evc[0] += 1
                    self.copy(eng, d_.ap[pc], bv, [b.res], [d_.res])

            for pr_ in range(2):
                grp(pr_, MSETS[pr_]['MX'][0], Bt, At, MLT)
            for pr_ in range(2):
                grp(pr_, MSETS[pr_]['MY'][0], At, Bt, MGT)
            for pr_ in range(2):
                grp(pr_, MSETS[pr_]['AK'], Kt, At, MLT)
            for pr_ in range(2):
                grp(pr_, MSETS[pr_]['RB'], Bt, Rt, MLE)
            for pr_ in range(2):
                grp(pr_, MSETS[pr_]['RK'], Kt, Rt, MLE)
            cut(4)
            for pr_ in range(2):
                M_ = MSETS[pr_]
                for hg in range(2):
                    self.tt('dve', M_['ZT'][hg].ap, M_['MX'][0][hg].ap, bc6(IDB), ALU.add,
                            [M_['MX'][0][hg].res, IDB.res], [M_['ZT'][hg].res])
            cur = 0
            for kstep in range(6):
                for hg in range(2):
                    for pr_ in range(2):
                        M_ = MSETS[pr_]
                        Xc, Yc = M_['MX'][cur][hg], M_['MY'][cur][hg]
                        if kstep >= 1:
                            grp2(None, Yc, M_['ZT'][hg], accum_into=M_['ZT'][hg])
                        if kstep <= 3:
                            grp2(M_['MX'][1 - cur][hg], Yc, Xc, evac='act')
                        if kstep <= 4:
                            grp2(M_['MY'][1 - cur][hg], Xc, Yc, evac='dve')
                cur = 1 - cur
            for hg in range(2):
                for c2 in range(2):
                    for pr_ in range(2):
                        M_ = MSETS[pr_]
                        b = self.bank()
                        pc = slice(c2 * 64, (c2 + 1) * 64)
                        for hh in range(6):
                            h = 2 * hh + hg
                            self.mm(b.ap[pc, hh * 64:(hh + 1) * 64], M_['AK'][hg].ap[pc, hh, :],
                                    Vt.ap[pc, pr_, h * 64:(h + 1) * 64], True, True, [M_['AK'][hg].res, Vt.res], [b.res])
                        self.copy('act' if c2 == 0 else 'dve', M_['W0'][hg].ap[pc],
                                  b.ap[pc, 0:384].rearrange("p (h t) -> p h t", h=6), [b.res], [M_['W0'][hg].res])

            for pr_ in range(2):
                M_ = MSETS[pr_]
                ZT, AK, RB, RK, W0 = M_['ZT'], M_['AK'], M_['RB'], M_['RK'], M_['W0']

                def tsl(c2):
                    o = pr_ * 128 + c2 * 64
                    return slice(o, o + 64)

                cut(5)
                ps = self.ps
                for c2 in range(2):
                    pc = slice(c2 * 64, (c2 + 1) * 64)
                    cglob = pr_ * 2 + c2
                    for hg in range(2):
                        b = ps[hg]
                        for hh in range(6):
                            h, ch, hp = hidx(hg, hh)
                            self.mm(b.ap[pc, hh * 64:(hh + 1) * 64], At.ap[hp, ch, tsl(c2)], HB.ap[hp, ch, :], True, True,
                                    [At.res, HB.res], [b.res])
                        self.tt('dve', WB[hg].ap[pc], b.ap[pc, 0:384].rearrange("p (h t) -> p h t", h=6), W0[hg].ap[pc],
                                ALU.add, [b.res, W0[hg].res], [WB[hg].res])
                    for hg in range(2):
                        b = ps[3 + hg]
                        for hh in range(6):
                            h, ch, hp = hidx(hg, hh)
                            self.mm(b.ap[pc, hh * 64:(hh + 1) * 64], Rt.ap[hp, ch, tsl(c2)], HB.ap[hp, ch, :], True, True,
                                    [Rt.res, HB.res], [b.res])
                        self.copy('act', OT[hg].ap[pc], b.ap[pc, 0:384].rearrange("p (h t) -> p h t", h=6), [b.res], [OT[hg].res])
                    for hg in range(2):
                        b = ps[hg]
                        for hh in range(6):
                            self.mm(b.ap[pc, hh * 64:(hh + 1) * 64], ZT[hg].ap[pc, hh, :], WB[hg].ap[pc, hh, :], True, True,
                                    [ZT[hg].res, WB[hg].res], [b.res])
                        self.copy('act', UB[hg].ap[pc], b.ap[pc, 0:384].rearrange("p (h t) -> p h t", h=6), [b.res], [UB[hg].res])
                    bH = ps[2]
                    for hg in range(2):
                        for hh in range(6):
                            h, ch, hp = hidx(hg, hh)
                            hs = slice(h * 64, (h + 1) * 64)
                            self.mm(bH.ap[hp, ch * 64:(ch + 1) * 64], Bh.ap[pc, pr_, hs], UB[hg].ap[pc, hh, :], True, False,
                                    [Bh.res, UB[hg].res], [bH.res])
                            self.mm(bH.ap[hp, ch * 64:(ch + 1) * 64], Kh.ap[pc, pr_, hs], Vt.ap[pc, pr_, hs], False, True,
                                    [Kh.res, Vt.res], [bH.res])
                    for hg in range(2):
                        b = ps[5 + hg]
                        for hh in range(6):
                            h = 2 * hh + hg
                            hs = slice(h * 64, (h + 1) * 64)
                            self.mm(b.ap[pc, hh * 64:(hh + 1) * 64], RB[hg].ap[pc, hh, :], UB[hg].ap[pc, hh, :], True, False,
                                    [RB[hg].res, UB[hg].res], [b.res])
                            self.mm(b.ap[pc, hh * 64:(hh + 1) * 64], RK[hg].ap[pc, hh, :], Vt.ap[pc, pr_, hs], False, True,
                                    [RK[hg].res, Vt.res], [b.res])
                        self.tt('dve', OO.ap.rearrange("p (c g) t -> p c g t", g=2)[pc, :, hg, :],
                                b.ap[pc, 0:384].rearrange("p (h t) -> p h t", h=6),
                                OT[hg].ap[pc], ALU.add, [b.res, OT[hg].res], [OO.res])
                    gcb = GC.ap[:, :, cglob:cglob + 1].to_broadcast([128, 6, 64])
                    self.tt('dve', HF.ap, HF.ap, gcb, ALU.mult, [HF.res, GC.res], [HF.res])
                    self.tt('dve', HF.ap, HF.ap, bH.ap[:, 0:384].rearrange("p (c i) -> p c i", c=6), ALU.add,
                            [HF.res, bH.res], [HF.res])
                    self.copy('act', HB.ap, HF.ap, [HF.res], [HB.res])

                cut(6)
                s1, s2, mean, rstd = ST
                P.op('dve', lambda e_: e_.tensor_reduce(out=s1.ap, in_=OO.ap, axis=AX.X, op=ALU.add),
                     reads=[OO.res], writes=[s1.res])
                self.act(SQ.ap, OO.ap, AF.Square, [OO.res], [SQ.res])
                P.op('dve', lambda e_: e_.tensor_reduce(out=s2.ap, in_=SQ.ap, axis=AX.X, op=ALU.add),
                     reads=[SQ.res], writes=[s2.res])
                self.ts('dve', mean.ap, s1.ap, 1.0 / 64, None, ALU.mult, None, [s1.res], [mean.res])
                self.tt('dve', s1.ap, mean.ap, mean.ap, ALU.mult, [mean.res], [s1.res])
                self.stt(s2.ap, s2.ap, 1.0 / 64, s1.ap, ALU.mult, ALU.subtract, [s2.res, s1.res], [s2.res])
                self.rsqrt(rstd.ap, s2.ap, LNX_EPS, [s2.res], [rstd.res])
                self.tt('dve', ON.ap, OO.ap, mean.ap.unsqueeze(2).to_broadcast([128, 12, 64]), ALU.subtract,
                        [OO.res, mean.res], [ON.res])
                self.tt('dve', ON.ap, ON.ap, rstd.ap.unsqueeze(2).to_broadcast([128, 12, 64]), ALU.mult,
                        [ON.res, rstd.res], [ON.res])
                ONf = ON.ap.rearrange("p h t -> p (h t)")
                self.copy('act', ONH.ap, ONf, [ON.res], [ONH.res])
                self.tt('dve', ONL.ap, ONf, ONH.ap, ALU.subtract, [ON.res, ONH.res], [ONL.res])
                for half in range(2):
                    bt = self.bank()
                    for j in range(3):
                        ch = half * 3 + j
                        self.tr(bt.ap[:, j * 128:(j + 1) * 128], ONH.ap[:, ch * 128:(ch + 1) * 128], [ONH.res], [bt.res],
                                True, False)
                        self.tr(bt.ap[:, j * 128:(j + 1) * 128], ONL.ap[:, ch * 128:(ch + 1) * 128], [ONL.res], [bt.res],
                                False, True)
                    for j in range(3):
                        ch = half * 3 + j
                        y = ytl[j]
                        bsl = slice(pr_ * 128, (pr_ + 1) * 128)
                        self.act(y.ap[:, 0:128], bt.ap[:, j * 128:(j + 1) * 128], AF.Identity, [bt.res, RP.res], [y.res],
                                 scale=rpc(PC_LG + ch), bias=rpc(PC_LB + ch))
                        self.tt('dve', y.ap[:, 0:128], y.ap[:, 0:128], BON.ap[:, ch, bsl], ALU.add, [y.res, BON.res], [y.res])
                        self.tt('dve', HD.ap[:, ch, bsl], y.ap[:, 0:128], G.ap[:, ch, bsl], ALU.mult, [y.res, G.res], [HD.res])

            cut(7)
            self.mem_attn(MQ.ap, MQ.res, HD.ap, HD.res, 0, T, 0, E)
            self.dbg(f"hd{blk}", HD.ap, [128, NKC, T], [HD.res], BF16)
            self.out_proj_ln(0, HD.ap, HD.res, 0, t0, T, lt)
            P.barrier()
    def sb_stage(self):
        P = self.P
        XB = self.XB
        self.aoff = NKC * S // 2
        A = self.alloc
        QT = A(6 * S, BF16, "QT", "p (c t) -> p c t", c=6)
        KT = A(6 * S, BF16, "KT", "p (c t) -> p c t", c=6)
        VT = A(16 * 768, BF16, "VTs", "p (s c) -> p s c", s=16)
        MQ = A(2 * S, BF16, "MQs", "p (c t) -> p c t", c=2)
        QTr = [[P.res(f"qt{c}_{t}") for t in range(4)] for c in range(6)]
        KTr = [P.res(f"kt{c}") for c in range(6)]

        for cc in range(20):
            w = self.w_get('A')
            if 12 <= cc < 18:
                for g4 in range(4):
                    b = self.bank()
                    for j in range(4):
                        st = g4 * 4 + j
                        for kc in range(NKC):
                            self.mm(b.ap[:, j * 128:(j + 1) * 128], XB[:, kc, st * 128:(st + 1) * 128], w.ap[:, kc, :],
                                    kc == 0, kc == NKC - 1, [w.res] + self.xr('b', kc, st * 128, 128), [b.res])
                    dst = VT.ap[:, g4 * 4:(g4 + 1) * 4, (cc - 12) * 128:(cc - 11) * 128]
                    self.copy('act' if g4 % 2 == 0 else 'dve', dst, b.ap[:].rearrange("p (j c) -> p j c", j=4), [b.res], [VT.res])
                continue
            for tt in range(4):
                tok = slice(tt * 512, (tt + 1) * 512)
                b = self.bank()
                for kc in range(NKC):
                    self.mm(b.ap[:], w.ap[:, kc, :], XB[:, kc, tok], kc == 0, kc == NKC - 1,
                            [w.res] + self.xr('b', kc, tt * 512, 512), [b.res])
                eng = 'act' if tt % 2 == 0 else 'dve'
                if cc < 6:
                    self.copy(eng, QT.ap[:, cc, tok], b.ap[:], [b.res], [QTr[cc][tt]])
                elif cc < 12:
                    self.copy(eng, KT.ap[:, cc - 6, tok], b.ap[:], [b.res], [KTr[cc - 6]])
                else:
                    self.copy(eng, MQ.ap[:, cc - 18, tok], b.ap[:], [b.res], [MQ.res])
        P.barrier()

        save = self.aoff
        self.aoff = 0
        TRI = A(128, BF16, "tri")
        TRC = A(128, BF16, "trc")
        MSK = [A(512, BF16, f"dmask{o}") for o in range(4)]
        EX = [A(512, F32, f"ex{i}") for i in range(2)]
        SP = [A(512, F32, f"sp{i}") for i in range(3)]
        SPH = [A(512, BF16, f"sph{i}") for i in range(5)]
        SPL = [A(512, BF16, f"spl{i}") for i in range(5)]
        ARG = [A(512, F32, f"arg{i}") for i in range(2)]
        ATT = [A(512, BF16, f"att{i}") for i in range(3)]
        assert self.aoff <= NKC * S // 2, self.aoff
        ones_b = self.ones_b
        P.op('pool', lambda e: e.affine_select(out=TRI.ap, in_=ones_b.ap[:], pattern=[[-1, 128]], compare_op=ALU.is_gt,
                                               fill=0.0, base=0, channel_multiplier=1), reads=[ones_b.res], writes=[TRI.res])
        P.op('pool', lambda e: e.affine_select(out=TRC.ap, in_=ones_b.ap[:], pattern=[[1, 128]], compare_op=ALU.is_ge,
                                               fill=0.0, base=0, channel_multiplier=-1), reads=[ones_b.res], writes=[TRC.res])
        self.memset('dve', EX[0].ap, 1.0, EX[0].res)
        for o in range(4):
            P.op('pool', (lambda o: lambda e: e.affine_select(out=MSK[o].ap, in_=EX[0].ap, pattern=[[1, 512]],
                                                              compare_op=ALU.is_gt, fill=0.0, base=-128 * o,
                                                              channel_multiplier=-1))(o),
                 reads=[EX[0].res], writes=[MSK[o].res])

        ps = self.ps
        pairs = []
        it = 0
        for hpair in range(6):
            for qt in range(4):
                nkt = 4 * (qt + 1)
                for idx, kt in enumerate(range(nkt - 1, -1, -1)):
                    for j in range(2):
                        pairs.append(dict(h=2 * hpair + j, qt=qt, kt=kt, idx=idx, nkt=nkt, it=it + j, g=len(pairs)))
                it += 2

        def bufs(p):
            g = p['g']
            return dict(zb=ps[2 + g % 3], ex=EX[g % 2], sp=SP[g % 3], sph=SPH[g % 5], spl=SPL[g % 5], arg=ARG[g % 2],
                        att=ATT[g % 3], xs=ps[5 + p['it'] % 2], ob=ps[p['it'] % 2])

        def stage0(p):
            b = bufs(p)
            h, qt, kt = p['h'], p['qt'], p['kt']
            ch, hp = h // 2, slice((h % 2) * 64, (h % 2) * 64 + 64)
            qtok = slice(qt * 512, (qt + 1) * 512)
            diag = kt - 4 * qt
            zb, ex, sp, sph, spl = b['zb'], b['ex'], b['sp'], b['sph'], b['spl']
            self.mm(zb.ap[:], KT.ap[hp, ch, kt * 128:(kt + 1) * 128], QT.ap[hp, ch, qtok], True, True,
                    [KTr[ch], QTr[ch][qt]], [zb.res])
            self.act(ex.ap, zb.ap[:], AF.Exp, [zb.res], [ex.res], scale=0.125)
            if diag >= 0:
                self.tt('pool', ex.ap, ex.ap, MSK[diag].ap, ALU.mult, [ex.res, MSK[diag].res], [ex.res])
            self.act(sp.ap, ex.ap, AF.Ln, [ex.res], [sp.res], bias=self.cst(1.0), scale=1.0)
            self.act(sph.ap, ex.ap, AF.Ln, [ex.res], [sph.res], bias=self.cst(1.0), scale=1.0)
            self.tt('dve', spl.ap, sp.ap, sph.ap, ALU.subtract, [sp.res, sph.res], [spl.res])

        def stage1(p, prev):
            b = bufs(p)
            diag = p['kt'] - 4 * p['qt']
            zb, sp, sph, spl, arg, att, xs = b['zb'], b['sp'], b['sph'], b['spl'], b['arg'], b['att'], b['xs']
            first = p['idx'] == 0
            if not first:
                pb = bufs(prev)
                self.mm(xs.ap[:], TRC.ap, pb['sph'].ap, False, False, [TRC.res, pb['sph'].res], [xs.res], skip=True)
                self.mm(xs.ap[:], TRC.ap, pb['spl'].ap, False, False, [TRC.res, pb['spl'].res], [xs.res], skip=True)
            self.mm(xs.ap[:], TRI.ap, sph.ap, first, False, [TRI.res, sph.res], [xs.res], skip=not first)
            self.mm(xs.ap[:], TRI.ap, spl.ap, False, True, [TRI.res, spl.res], [xs.res], skip=not first)
            self.stt(arg.ap, zb.ap[:], 0.125, sp.ap, ALU.mult, ALU.subtract, [zb.res, sp.res], [arg.res])
            self.tt('dve', arg.ap, arg.ap, xs.ap[:], ALU.subtract, [arg.res, xs.res], [arg.res])

        def stage1b(p):
            b = bufs(p)
            diag = p['kt'] - 4 * p['qt']
            arg, att = b['arg'], b['att']
            self.act(att.ap, arg.ap, AF.Exp, [arg.res], [att.res])
            if diag >= 0:
                self.tt('pool', att.ap, att.ap, MSK[diag].ap, ALU.mult, [att.res, MSK[diag].res], [att.res])

        def stage2(p):
            b = bufs(p)
            h, qt, kt = p['h'], p['qt'], p['kt']
            ch, hp = h // 2, slice((h % 2) * 64, (h % 2) * 64 + 64)
            qtok = slice(qt * 512, (qt + 1) * 512)
            ob, att = b['ob'], b['att']
            self.mm(ob.ap[hp, :], VT.ap[:, kt, h * 64:(h + 1) * 64], att.ap, p['idx'] == 0, p['idx'] == p['nkt'] - 1,
                    [VT.res, att.res], [ob.res])
            if p['idx'] == p['nkt'] - 1:
                self.copy('act', QT.ap[hp, ch, qtok], ob.ap[hp, :], [ob.res], [QTr[ch][qt]])

        N = len(pairs)
        for step in range(N + 5):
            if 0 <= step - 2 < N:
                p = pairs[step - 2]
                stage1(p, pairs[step - 4] if p['idx'] > 0 else None)
            if 0 <= step - 3 < N:
                stage1b(pairs[step - 3])
            if step < N:
                stage0(pairs[step])
            if 0 <= step - 5 < N:
                stage2(pairs[step - 5])
        P.barrier()
        self.aoff = 0
        E = [A(512, BF16, "E0s"), A(512, BF16, "E1s"), A(512, F32, "E2s")]
        lt = [A(512, F32, f"lnts{i}") for i in range(8)]
        assert self.aoff <= NKC * S // 2, self.aoff
        self.aoff = save

        for tt in range(4):
            t0 = tt * 512
            HDv = None
            self.mem_attn_sb(MQ, t0, E)
            self.out_proj_ln_sb(QT, QTr, MQ, t0, lt)

    def mem_attn_sb(self, MQ, t0, E):
        T = 512
        for h in range(4):
            cq, hp = h // 2, h % 2
            pr = slice(hp * 64, (hp + 1) * 64)
            for mt in range(2):
                b = self.bank()
                self.mm(b.ap[:, 0:T], self.MK.ap[pr, cq, mt * 128:(mt + 1) * 128], MQ.ap[pr, cq, t0:t0 + T], True, True,
                        [self.MK.res, MQ.res], [b.res])
                self.act(E[mt].ap[:, 0:T], b.ap[:, 0:T], AF.Exp, [b.res], [E[mt].res], scale=0.125)
            bn = self.bank()
            bd = self.bank()
            for mt in range(2):
                self.mm(bn.ap[pr, 0:T], self.MV.ap[:, mt, cq, hp * 64:(hp + 1) * 64], E[mt].ap[:, 0:T], mt == 0, mt == 1,
                        [self.MV.res, E[mt].res], [bn.res])
            for mt in range(2):
                self.mm(bd.ap[pr, 0:T], self.ones_b.ap[:, 0:64], E[mt].ap[:, 0:T], mt == 0, mt == 1,
                        [self.ones_b.res, E[mt].res], [bd.res])
            rd = E[2]
            self.P.op('dve', (lambda o, i: lambda e: e.reciprocal(out=o, in_=i))(rd.ap[pr, 0:T], bd.ap[pr, 0:T]),
                      reads=[bd.res], writes=[rd.res])
            self.tt('dve', MQ.ap[pr, cq, t0:t0 + T], bn.ap[pr, 0:T], rd.ap[pr, 0:T], ALU.mult,
                    [bn.res, rd.res], [MQ.res])

    def out_proj_ln_sb(self, QT, QTr, MQ, t0, tmps):
        T = 512
        tt = t0 // 512
        lncol = (1 * 3 + 1) * 8
        s1, s2 = self.ps[0], self.ps[1]
        self.ring = list(range(2, 8))
        for dc in range(NKC):
            wo = self.w_get('A')
            by = self.bank()
            for cch in range(NKC):
                if cch < 6:
                    rhs, rr = QT.ap[:, cch, t0:t0 + T], QTr[cch][tt]
                else:
                    rhs, rr = MQ.ap[:, cch - 6, t0:t0 + T], MQ.res
                self.mm(by.ap[:, 0:T], wo.ap[:, cch, :], rhs, cch == 0, cch == NKC - 1, [wo.res, rr], [by.res])
            self.resid_stats(dc, t0, T, by, ALPHA, s1, s2, tmps[2 + dc % 2])
        self.ring = list(range(8))
        self.ln_finish(t0, T, s1, s2, LN_EPS, lncol, tmps, write_xb=False)


def tile_a(w):
    lead = w.shape[:-2]
    n = w.shape[-1] // 128
    w = w.reshape(lead + (NKC, 128, n, 128))
    nd = len(lead)
    w = np.transpose(w, tuple(range(nd)) + (nd + 2, nd + 1, nd + 0, nd + 3))
    return np.ascontiguousarray(w).reshape(lead + (n, 128, NKC * 128))


def tile_d(w):
    lead = w.shape[:-2]
    w = w.reshape(lead + (NFC, 128, NKC, 128))
    nd = len(lead)
    w = np.transpose(w, tuple(range(nd)) + (nd + 2, nd + 1, nd + 0, nd + 3))
    return np.ascontiguousarray(w).reshape(lead + (NKC, 128, NFC * 128))


def cols(v, n):
    return np.asarray(v, dtype=np.float32).reshape(n, 128).T


def prep_shared(inputs):
    f = lambda a: np.asarray(a, dtype=np.float32)
    sh = {}
    for which in (1, 2):
        sh[f"g{which}"] = tile_a(f(inputs[f"ffn{which}_w_gate"]))
        sh[f"u{which}"] = tile_a(f(inputs[f"ffn{which}_w_up"]))
        sh[f"d{which}"] = tile_d(f(inputs[f"ffn{which}_w_down"]))
    sh["lng"] = np.ascontiguousarray(f(inputs["ln_g"]).reshape(6, NKC, 128).transpose(2, 0, 1).reshape(128, 48))
    sh["lnb"] = np.ascontiguousarray(f(inputs["ln_b"]).reshape(6, NKC, 128).transpose(2, 0, 1).reshape(128, 48))
    sh["wout"] = tile_a(f(inputs["w_out"]))
    sh["wmem"] = tile_a(f(inputs["w_mem_kv"]))
    sh["rwin"] = tile_a(f(inputs["rwkv_w_in"])[0])
    sh["sbin"] = tile_a(f(inputs["sb_w_in"])[0])
    rp = np.zeros((128, 64), np.float32)
    rp[:, PC_MU:PC_MU + 20] = cols(f(inputs["rwkv_mu"])[0], 20)
    for off, key in ((PC_W0, "rwkv_w0"), (PC_A0, "rwkv_a0"), (PC_KK, "rwkv_k_k"), (PC_KA, "rwkv_k_a"),
                     (PC_RK, "rwkv_r_k"), (PC_LG, "rwkv_lnx_g"), (PC_LB, "rwkv_lnx_b")):
        rp[:, off:off + 6] = cols(f(inputs[key])[0].reshape(-1), 6)
    sh["rpar"] = rp
    sh["lw"] = np.ascontiguousarray(np.concatenate([f(inputs["rwkv_w_up"])[0], f(inputs["rwkv_a_up"])[0]], axis=0))
    sh["gw"] = np.ascontiguousarray(f(inputs["rwkv_g_up"])[0])
    return sh


_NC_CACHE = {}


def get_nc(stop_after=None, start_at=0, dbg=(), rw_blocks=8):
    key = (stop_after, start_at, tuple(dbg), rw_blocks)
    if key not in _NC_CACHE:
        _NC_CACHE[key] = Builder(stop_after, start_at, dbg, rw_blocks).build()
    return _NC_CACHE[key]


def make_in_maps(inputs, cores=8, x_override=None):
    sh = prep_shared(inputs)
    x = np.asarray(inputs["x"], dtype=np.float32) if x_override is None else x_override
    mem = np.asarray(inputs["mem"], dtype=np.float32)
    in_maps = []
    for b in range(cores):
        m = dict(sh)
        m["xT"] = np.ascontiguousarray(x[b].T)
        m["memT"] = np.ascontiguousarray(mem[b].T)
        in_maps.append(m)
    return in_maps


def run(inputs, cores=8, stop_after=None, start_at=0, dbg=(), trace=False, x_override=None, rw_blocks=8):
    sh = prep_shared(inputs)
    x = np.asarray(inputs["x"], dtype=np.float32) if x_override is None else x_override
    mem = np.asarray(inputs["mem"], dtype=np.float32)
    in_maps = []
    for b in range(cores):
        m = dict(sh)
        m["xT"] = np.ascontiguousarray(x[b].T)
        m["memT"] = np.ascontiguousarray(mem[b].T)
        in_maps.append(m)
    nc = get_nc(stop_after, start_at, dbg, rw_blocks)
    res = run_bass_kernel_spmd(nc, in_maps, core_ids=list(range(cores)), trace=trace)
    out = np.stack([np.ascontiguousarray(r["outT"].T) for r in res.results], axis=0)
    return out, res


def kernel(**inputs):
    out, _ = run(inputs, cores=8)
    return out.astype(np.float32)
```

```python
import os
import numpy as np
from contextlib import ExitStack

import concourse.bass as bass
import concourse.mybir as mybir
from concourse.bass_utils import run_bass_kernel_spmd

F32 = mybir.dt.float32
BF16 = mybir.dt.bfloat16
AF = mybir.ActivationFunctionType
ALU = mybir.AluOpType

D = 1024
S = 2048
DFF = 2816
NKC = 8
NFC = 22
NMEM = 256
ALPHA = 4.0 ** 0.25
LN_EPS = 1e-5
LNX_EPS = 64e-5
DECAY_C = -float(np.exp(-0.5))

ENGS = ['pe', 'act', 'dve', 'pool', 'sp']
SAME_ENGINE_SYNC = os.environ.get("SES", "1") == "1"


class Res:
    __slots__ = ('name', 'last_w', 'readers', 'dreaders', 'sem', 'dma_cnt', 'excl')

    def __init__(self, name):
        self.name = name
        self.excl = False
        self.last_w = None
        self.readers = {}
        self.dreaders = []
        self.sem = None
        self.dma_cnt = 0


class OpRec:
    __slots__ = ('eng', 'fn', 'is_dma', 'deps', 'flagged', 'count', 'dres', 'dval', 'idx')


class Prog:
    def __init__(self, nc, es):
        self.nc = nc
        self.es = es
        self.q = {e: [] for e in ENGS}
        self.nres = 0
        self.all_dma = []
        self.bar = {e: None for e in ENGS}

    def res(self, name=None):
        self.nres += 1
        return Res((name or "r") + str(self.nres))

    def _track(self, op, reads, writes):
        deps = []
        for r in reads:
            if r.last_w is not None:
                deps.append(r.last_w)
        for w in writes:
            if w.last_w is not None:
                deps.append(w.last_w)
            deps.extend(w.readers.values())
            deps.extend(w.dreaders)
        b = self.bar[op.eng]
        if b is not None:
            deps.extend(b)
            self.bar[op.eng] = None
        for r in reads:
            if op.is_dma:
                r.dreaders.append(op)
            else:
                r.readers[op.eng] = op
        for w in writes:
            w.last_w = op
            w.readers = {}
            w.dreaders = []
        best = {}
        out = []
        for d in deps:
            if d is op:
                continue
            if d.is_dma:
                out.append(d)
                continue
            if (not op.is_dma) and d.eng == 'pe' and op.eng == 'pe':
                continue
            if (not op.is_dma) and d.eng == op.eng and not SAME_ENGINE_SYNC:
                continue
            if d.eng not in best or best[d.eng].idx < d.idx:
                best[d.eng] = d
        out.extend(best.values())
        op.deps = out

    def op(self, eng, fn, reads=(), writes=()):
        if any(r.excl for r in reads):
            writes = list(writes) + [r for r in reads if r.excl]
            reads = [r for r in reads if not r.excl]
        o = OpRec()
        o.eng = eng
        o.fn = fn
        o.is_dma = False
        o.flagged = False
        o.count = None
        o.dres = None
        o.dval = None
        o.idx = len(self.q[eng])
        self._track(o, reads, writes)
        self.q[eng].append(o)
        return o

    def dma(self, eng, fn, sres, reads=(), writes=()):
        o = OpRec()
        o.eng = eng
        o.fn = fn
        o.is_dma = True
        o.flagged = False
        o.count = None
        o.dres = sres
        sres.dma_cnt += 16
        o.dval = sres.dma_cnt
        o.idx = len(self.q[eng])
        self._track(o, reads, writes)
        self.q[eng].append(o)
        self.all_dma.append(o)
        return o

    def barrier(self):
        deps = list(self.all_dma)
        self.all_dma = []
        for e in ENGS:
            for o in reversed(self.q[e]):
                if not o.is_dma:
                    deps.append(o)
                    break
        for e in ENGS:
            prev = self.bar[e]
            self.bar[e] = (prev or []) + deps

    def finalize(self, final_waits=()):
        nc = self.nc
        es = self.es
        for e in ENGS:
            for o in self.q[e]:
                for d in o.deps:
                    if not d.is_dma:
                        d.flagged = True
        for o in final_waits:
            if not o.is_dma:
                o.flagged = True
        esem = {}
        for e in ENGS:
            c = 0
            for o in self.q[e]:
                if o.is_dma:
                    if o.dres.sem is None:
                        o.dres.sem = es.enter_context(nc.semaphore("d_" + o.dres.name))
                elif o.flagged:
                    c += 1
                    o.count = c
            esem[e] = es.enter_context(nc.semaphore("e_" + e))

        def tok(d):
            if d.is_dma:
                return d.dres.sem, d.dval
            return esem[d.eng], d.count

        def emit(ename, eng):
            known = {}
            for o in self.q[ename]:
                need = {}
                for d in o.deps:
                    s, v = tok(d)
                    k = id(s)
                    if known.get(k, 0) >= v:
                        continue
                    if k not in need or need[k][1] < v:
                        need[k] = (s, v)
                for k, (s, v) in need.items():
                    eng.wait_ge(s, v)
                    known[k] = v
                ins = o.fn(eng)
                if o.is_dma:
                    ins.then_inc(o.dres.sem, 16)
                elif o.flagged:
                    ins.then_inc(esem[ename], 1)
            if ename == 'sp':
                for d in final_waits:
                    s, v = tok(d)
                    eng.wait_ge(s, v)

        with nc.Block() as block:
            @block.tensor
            def _(eng):
                emit('pe', eng)

            @block.scalar
            def _(eng):
                emit('act', eng)

            @block.vector
            def _(eng):
                emit('dve', eng)

            @block.gpsimd
            def _(eng):
                emit('pool', eng)

            @block.sync
            def _(eng):
                emit('sp', eng)


class CutHere(Exception):
    pass


import os
RW_CUT = int(os.environ.get("RW_CUT", "0"))


def cut(n):
    if RW_CUT == n:
        raise CutHere()


class Tile:
    __slots__ = ('ap', 'res')

    def __init__(self, ap, res):
        if not isinstance(ap, bass.AP):
            ap = ap[:]
        self.ap = ap
        self.res = res


AX = mybir.AxisListType
RW_ORDER = [18, 19] + [c for ch in range(6) for c in (ch, 6 + ch, 12 + ch)] + [20, 21]
PC_MU, PC_W0, PC_A0, PC_KK, PC_KA, PC_RK, PC_LG, PC_LB = 0, 20, 26, 32, 38, 44, 50, 56


class Builder:
    STAGES = ["l0_x1", "l0_x2", "l0_x3", "l1_x1", "l1_x2", "l1_x3"]

    def __init__(self, stop_after=None, start_at=0, dbg=(), rw_blocks=8):
        self.stop_after = stop_after
        self.start_at = start_at
        self.rw_blocks = rw_blocks
        last = self.STAGES.index(stop_after) if stop_after else 5
        self.run_stages = self.STAGES[start_at:last + 1]
        self.dbg_names = set(dbg)
        self.dbg_outs = []
        self.nc = bass.Bass("TRN2", target_bir_lowering=False)
        self.es = ExitStack()
        self.bank_rr = 0
        self.ring = list(range(8))

    def sb(self, name, shape, dt):
        return self.es.enter_context(self.nc.sbuf_tensor(name, shape, dt))

    def dram_in(self, name, shape):
        return self.nc.dram_tensor(name, list(shape), F32, kind="ExternalInput").ap()

    def mm(self, out, lhsT, rhs, start, stop, reads, writes, skip=False):
        return self.P.op('pe', lambda e: e.matmul(out, lhsT=lhsT, rhs=rhs, start=start, stop=stop, skip_group_check=skip),
                         reads=reads, writes=writes)

    def tr(self, out, in_, reads, writes, start=True, stop=True):
        ident = self.IDB16.ap
        return self.P.op('pe', lambda e: e.matmul(out, lhsT=in_, rhs=ident, start=start, stop=stop),
                         reads=list(reads) + [self.IDB16.res], writes=writes)

    def act(self, out, in_, func, reads, writes, scale=None, bias=None):
        kw = {}
        if scale is not None:
            kw['scale'] = scale
        if bias is not None:
            kw['bias'] = bias
        return self.P.op('act', lambda e: e.activation(out=out, in_=in_, func=func, **kw),
                         reads=reads, writes=writes)

    def asc(self, out, in_, scale, bias, reads, writes):
        if bias is None:
            return self.P.op('act', lambda e: e.activation(out=out, in_=in_, func=AF.Copy, scale=scale),
                             reads=reads, writes=writes)
        return self.P.op('act', lambda e: e.activation(out=out, in_=in_, func=AF.Identity, scale=scale, bias=bias),
                         reads=reads, writes=writes)

    def tt(self, eng, out, in0, in1, op, reads, writes):
        return self.P.op(eng, lambda e: e.tensor_tensor(out=out, in0=in0, in1=in1, op=op),
                         reads=reads, writes=writes)

    def ts(self, eng, out, in0, s1, s2, op0, op1, reads, writes):
        if op1 is None:
            return self.P.op(eng, lambda e: e.tensor_scalar(out=out, in0=in0, scalar1=s1, scalar2=None, op0=op0),
                             reads=reads, writes=writes)
        return self.P.op(eng, lambda e: e.tensor_scalar(out=out, in0=in0, scalar1=s1, scalar2=s2, op0=op0, op1=op1),
                         reads=reads, writes=writes)

    def stt(self, out, in0, scalar, in1, op0, op1, reads, writes):
        return self.P.op('dve', lambda e: e.scalar_tensor_tensor(out=out, in0=in0, scalar=scalar, in1=in1,
                                                                 op0=op0, op1=op1),
                         reads=reads, writes=writes)

    def rsqrt(self, out, in_, eps, reads, writes, eng='dve'):
        self.P.op('act', lambda e: e.activation(out=out, in_=in_, func=AF.Sqrt, bias=self.cst(eps), scale=1.0),
                  reads=list(reads) + [self.CONST.res], writes=writes)
        return self.P.op(eng, lambda e: e.reciprocal(out=out, in_=out), reads=writes, writes=writes)

    def cst(self, v, n=128, base=0):
        c = self.const_cols[v]
        return self.CONST.ap[base:base + n, c:c + 1]

    def copy(self, eng, out, in_, reads, writes):
        if eng == 'act':
            return self.P.op('act', lambda e: e.copy(out=out, in_=in_), reads=reads, writes=writes)
        return self.P.op(eng, lambda e: e.tensor_copy(out=out, in_=in_), reads=reads, writes=writes)

    def memset(self, eng, ap, v, res):
        return self.P.op(eng, lambda e: e.memset(ap, v), writes=[res])

    def dma_in(self, eng, dst, src, res):
        return self.P.dma(eng, lambda e: e.dma_start(out=dst, in_=src), res, writes=[res])

    def bank(self):
        ring = self.ring
        b = self.ps[ring[self.bank_rr % len(ring)]]
        self.bank_rr += 1
        return b

    def dbg(self, name, ap, shape, res_list, dt=F32):
        if name not in self.dbg_names:
            return
        t = self.nc.dram_tensor("dbg_" + name, list(shape), dt, kind="ExternalOutput").ap()
        r = self.P.res("dbg_" + name)
        o = self.P.dma('sp', lambda e: e.dma_start(out=t, in_=ap), r, reads=res_list)
        self.dbg_outs.append(o)

    def xr(self, kind, kc, t0, T):
        rr = self.XFr if kind == 'f' else self.XBr
        return [rr[kc][b] for b in range(t0 // 256, (t0 + T + 255) // 256)]

    def w_schedule(self):
        sched = []
        for i in range(4):
            sched.append(('A', self.wmem_d[i]))
        for st in self.run_stages:
            L = int(st[1])
            if st.endswith("x2"):
                sched.extend(self.attn_w_schedule(L))
                continue
            which = 1 if st.endswith("x1") else 2
            g, u, d = self.wd[f"g{which}"], self.wd[f"u{which}"], self.wd[f"d{which}"]
            for half in range(2):
                for fc in range(NFC):
                    sched.append(('A', g[L, fc]))
                    sched.append(('A', u[L, fc]))
                for dc in range(NKC):
                    sched.append(('D', d[L, dc]))
        return sched

    def attn_w_schedule(self, L):
        s = []
        if L == 0:
            for blk in range(self.rw_blocks):
                for cc in RW_ORDER:
                    s.append(('A', self.rwin_d[cc]))
                for dc in range(NKC):
                    s.append(('A', self.wout_d[0, dc]))
        else:
            for cc in range(20):
                s.append(('A', self.sbin_d[cc]))
            for tt in range(4):
                for dc in range(NKC):
                    s.append(('A', self.wout_d[1, dc]))
        return s

    def w_init(self):
        self.NA = 6
        self.ND = 2
        self.wslots = {'A': [], 'D': []}
        for i in range(self.NA):
            t = self.sb(f"wa{i}", [128, NKC, 128], BF16)
            self.wslots['A'].append(Tile(t, self.P.res(f"wa{i}")))
        for i in range(self.ND):
            o = 23552 + i * 1408
            t = self.carve(o, 1408, BF16, "p (f d) -> p f d", f=NFC)
            self.wslots['D'].append(Tile(t, self.P.res(f"wd{i}")))
        self.wsched = self.w_schedule()
        self.w_issued = 0
        self.w_next = 0
        self.w_kcount = {'A': 0, 'D': 0}
        self.w_tile_slot = []
        self.w_ahead = {'A': 3, 'D': 1}

    def _w_issue_one(self):
        kind, src = self.wsched[self.w_issued]
        n = self.w_kcount[kind]
        self.w_kcount[kind] = n + 1
        slots = self.wslots[kind]
        t = slots[n % len(slots)]
        self.w_tile_slot.append(t)
        if kind == 'A':
            dst = t.ap[:].rearrange("p a b -> p (a b)")
            self.P.dma('pool', lambda e: e.dma_start(out=dst, in_=src), t.res, writes=[t.res])
        else:
            dst = t.ap.rearrange("p a b -> p (a b)").rearrange("p (h x) -> p h x", h=2)
            s2 = src.rearrange("p (h x) -> p h x", h=2)
            self.P.dma('pool', lambda e: e.dma_start(out=dst, in_=s2), t.res, writes=[t.res])
        self.w_issued += 1

    def w_get(self, kind):
        i = self.w_next
        assert self.wsched[i][0] == kind, (i, self.wsched[i][0], kind)
        while self.w_issued <= i:
            self._w_issue_one()
        ahead_cnt = {'A': 0, 'D': 0}
        for j in range(i + 1, self.w_issued):
            ahead_cnt[self.wsched[j][0]] += 1
        while self.w_issued < len(self.wsched):
            k = self.wsched[self.w_issued][0]
            if ahead_cnt[k] >= self.w_ahead[k]:
                break
            self._w_issue_one()
            ahead_cnt[k] += 1
        self.w_next += 1
        return self.w_tile_slot[i]

    def build(self):
        nc = self.nc
        es = self.es
        with es:
            self.P = P = Prog(nc, es)
            self.xT = self.dram_in("xT", [D, S])
            self.memT = self.dram_in("memT", [D, NMEM])
            self.outT = nc.dram_tensor("outT", [D, S], F32, kind="ExternalOutput").ap()
            self.wd = {}
            for which in (1, 2):
                self.wd[f"g{which}"] = self.dram_in(f"g{which}", [2, NFC, 128, NKC * 128])
                self.wd[f"u{which}"] = self.dram_in(f"u{which}", [2, NFC, 128, NKC * 128])
                self.wd[f"d{which}"] = self.dram_in(f"d{which}", [2, NKC, 128, NFC * 128])
            self.lng_d = self.dram_in("lng", [128, 48])
            self.lnb_d = self.dram_in("lnb", [128, 48])
            self.wout_d = self.dram_in("wout", [2, NKC, 128, NKC * 128])
            self.wmem_d = self.dram_in("wmem", [4, 128, NKC * 128])
            self.rwin_d = self.dram_in("rwin", [22, 128, NKC * 128])
            self.sbin_d = self.dram_in("sbin", [20, 128, NKC * 128])
            self.rpar_d = self.dram_in("rpar", [128, 64])
            self.lw_d = self.dram_in("lw", [128, 768])
            self.gw_d = self.dram_in("gw", [128, 768])

            self.XF = self.sb("XF", [128, NKC, S], F32)
            self.XFr = [[P.res(f"xf{k}_{t}") for t in range(8)] for k in range(NKC)]
            self.XBr = [[P.res(f"xb{k}_{t}") for t in range(8)] for k in range(NKC)]
            self.LNG = Tile(self.sb("LNG", [128, 48], F32), P.res("lng"))
            self.LNB = Tile(self.sb("LNB", [128, 48], F32), P.res("lnb"))
            self.ones_f = Tile(self.sb("ones_f", [128, 128], F32), P.res("ones_f"))
            self.ones_b = Tile(self.sb("ones_b", [128, 128], BF16), P.res("ones_b"))
            self.IDF = Tile(self.sb("IDF", [128, 128], F32), P.res("idf"))
            self.BDF = Tile(self.sb("BDF", [128, 128], F32), P.res("bdf"))
            self.IDB16 = Tile(self.sb("IDB16", [128, 128], BF16), P.res("idb16"))
            self.MK = Tile(self.sb("MK", [128, 2, NMEM], BF16), P.res("mk"))
            self.MV = Tile(self.sb("MV", [128, 2, 2, 128], BF16), P.res("mv"))
            self.ARENA_F32 = 32600
            self.arena = self.sb("arena", [128, self.ARENA_F32], F32)
            self.w_init()
            self.ps = []
            for i in range(8):
                t = es.enter_context(nc.psum_tensor(f"ps{i}", [128, 512], F32))
                self.ps.append(Tile(t, P.res(f"ps{i}")))
                self.ps[-1].res.excl = True

            self.memset('dve', self.ones_f.ap[:], 1.0, self.ones_f.res)
            self.memset('dve', self.ones_b.ap[:], 1.0, self.ones_b.res)
            self.CONST = Tile(self.sb("CONST", [128, 8], F32), P.res("const"))
            self.const_cols = {}
            for ci, cv in enumerate([4.0 * LN_EPS, LN_EPS, LNX_EPS, 1.0, 0.0]):
                self.const_cols[cv] = ci
                self.memset('dve', self.CONST.ap[:, ci:ci + 1], cv, self.CONST.res)
            P.op('pool', lambda e: e.affine_select(out=self.IDF.ap[:], in_=self.ones_f.ap[:], pattern=[[-1, 128]],
                                                   compare_op=ALU.is_equal, fill=0.0, base=0, channel_multiplier=1),
                 reads=[self.ones_f.res], writes=[self.IDF.res])
            self.copy('dve', self.IDB16.ap, self.IDF.ap, [self.IDF.res], [self.IDB16.res])
            self.memset('dve', self.BDF.ap[:], 0.0, self.BDF.res)
            self.memset('dve', self.BDF.ap[0:64, 0:64], 1.0, self.BDF.res)
            self.memset('dve', self.BDF.ap[64:128, 64:128], 1.0, self.BDF.res)

            self.dma_in('sp', self.LNG.ap[:], self.lng_d, self.LNG.res)
            self.dma_in('sp', self.LNB.ap[:], self.lnb_d, self.LNB.res)
            for kc in range(NKC):
                for t in range(8):
                    r = self.XFr[kc][t]
                    self.dma_in('sp', self.XF[:, kc, t * 256:(t + 1) * 256],
                                self.xT[kc * 128:(kc + 1) * 128, t * 256:(t + 1) * 256], r)

            self.arena_ffn()
            self.mem_setup()
            for kc in range(NKC):
                for t in range(4):
                    sl = slice(t * 512, (t + 1) * 512)
                    self.copy('dve' if (kc + t) % 2 == 0 else 'act', self.XB[:, kc, sl], self.XF[:, kc, sl],
                              reads=self.xr('f', kc, t * 512, 512), writes=self.xr('b', kc, t * 512, 512))

            for st in self.run_stages:
                L = int(st[1])
                if st.endswith("x1"):
                    self.ffn(L, 1)
                elif st.endswith("x3"):
                    self.ffn(L, 2)
                else:
                    P.barrier()
                    if L == 0:
                        try:
                            self.rwkv_stage()
                        except CutHere:
                            pass
                    else:
                        self.sb_stage()
                    P.barrier()
                    self.arena_ffn()
                    if L == 1:
                        for kc in range(NKC):
                            for t in range(4):
                                sl = slice(t * 512, (t + 1) * 512)
                                self.copy('dve' if (kc + t) % 2 == 0 else 'act', self.XB[:, kc, sl], self.XF[:, kc, sl],
                                          reads=self.xr('f', kc, t * 512, 512), writes=self.xr('b', kc, t * 512, 512))

            finals = list(self.dbg_outs)
            for kc in range(NKC):
                for t in range(4):
                    rl = self.xr('f', kc, t * 512, 512)
                    src = self.XF[:, kc, t * 512:(t + 1) * 512]
                    dst = self.outT[kc * 128:(kc + 1) * 128, t * 512:(t + 1) * 512]
                    o = P.dma('sp', (lambda dst, src: lambda e: e.dma_start(out=dst, in_=src))(dst, src), rl[0], reads=rl)
                    finals.append(o)
            P.finalize(final_waits=finals)
        return nc

    def carve(self, off_words, nwords, dt, pattern=None, **kw):
        ap = self.arena[:, off_words:off_words + nwords]
        if dt == BF16:
            ap = ap.bitcast(BF16)
        if pattern:
            ap = ap.rearrange(pattern, **kw)
        return ap

    def alloc(self, nelem, dt, name, pattern=None, **kw):
        nwords = nelem if dt == F32 else (nelem + 1) // 2
        ap = self.carve(self.aoff, nwords, dt, pattern, **kw)
        self.aoff += nwords
        assert self.aoff <= self.ARENA_F32, (name, self.aoff)
        return Tile(ap, self.P.res(name))

    def arena_ffn(self):
        P = self.P
        o = 0
        self.XB = self.carve(o, NKC * S // 2, BF16, "p (k t) -> p k t", k=NKC)
        o += NKC * S // 2
        self.HT = self.carve(o, NFC * 1024 // 2, BF16, "p (f t) -> p f t", f=NFC)
        o += NFC * 1024 // 2
        self.HTr = [[P.res(f"ht{f}_{t}") for t in range(2)] for f in range(NFC)]
        self.tmpf = []
        for i in range(8):
            self.tmpf.append(Tile(self.carve(o, 512, F32), P.res(f"tmpf{i}")))
            o += 512
        assert o == 23552
        o = 23552 + 2 * 1408
        for i in range(8, 12):
            self.tmpf.append(Tile(self.carve(o, 512, F32), P.res(f"tmpf{i}")))
            o += 512
        assert o <= self.ARENA_F32, o

    def mem_setup(self):
        P = self.P
        MT = self.HT[:, 0:4, :].rearrange("p a b -> p (a b)")[:, 0:NKC * NMEM].rearrange("p (k m) -> p k m", k=NKC)
        mtr = P.res("mt")
        for kc in range(NKC):
            P.dma('pool', (lambda kc: lambda e: e.dma_start(out=MT[:, kc, :], in_=self.memT[kc * 128:(kc + 1) * 128, :]))(kc),
                  mtr, writes=[mtr])
        for c in range(2):
            w = self.w_get('A')
            b = self.bank()
            for kc in range(NKC):
                self.mm(b.ap[:, 0:NMEM], w.ap[:, kc, :], MT[:, kc, :], kc == 0, kc == NKC - 1, [w.res, mtr], [b.res])
            self.copy('act', self.MK.ap[:, c, :], b.ap[:, 0:NMEM], [b.res], [self.MK.res])
        for c in range(2):
            w = self.w_get('A')
            b = self.bank()
            for mt in range(2):
                for kc in range(NKC):
                    self.mm(b.ap[:, mt * 128:(mt + 1) * 128], MT[:, kc, mt * 128:(mt + 1) * 128], w.ap[:, kc, :],
                            kc == 0, kc == NKC - 1, [w.res, mtr], [b.res])
            for mt in range(2):
                self.copy('act', self.MV.ap[:, mt, c, :], b.ap[:, mt * 128:(mt + 1) * 128], [b.res], [self.MV.res])

    def ffn(self, L, which):
        XF, XB, HT, ps = self.XF, self.XB, self.HT, self.ps
        lncol = (L * 3 + (0 if which == 1 else 2)) * 8
        it = 0
        deferred = []
        tf = self.tmpf
        for half in range(2):
            t0 = half * 1024
            for fc in range(NFC):
                if deferred and fc >= 1:
                    dt0, drstd, dmr, ddc = deferred.pop(0)
                    self.ln_apply(dt0, 512, drstd, dmr, lncol, tf[10:12], True, dcs=[ddc])
                wg = self.w_get('A')
                wu = self.w_get('A')
                for tt in range(2):
                    ts0 = t0 + tt * 512
                    tok = slice(ts0, ts0 + 512)
                    bg = ps[(it % 2) * 2]
                    bu = ps[(it % 2) * 2 + 1]
                    for kc in range(NKC):
                        self.mm(bg.ap[:], wg.ap[:, kc, :], XB[:, kc, tok], kc == 0, kc == NKC - 1,
                                [wg.res] + self.xr('b', kc, ts0, 512), [bg.res])
                    for kc in range(NKC):
                        self.mm(bu.ap[:], wu.ap[:, kc, :], XB[:, kc, tok], kc == 0, kc == NKC - 1,
                                [wu.res] + self.xr('b', kc, ts0, 512), [bu.res])
                    st = self.tmpf[it % 2]
                    self.act(st.ap, bg.ap[:], AF.Silu, [bg.res], [st.res])
                    self.tt('dve', HT[:, fc, tt * 512:(tt + 1) * 512], st.ap, bu.ap[:], ALU.mult,
                            [st.res, bu.res], [self.HTr[fc][tt]])
                    it += 1
            sbk = [(ps[2], ps[3]), (ps[0], ps[1])]
            for dc in range(NKC):
                wd = self.w_get('D')
                for tt in range(2):
                    ts0 = t0 + tt * 512
                    tok = slice(ts0, ts0 + 512)
                    by = ps[4 + (it % 2)]
                    for fc in range(NFC):
                        self.mm(by.ap[:], wd.ap[:, fc, :], HT[:, fc, tt * 512:(tt + 1) * 512], fc == 0, fc == NFC - 1,
                                [wd.res, self.HTr[fc][tt]], [by.res])
                    s1, s2 = sbk[tt]
                    self.resid_stats(dc, ts0, 512, by, 2.0 * ALPHA, s1, s2, self.tmpf[2 + it % 2])
                    it += 1
            for tt in range(2):
                s1, s2 = sbk[tt]
                rstd, mr = (tf[6], tf[7]) if tt == 0 else (tf[8], tf[9])
                self.ln_stats(512, s1, s2, 4.0 * LN_EPS, tf[4], tf[5], rstd, mr)
                if half == 0:
                    for dc in range(NKC):
                        deferred.append((t0 + tt * 512, rstd, mr, dc))
                else:
                    self.ln_apply(t0 + tt * 512, 512, rstd, mr, lncol, tf[10:12], True)

    def resid_stats(self, dc, t0, T, by, xscale, s1, s2, sq):
        XF = self.XF
        tok = slice(t0, t0 + T)
        xr = self.xr('f', dc, t0, T)
        self.stt(XF[:, dc, tok], XF[:, dc, tok], xscale, by.ap[:, 0:T], ALU.mult, ALU.add, xr + [by.res], xr)
        self.act(sq.ap[:, 0:T], XF[:, dc, tok], AF.Square, xr, [sq.res])
        self.mm(s1.ap[:, 0:T], self.ones_f.ap[:], XF[:, dc, tok], dc == 0, dc == NKC - 1,
                [self.ones_f.res] + xr, [s1.res])
        self.mm(s2.ap[:, 0:T], self.ones_f.ap[:], sq.ap[:, 0:T], dc == 0, dc == NKC - 1,
                [self.ones_f.res, sq.res], [s2.res])

    def ln_finish(self, t0, T, s1, s2, eps, lncol, tmps, write_xb=True):
        self.ln_stats(T, s1, s2, eps, tmps[4], tmps[5], tmps[6], tmps[7])
        self.ln_apply(t0, T, tmps[6], tmps[7], lncol, tmps[0:2], write_xb)

    def ln_stats(self, T, s1, s2, eps, mean, msq, rstd, mr):
        w = slice(0, T)
        self.act(mean.ap[:, w], s1.ap[:, w], AF.Copy, [s1.res], [mean.res], scale=1.0 / D)
        self.tt('dve', msq.ap[:, w], mean.ap[:, w], mean.ap[:, w], ALU.mult, [mean.res], [msq.res])
        self.stt(msq.ap[:, w], s2.ap[:, w], 1.0 / D, msq.ap[:, w], ALU.mult, ALU.subtract, [s2.res, msq.res], [msq.res])
        self.rsqrt(rstd.ap[:, w], msq.ap[:, w], eps, [msq.res], [rstd.res])
        self.tt('dve', mr.ap[:, w], mean.ap[:, w], rstd.ap[:, w], ALU.mult, [mean.res, rstd.res], [mr.res])

    def ln_apply(self, t0, T, rstd, mr, lncol, us, write_xb=True, dcs=range(NKC)):
        XF, XB = self.XF, self.XB
        tok = slice(t0, t0 + T)
        w = slice(0, T)
        for dc in dcs:
            xr = self.xr('f', dc, t0, T)
            u = us[dc % 2]
            self.tt('dve', u.ap[:, w], XF[:, dc, tok], rstd.ap[:, w], ALU.mult, xr + [rstd.res], [u.res])
            self.tt('dve', u.ap[:, w], u.ap[:, w], mr.ap[:, w], ALU.subtract, [u.res, mr.res], [u.res])
            g = self.LNG.ap[:, lncol + dc:lncol + dc + 1]
            b = self.LNB.ap[:, lncol + dc:lncol + dc + 1]
            self.act(XF[:, dc, tok], u.ap[:, w], AF.Identity, [u.res, self.LNG.res, self.LNB.res], xr, scale=g, bias=b)
            if write_xb:
                self.act(XB[:, dc, tok], u.ap[:, w], AF.Identity, [u.res, self.LNG.res, self.LNB.res],
                         self.xr('b', dc, t0, T), scale=g, bias=b)

    def mem_attn(self, MQ, mq_res, HD, hd_res, t0q, T, t0h, E):
        for h in range(4):
            cq, hp = h // 2, h % 2
            pr = slice(hp * 64, (hp + 1) * 64)
            for mt in range(2):
                b = self.bank()
                self.mm(b.ap[:, 0:T], self.MK.ap[pr, cq, mt * 128:(mt + 1) * 128], MQ[pr, cq, t0q:t0q + T], True, True,
                        [self.MK.res, mq_res], [b.res])
                self.act(E[mt].ap[:, 0:T], b.ap[:, 0:T], AF.Exp, [b.res], [E[mt].res], scale=0.125)
            bn = self.bank()
            bd = self.bank()
            for mt in range(2):
                self.mm(bn.ap[pr, 0:T], self.MV.ap[:, mt, cq, hp * 64:(hp + 1) * 64], E[mt].ap[:, 0:T], mt == 0, mt == 1,
                        [self.MV.res, E[mt].res], [bn.res])
            for mt in range(2):
                self.mm(bd.ap[pr, 0:T], self.ones_b.ap[:, 0:64], E[mt].ap[:, 0:T], mt == 0, mt == 1,
                        [self.ones_b.res, E[mt].res], [bd.res])
            rd = E[2]
            self.P.op('dve', (lambda o, i: lambda e: e.reciprocal(out=o, in_=i))(rd.ap[pr, 0:T], bd.ap[pr, 0:T]),
                      reads=[bd.res], writes=[rd.res])
            self.tt('dve', HD[pr, 6 + cq, t0h:t0h + T], bn.ap[pr, 0:T], rd.ap[pr, 0:T], ALU.mult,
                    [bn.res, rd.res], [hd_res])

    def out_proj_ln(self, L, HD, hd_res, t0h, t0, T, tmps):
        lncol = (L * 3 + 1) * 8
        s1, s2 = self.ps[0], self.ps[1]
        self.ring = list(range(2, 8))
        for dc in range(NKC):
            wo = self.w_get('A')
            by = self.bank()
            for cch in range(NKC):
                self.mm(by.ap[:, 0:T], wo.ap[:, cch, :], HD[:, cch, t0h:t0h + T], cch == 0, cch == NKC - 1,
                        [wo.res, hd_res], [by.res])
            self.resid_stats(dc, t0, T, by, ALPHA, s1, s2, tmps[2 + dc % 2])
        self.ring = list(range(8))
        self.ln_finish(t0, T, s1, s2, LN_EPS, lncol, tmps)
    def rwkv_stage(self):
        P = self.P
        XB = self.XB
        T = 256
        self.aoff = NKC * S // 2
        A = self.alloc
        RP = A(64, F32, "rpar")
        OM = A(32, F32, "om")
        LW = A(768, BF16, "lw")
        GW = A(768, BF16, "gw")
        SM = A(T, F32, "scanmask")
        MLT = A(64, F32, "mlt")
        MLE = A(64, F32, "mle")
        MGT = A(64, F32, "mgt")
        IDB = A(64, F32, "idb")
        IDBb = A(64, BF16, "idbb")
        CAR = A(20, F32, "carry")
        HF = A(6 * 64, F32, "hf", "p (c i) -> p c i", c=6)
        HB = A(6 * 64, BF16, "hb", "p (c i) -> p c i", c=6)
        TD = A(T, BF16, "td")
        SDG = A(T, BF16, "sdg")
        At = A(6 * T, BF16, "At", "p (c t) -> p c t", c=6)
        Bt = A(6 * T, BF16, "Bt", "p (c t) -> p c t", c=6)
        Kt = A(6 * T, BF16, "Kt", "p (c t) -> p c t", c=6)
        Rt = A(6 * T, BF16, "Rt", "p (c t) -> p c t", c=6)
        Bh = A(2 * 768, BF16, "Bh", "p (r c) -> p r c", r=2)
        Kh = A(2 * 768, BF16, "Kh", "p (r c) -> p r c", r=2)
        Vt = A(2 * 768, BF16, "Vt", "p (r c) -> p r c", r=2)
        G = A(6 * T, BF16, "G", "p (c t) -> p c t", c=6)
        BON = A(6 * T, BF16, "BON", "p (c t) -> p c t", c=6)
        GC = A(6 * 4, F32, "gC", "p (c k) -> p c k", c=6)
        HD = A(NKC * T, BF16, "HD", "p (c t) -> p c t", c=NKC)
        MQ = A(2 * T, BF16, "MQ", "p (c t) -> p c t", c=2)
        chres = [[P.res(f"chout{c}_{i}") for i in range(10)] for c in range(6)]
        ov = self.aoff
        SETS = []
        for si in range(2):
            SETS.append(dict(PR=[A(T + 1, F32, f"praw{si}_{i}") for i in range(3)],
                             tm=[A(T, F32, f"rt{si}_{i}") for i in range(14)],
                             tx=[A(T, F32, f"rtx{si}_{i}") for i in range(6)],
                             tb=[A(T, BF16, f"rtb{si}_{i}") for i in range(3)]))
        endA = self.aoff
        self.aoff = ov
        MSETS = []
        for pi in range(2):
            MSETS.append(dict(
                MX=[[A(6 * 64, BF16, f"X{pi}{b}{g}", "p (h t) -> p h t", h=6) for g in range(2)] for b in range(2)],
                MY=[[A(6 * 64, BF16, f"Y{pi}{b}{g}", "p (h t) -> p h t", h=6) for g in range(2)] for b in range(2)],
                ZT=[A(6 * 64, BF16, f"ZT{pi}{g}", "p (h t) -> p h t", h=6) for g in range(2)],
                AK=[A(6 * 64, BF16, f"AK{pi}{g}", "p (h t) -> p h t", h=6) for g in range(2)],
                RB=[A(6 * 64, BF16, f"RB{pi}{g}", "p (h t) -> p h t", h=6) for g in range(2)],
                RK=[A(6 * 64, BF16, f"RK{pi}{g}", "p (h t) -> p h t", h=6) for g in range(2)],
                W0=[A(6 * 64, BF16, f"W0{pi}{g}", "p (h t) -> p h t", h=6) for g in range(2)]))
        WB = [A(6 * 64, BF16, f"WB{g}", "p (h t) -> p h t", h=6) for g in range(2)]
        UB = [A(6 * 64, BF16, f"UB{g}", "p (h t) -> p h t", h=6) for g in range(2)]
        OT = [A(6 * 64, F32, f"OT{g}", "p (h t) -> p h t", h=6) for g in range(2)]
        OO = A(768, F32, "OO", "p (h t) -> p h t", h=12)
        ON = A(768, F32, "ON", "p (h t) -> p h t", h=12)
        SQ = ON
        ONH = A(768, BF16, "ONH")
        ONL = A(768, BF16, "ONL")
        ST = [A(12, F32, f"gst{i}") for i in range(4)]
        lt = [A(T, F32, f"lnt{i}") for i in range(8)]
        ytl = [A(128, F32, f"ytl{i}") for i in range(3)]
        E = [TD, SDG, A(T, F32, "E2")]
        endB = self.aoff
        self.aoff = max(endA, endB)
        tm = SETS[0]['tm']

        self.dma_in('sp', RP.ap, self.rpar_d, RP.res)
        self.dma_in('pool', LW.ap, self.lw_d, LW.res)
        self.dma_in('pool', GW.ap, self.gw_d, GW.res)
        self.ts('dve', OM.ap[:, 0:20], RP.ap[:, PC_MU:PC_MU + 20], -1.0, 1.0, ALU.mult, ALU.add, [RP.res], [OM.res])
        self.ts('dve', OM.ap[:, 20:26], RP.ap[:, PC_KA:PC_KA + 6], -1.0, 1.0, ALU.mult, ALU.add, [RP.res], [OM.res])
        self.memset('dve', SM.ap, 1.0, SM.res)
        self.memset('dve', SM.ap[:, 0:T:64], 0.0, SM.res)
        self.memset('dve', CAR.ap, 0.0, CAR.res)
        self.memset('dve', HF.ap, 0.0, HF.res)
        self.memset('dve', HB.ap, 0.0, HB.res)
        ones = self.ones_f

        def asel(dst, pattern, cmul, op, base=0):
            for hp in range(2):
                pr = slice(hp * 64, (hp + 1) * 64)
                P.op('pool', (lambda pr: lambda e: e.affine_select(
                    out=dst.ap[pr, :], in_=ones.ap[pr, 0:64], pattern=pattern, compare_op=op, fill=0.0,
                    base=base, channel_multiplier=cmul))(pr), reads=[ones.res], writes=[dst.res])
        asel(MLT, [[1, 64]], -1, ALU.is_gt)
        asel(MLE, [[1, 64]], -1, ALU.is_ge)
        asel(MGT, [[-1, 64]], 1, ALU.is_gt)
        asel(IDB, [[-1, 64]], 1, ALU.is_equal)
        self.copy('dve', IDBb.ap, IDB.ap, [IDB.res], [IDBb.res])
        evc = [0]

        def bc6(m):
            return m.ap.unsqueeze(1).to_broadcast([128, 6, 64])
        cut(1)

        rpc = lambda c: RP.ap[:, c:c + 1]

        for blk in range(self.rw_blocks):
            t0 = blk * T
            tokb = slice(t0, t0 + T)
            wt = {}

            def inproj(cc, dst_tile):
                w = self.w_get('A')
                b = self.bank()
                for kc in range(NKC):
                    self.mm(b.ap[:, 0:T], w.ap[:, kc, :], XB[:, kc, tokb], kc == 0, kc == NKC - 1,
                            [w.res] + self.xr('b', kc, t0, T), [b.res])
                self.copy('dve', dst_tile.ap[:, 0:1], CAR.ap[:, cc:cc + 1], [CAR.res], [dst_tile.res])
                self.copy('act', dst_tile.ap[:, 1:T + 1], b.ap[:, 0:T], [b.res], [dst_tile.res])
                self.copy('act', CAR.ap[:, cc:cc + 1], dst_tile.ap[:, T:T + 1], [dst_tile.res], [CAR.res])

            def shift(praw, cc, out):
                self.asc(out.ap, praw.ap[:, 0:T], rpc(PC_MU + cc), None, [praw.res, RP.res], [out.res])
                self.stt(out.ap, praw.ap[:, 1:T + 1], OM.ap[:, cc:cc + 1], out.ap, ALU.mult, ALU.add,
                         [praw.res, OM.res, out.res], [out.res])

            PL = SETS[0]['PR']
            inproj(18, PL[0])
            inproj(19, PL[1])
            zl0, zl1 = SETS[0]['tm'][0], SETS[0]['tm'][1]
            shift(PL[0], 18, zl0)
            shift(PL[1], 19, zl1)
            self.act(TD.ap[0:64, :], zl0.ap[0:64, :], AF.Tanh, [zl0.res], [TD.res])
            self.copy('dve', TD.ap[64:128, :], zl0.ap[64:128, :], [zl0.res], [TD.res])
            self.act(SDG.ap, zl1.ap, AF.Sigmoid, [zl1.res], [SDG.res])
            cut(2)

            def ch_gen(ch, S_):
                PR, tm, tx, tb = S_['PR'], S_['tm'], S_['tx'], S_['tb']
                cr = chres[ch]
                cs = slice(ch * 128, (ch + 1) * 128)
                inproj(ch, PR[0])
                inproj(6 + ch, PR[1])
                inproj(12 + ch, PR[2])
                yield
                zr, zk, zv = tm[2], tm[3], tm[4]
                shift(PR[0], ch, zr)
                shift(PR[1], 6 + ch, zk)
                shift(PR[2], 12 + ch, zv)
                yield
                bw, ba, bg_ = self.bank(), self.bank(), self.bank()
                self.mm(bw.ap[:, 0:T], LW.ap[0:64, cs], TD.ap[0:64, :], True, True, [LW.res, TD.res], [bw.res])
                self.mm(ba.ap[:, 0:T], LW.ap[64:128, cs], TD.ap[64:128, :], True, True, [LW.res, TD.res], [ba.res])
                self.mm(bg_.ap[:, 0:T], GW.ap[:, cs], SDG.ap, True, True, [GW.res, SDG.res], [bg_.res])
                sg, a = tm[5], tm[6]
                self.act(sg.ap, bw.ap[:, 0:T], AF.Sigmoid, [bw.res, RP.res], [sg.res], bias=rpc(PC_W0 + ch), scale=1.0)
                self.act(a.ap, ba.ap[:, 0:T], AF.Sigmoid, [ba.res, RP.res], [a.res], bias=rpc(PC_A0 + ch), scale=1.0)
                self.copy('act', G.ap[:, ch, :], bg_.ap[:, 0:T], [bg_.res], [cr[0]])
                yield
                kk, t1 = tm[7], tm[8]
                t1b, t1c, e2, e3, e4, t2b = tx
                self.asc(kk.ap, zk.ap, rpc(PC_KK + ch), None, [zk.res, RP.res], [kk.res])
                self.act(t1.ap, kk.ap, AF.Square, [kk.res], [t1.res])
                bs = self.bank()
                self.mm(bs.ap[:, 0:T], self.BDF.ap, t1.ap, True, True, [self.BDF.res, t1.res], [bs.res])
                self.ts('dve', t1.ap, bs.ap[:, 0:T], 1e-24, None, ALU.max, None, [bs.res], [t1.res])
                yield
                self.rsqrt(t1.ap, t1.ap, 0.0, [t1.res], [t1.res])
                self.tt('dve', kk.ap, kk.ap, t1.ap, ALU.mult, [kk.res, t1.res], [kk.res])
                yield
                k, bb = tm[9], tm[10]
                self.asc(t1b.ap, a.ap, rpc(PC_KA + ch), OM.ap[:, 20 + ch:21 + ch], [a.res, RP.res, OM.res], [t1b.res])
                self.tt('dve', k.ap, zk.ap, t1b.ap, ALU.mult, [zk.res, t1b.res], [k.res])
                self.tt('dve', bb.ap, kk.ap, a.ap, ALU.mult, [kk.res, a.res], [bb.res])
                yield
                Lp, e, t2 = tm[11], tm[12], tm[13]
                P.op('dve', lambda e_: e_.tensor_tensor_scan(out=Lp.ap, data0=SM.ap, data1=sg.ap, initial=0.0,
                                                             op0=ALU.mult, op1=ALU.add),
                     reads=[SM.res, sg.res], writes=[Lp.res])
                Lp4 = Lp.ap.rearrange("p (c t) -> p c t", c=4)
                yield
                self.act(e.ap, Lp.ap, AF.Exp, [Lp.res], [e.res], scale=DECAY_C)
                self.copy('dve', GC.ap[:, ch, :], e.ap[:, 63:T:64], [e.res], [cr[1]])
                self.tt('dve', Rt.ap[:, ch, :], zr.ap, e.ap, ALU.mult, [zr.res, e.res], [cr[2]])
                self.act(e2.ap, Lp.ap, AF.Exp, [Lp.res], [e2.res], scale=-DECAY_C)
                self.tt('dve', Bt.ap[:, ch, :], bb.ap, e2.ap, ALU.mult, [bb.res, e2.res], [cr[3]])
                self.tt('dve', Kt.ap[:, ch, :], k.ap, e2.ap, ALU.mult, [k.res, e2.res], [cr[4]])
                yield
                self.tt('dve', t2.ap, Lp.ap, sg.ap, ALU.subtract, [Lp.res, sg.res], [t2.res])
                self.act(e3.ap, t2.ap, AF.Exp, [t2.res], [e3.res], scale=DECAY_C)
                self.stt(At.ap[:, ch, :], kk.ap, -1.0, e3.ap, ALU.mult, ALU.mult, [kk.res, e3.res], [cr[5]])
                t24 = t2b.ap.rearrange("p (c t) -> p c t", c=4)
                self.tt('dve', t24, Lp4[:, :, 63:64].to_broadcast([128, 4, 64]), Lp4, ALU.subtract, [Lp.res], [t2b.res])
                self.act(e4.ap, t2b.ap, AF.Exp, [t2b.res], [e4.res], scale=DECAY_C)
                self.tt('dve', tb[0].ap, bb.ap, e4.ap, ALU.mult, [bb.res, e4.res], [tb[0].res])
                self.tt('dve', tb[1].ap, k.ap, e4.ap, ALU.mult, [k.res, e4.res], [tb[1].res])
                self.copy('act', tb[2].ap, zv.ap, [zv.res], [tb[2].res])
                yield
                for pr_ in range(2):
                    ps_ = slice(pr_ * 128, (pr_ + 1) * 128)
                    bt = self.bank()
                    self.tr(bt.ap[:, 0:128], tb[0].ap[:, ps_], [tb[0].res], [bt.res])
                    self.tr(bt.ap[:, 128:256], tb[1].ap[:, ps_], [tb[1].res], [bt.res])
                    self.tr(bt.ap[:, 256:384], tb[2].ap[:, ps_], [tb[2].res], [bt.res])
                    self.copy('act', Bh.ap[:, pr_, cs], bt.ap[:, 0:128], [bt.res], [cr[6]])
                    self.copy('dve', Kh.ap[:, pr_, cs], bt.ap[:, 128:256], [bt.res], [cr[7]])
                    self.copy('act', Vt.ap[:, pr_, cs], bt.ap[:, 256:384], [bt.res], [cr[8]])
                    yield
                self.tt('dve', t1c.ap, zr.ap, k.ap, ALU.mult, [zr.res, k.res], [t1c.res])
                self.asc(t1c.ap, t1c.ap, rpc(PC_RK + ch), None, [t1c.res, RP.res], [t1c.res])
                bs2 = self.bank()
                self.mm(bs2.ap[:, 0:T], self.BDF.ap, t1c.ap, True, True, [self.BDF.res, t1c.res], [bs2.res])
                self.tt('dve', BON.ap[:, ch, :], bs2.ap[:, 0:T], zv.ap, ALU.mult, [bs2.res, zv.res], [cr[9]])

            for c0 in (0, 2, 4):
                alive = [ch_gen(c0, SETS[0]), ch_gen(c0 + 1, SETS[1])]
                while alive:
                    for g_ in list(alive):
                        try:
                            next(g_)
                        except StopIteration:
                            alive.remove(g_)
            cut(3)

            for c2 in range(2):
                w = self.w_get('A')
                b = self.bank()
                for kc in range(NKC):
                    self.mm(b.ap[:, 0:T], w.ap[:, kc, :], XB[:, kc, tokb], kc == 0, kc == NKC - 1,
                            [w.res] + self.xr('b', kc, t0, T), [b.res])
                self.copy('act', MQ.ap[:, c2, :], b.ap[:, 0:T], [b.res], [MQ.res])

            P.barrier()
            def hidx(hg, hh):
                h = 2 * hh + hg
                return h, hh, slice(hg * 64, hg * 64 + 64)

            def tslp(pr_, c2):
                o = pr_ * 128 + c2 * 64
                return slice(o, o + 64)

            def grp(pr_, dst, lhs, rhs, mask):
                for hg in range(2):
                    b = self.bank()
                    for c2 in range(2):
                        for hh in range(6):
                            h, ch, hp = hidx(hg, hh)
                            self.mm(b.ap[c2 * 64:(c2 + 1) * 64, hh * 64:(hh + 1) * 64],
                                    lhs.ap[hp, ch, tslp(pr_, c2)], rhs.ap[hp, ch, tslp(pr_, c2)], True, True,
                                    [lhs.res, rhs.res], [b.res])
                    bv = b.ap[:, 0:384].rearrange("p (h t) -> p h t", h=6)
                    self.tt('dve', dst[hg].ap, bv, bc6(mask), ALU.mult, [b.res, mask.res], [dst[hg].res])

            def grp2(dst, lhs, rhs, accum_into=None, evac='act'):
                for c2 in range(2):
                    b = self.bank()
                    pc = slice(c2 * 64, (c2 + 1) * 64)
                    for hh in range(6):
                        if accum_into is not None:
                            self.mm(b.ap[pc, hh * 64:(hh + 1) * 64], lhs.ap[pc, hh, :], rhs.ap[pc, hh, :], True, False,
                                    [lhs.res, rhs.res], [b.res])
                            self.mm(b.ap[pc, hh * 64:(hh + 1) * 64], IDBb.ap[pc, :], rhs.ap[pc, hh, :], False, True,
                                    [IDBb.res, rhs.res], [b.res])
                        else:
                            self.mm(b.ap[pc, hh * 64:(hh + 1) * 64], lhs.ap[pc, hh, :], rhs.ap[pc, hh, :], True, True,
                                    [lhs.res, rhs.res], [b.res])
                    bv = b.ap[pc, 0:384].rearrange("p (h t) -> p h t", h=6)
                    d_ = accum_into if accum_into is not None else dst
                    eng = 'act' if evc[0] % 2 == 0 else 'dve'
                    evc[0] += 1
                    self.copy(eng, d_.ap[pc], bv, [b.res], [d_.res])

            for pr_ in range(2):
                grp(pr_, MSETS[pr_]['MX'][0], Bt, At, MLT)
            for pr_ in range(2):
                grp(pr_, MSETS[pr_]['MY'][0], At, Bt, MGT)
            for pr_ in range(2):
                grp(pr_, MSETS[pr_]['AK'], Kt, At, MLT)
            for pr_ in range(2):
                grp(pr_, MSETS[pr_]['RB'], Bt, Rt, MLE)
            for pr_ in range(2):
                grp(pr_, MSETS[pr_]['RK'], Kt, Rt, MLE)
            cut(4)
            for pr_ in range(2):
                M_ = MSETS[pr_]
                for hg in range(2):
                    self.tt('dve', M_['ZT'][hg].ap, M_['MX'][0][hg].ap, bc6(IDB), ALU.add,
                            [M_['MX'][0][hg].res, IDB.res], [M_['ZT'][hg].res])
            cur = 0
            for kstep in range(6):
                for hg in range(2):
                    for pr_ in range(2):
                        M_ = MSETS[pr_]
                        Xc, Yc = M_['MX'][cur][hg], M_['MY'][cur][hg]
                        if kstep >= 1:
                            grp2(None, Yc, M_['ZT'][hg], accum_into=M_['ZT'][hg])
                        if kstep <= 3:
                            grp2(M_['MX'][1 - cur][hg], Yc, Xc, evac='act')
                        if kstep <= 4:
                            grp2(M_['MY'][1 - cur][hg], Xc, Yc, evac='dve')
                cur = 1 - cur
            for hg in range(2):
                for c2 in range(2):
                    for pr_ in range(2):
                        M_ = MSETS[pr_]
                        b = self.bank()
                        pc = slice(c2 * 64, (c2 + 1) * 64)
                        for hh in range(6):
                            h = 2 * hh + hg
                            self.mm(b.ap[pc, hh * 64:(hh + 1) * 64], M_['AK'][hg].ap[pc, hh, :],
                                    Vt.ap[pc, pr_, h * 64:(h + 1) * 64], True, True, [M_['AK'][hg].res, Vt.res], [b.res])
                        self.copy('act' if c2 == 0 else 'dve', M_['W0'][hg].ap[pc],
                                  b.ap[pc, 0:384].rearrange("p (h t) -> p h t", h=6), [b.res], [M_['W0'][hg].res])

            for pr_ in range(2):
                M_ = MSETS[pr_]
                ZT, AK, RB, RK, W0 = M_['ZT'], M_['AK'], M_['RB'], M_['RK'], M_['W0']

                def tsl(c2):
                    o = pr_ * 128 + c2 * 64
                    return slice(o, o + 64)

                cut(5)
                ps = self.ps
                for c2 in range(2):
                    pc = slice(c2 * 64, (c2 + 1) * 64)
                    cglob = pr_ * 2 + c2
                    for hg in range(2):
                        b = ps[hg]
                        for hh in range(6):
                            h, ch, hp = hidx(hg, hh)
                            self.mm(b.ap[pc, hh * 64:(hh + 1) * 64], At.ap[hp, ch, tsl(c2)], HB.ap[hp, ch, :], True, True,
                                    [At.res, HB.res], [b.res])
                        self.tt('dve', WB[hg].ap[pc], b.ap[pc, 0:384].rearrange("p (h t) -> p h t", h=6), W0[hg].ap[pc],
                                ALU.add, [b.res, W0[hg].res], [WB[hg].res])
                    for hg in range(2):
                        b = ps[3 + hg]
                        for hh in range(6):
                            h, ch, hp = hidx(hg, hh)
                            self.mm(b.ap[pc, hh * 64:(hh + 1) * 64], Rt.ap[hp, ch, tsl(c2)], HB.ap[hp, ch, :], True, True,
                                    [Rt.res, HB.res], [b.res])
                        self.copy('act', OT[hg].ap[pc], b.ap[pc, 0:384].rearrange("p (h t) -> p h t", h=6), [b.res], [OT[hg].res])
                    for hg in range(2):
                        b = ps[hg]
                        for hh in range(6):
                            self.mm(b.ap[pc, hh * 64:(hh + 1) * 64], ZT[hg].ap[pc, hh, :], WB[hg].ap[pc, hh, :], True, True,
                                    [ZT[hg].res, WB[hg].res], [b.res])
                        self.copy('act', UB[hg].ap[pc], b.ap[pc, 0:384].rearrange("p (h t) -> p h t", h=6), [b.res], [UB[hg].res])
                    bH = ps[2]
                    for hg in range(2):
                        for hh in range(6):
                            h, ch, hp = hidx(hg, hh)
                            hs = slice(h * 64, (h + 1) * 64)
                            self.mm(bH.ap[hp, ch * 64:(ch + 1) * 64], Bh.ap[pc, pr_, hs], UB[hg].ap[pc, hh, :], True, False,
                                    [Bh.res, UB[hg].res], [bH.res])
                            self.mm(bH.ap[hp, ch * 64:(ch + 1) * 64], Kh.ap[pc, pr_, hs], Vt.ap[pc, pr_, hs], False, True,
                                    [Kh.res, Vt.res], [bH.res])
                    for hg in range(2):
                        b = ps[5 + hg]
                        for hh in range(6):
                            h = 2 * hh + hg
                            hs = slice(h * 64, (h + 1) * 64)
                            self.mm(b.ap[pc, hh * 64:(hh + 1) * 64], RB[hg].ap[pc, hh, :], UB[hg].ap[pc, hh, :], True, False,
                                    [RB[hg].res, UB[hg].res], [b.res])
                            self.mm(b.ap[pc, hh * 64:(hh + 1) * 64], RK[hg].ap[pc, hh, :], Vt.ap[pc, pr_, hs], False, True,
                                    [RK[hg].res, Vt.res], [b.res])
                        self.tt('dve', OO.ap.rearrange("p (c g) t -> p c g t", g=2)[pc, :, hg, :],
                                b.ap[pc, 0:384].rearrange("p (h t) -> p h t", h=6),
                                OT[hg].ap[pc], ALU.add, [b.res, OT[hg].res], [OO.res])
                    gcb = GC.ap[:, :, cglob:cglob + 1].to_broadcast([128, 6, 64])
                    self.tt('dve', HF.ap, HF.ap, gcb, ALU.mult, [HF.res, GC.res], [HF.res])
                    self.tt('dve', HF.ap, HF.ap, bH.ap[:, 0:384].rearrange("p (c i) -> p c i", c=6), ALU.add,
                            [HF.res, bH.res], [HF.res])
                    self.copy('act', HB.ap, HF.ap, [HF.res], [HB.res])

                cut(6)
                s1, s2, mean, rstd = ST
                P.op('dve', lambda e_: e_.tensor_reduce(out=s1.ap, in_=OO.ap, axis=AX.X, op=ALU.add),
                     reads=[OO.res], writes=[s1.res])
                self.act(SQ.ap, OO.ap, AF.Square, [OO.res], [SQ.res])
                P.op('dve', lambda e_: e_.tensor_reduce(out=s2.ap, in_=SQ.ap, axis=AX.X, op=ALU.add),
                     reads=[SQ.res], writes=[s2.res])
                self.ts('dve', mean.ap, s1.ap, 1.0 / 64, None, ALU.mult, None, [s1.res], [mean.res])
                self.tt('dve', s1.ap, mean.ap, mean.ap, ALU.mult, [mean.res], [s1.res])
                self.stt(s2.ap, s2.ap, 1.0 / 64, s1.ap, ALU.mult, ALU.subtract, [s2.res, s1.res], [s2.res])
                self.rsqrt(rstd.ap, s2.ap, LNX_EPS, [s2.res], [rstd.res])
                self.tt('dve', ON.ap, OO.ap, mean.ap.unsqueeze(2).to_broadcast([128, 12, 64]), ALU.subtract,
                        [OO.res, mean.res], [ON.res])
                self.tt('dve', ON.ap, ON.ap, rstd.ap.unsqueeze(2).to_broadcast([128, 12, 64]), ALU.mult,
                        [ON.res, rstd.res], [ON.res])
                ONf = ON.ap.rearrange("p h t -> p (h t)")
                self.copy('act', ONH.ap, ONf, [ON.res], [ONH.res])
                self.tt('dve', ONL.ap, ONf, ONH.ap, ALU.subtract, [ON.res, ONH.res], [ONL.res])
                for half in range(2):
                    bt = self.bank()
                    for j in range(3):
                        ch = half * 3 + j
                        self.tr(bt.ap[:, j * 128:(j + 1) * 128], ONH.ap[:, ch * 128:(ch + 1) * 128], [ONH.res], [bt.res],
                                True, False)
                        self.tr(bt.ap[:, j * 128:(j + 1) * 128], ONL.ap[:, ch * 128:(ch + 1) * 128], [ONL.res], [bt.res],
                                False, True)
                    for j in range(3):
                        ch = half * 3 + j
                        y = ytl[j]
                        bsl = slice(pr_ * 128, (pr_ + 1) * 128)
                        self.act(y.ap[:, 0:128], bt.ap[:, j * 128:(j + 1) * 128], AF.Identity, [bt.res, RP.res], [y.res],
                                 scale=rpc(PC_LG + ch), bias=rpc(PC_LB + ch))
                        self.tt('dve', y.ap[:, 0:128], y.ap[:, 0:128], BON.ap[:, ch, bsl], ALU.add, [y.res, BON.res], [y.res])
                        self.tt('dve', HD.ap[:, ch, bsl], y.ap[:, 0:128], G.ap[:, ch, bsl], ALU.mult, [y.res, G.res], [HD.res])

            cut(7)
            self.mem_attn(MQ.ap, MQ.res, HD.ap, HD.res, 0, T, 0, E)
            self.dbg(f"hd{blk}", HD.ap, [128, NKC, T], [HD.res], BF16)
            self.out_proj_ln(0, HD.ap, HD.res, 0, t0, T, lt)
            P.barrier()
    def sb_stage(self):
        P = self.P
        XB = self.XB
        self.aoff = NKC * S // 2
        A = self.alloc
        QT = A(6 * S, BF16, "QT", "p (c t) -> p c t", c=6)
        KT = A(6 * S, BF16, "KT", "p (c t) -> p c t", c=6)
        VT = A(16 * 768, BF16, "VTs", "p (s c) -> p s c", s=16)
        MQ = A(2 * S, BF16, "MQs", "p (c t) -> p c t", c=2)
        QTr = [[P.res(f"qt{c}_{t}") for t in range(4)] for c in range(6)]
        KTr = [P.res(f"kt{c}") for c in range(6)]

        for cc in range(20):
            w = self.w_get('A')
            if 12 <= cc < 18:
                for g4 in range(4):
                    b = self.bank()
                    for j in range(4):
                        st = g4 * 4 + j
                        for kc in range(NKC):
                            self.mm(b.ap[:, j * 128:(j + 1) * 128], XB[:, kc, st * 128:(st + 1) * 128], w.ap[:, kc, :],
                                    kc == 0, kc == NKC - 1, [w.res] + self.xr('b', kc, st * 128, 128), [b.res])
                    dst = VT.ap[:, g4 * 4:(g4 + 1) * 4, (cc - 12) * 128:(cc - 11) * 128]
                    self.copy('act' if g4 % 2 == 0 else 'dve', dst, b.ap[:].rearrange("p (j c) -> p j c", j=4), [b.res], [VT.res])
                continue
            for tt in range(4):
                tok = slice(tt * 512, (tt + 1) * 512)
                b = self.bank()
                for kc in range(NKC):
                    self.mm(b.ap[:], w.ap[:, kc, :], XB[:, kc, tok], kc == 0, kc == NKC - 1,
                            [w.res] + self.xr('b', kc, tt * 512, 512), [b.res])
                eng = 'act' if tt % 2 == 0 else 'dve'
                if cc < 6:
                    self.copy(eng, QT.ap[:, cc, tok], b.ap[:], [b.res], [QTr[cc][tt]])
                elif cc < 12:
                    self.copy(eng, KT.ap[:, cc - 6, tok], b.ap[:], [b.res], [KTr[cc - 6]])
                else:
                    self.copy(eng, MQ.ap[:, cc - 18, tok], b.ap[:], [b.res], [MQ.res])
        P.barrier()

        save = self.aoff
        self.aoff = 0
        TRI = A(128, BF16, "tri")
        TRC = A(128, BF16, "trc")
        MSK = [A(512, BF16, f"dmask{o}") for o in range(4)]
        EX = [A(512, F32, f"ex{i}") for i in range(2)]
        SP = [A(512, F32, f"sp{i}") for i in range(3)]
        SPH = [A(512, BF16, f"sph{i}") for i in range(5)]
        SPL = [A(512, BF16, f"spl{i}") for i in range(5)]
        ARG = [A(512, F32, f"arg{i}") for i in range(2)]
        ATT = [A(512, BF16, f"att{i}") for i in range(3)]
        assert self.aoff <= NKC * S // 2, self.aoff
        ones_b = self.ones_b
        P.op('pool', lambda e: e.affine_select(out=TRI.ap, in_=ones_b.ap[:], pattern=[[-1, 128]], compare_op=ALU.is_gt,
                                               fill=0.0, base=0, channel_multiplier=1), reads=[ones_b.res], writes=[TRI.res])
        P.op('pool', lambda e: e.affine_select(out=TRC.ap, in_=ones_b.ap[:], pattern=[[1, 128]], compare_op=ALU.is_ge,
                                               fill=0.0, base=0, channel_multiplier=-1), reads=[ones_b.res], writes=[TRC.res])
        self.memset('dve', EX[0].ap, 1.0, EX[0].res)
        for o in range(4):
            P.op('pool', (lambda o: lambda e: e.affine_select(out=MSK[o].ap, in_=EX[0].ap, pattern=[[1, 512]],
                                                              compare_op=ALU.is_gt, fill=0.0, base=-128 * o,
                                                              channel_multiplier=-1))(o),
                 reads=[EX[0].res], writes=[MSK[o].res])

        ps = self.ps
        pairs = []
        it = 0
        for hpair in range(6):
            for qt in range(4):
                nkt = 4 * (qt + 1)
                for idx, kt in enumerate(range(nkt - 1, -1, -1)):
                    for j in range(2):
                        pairs.append(dict(h=2 * hpair + j, qt=qt, kt=kt, idx=idx, nkt=nkt, it=it + j, g=len(pairs)))
                it += 2

        def bufs(p):
            g = p['g']
            return dict(zb=ps[2 + g % 3], ex=EX[g % 2], sp=SP[g % 3], sph=SPH[g % 5], spl=SPL[g % 5], arg=ARG[g % 2],
                        att=ATT[g % 3], xs=ps[5 + p['it'] % 2], ob=ps[p['it'] % 2])

        def stage0(p):
            b = bufs(p)
            h, qt, kt = p['h'], p['qt'], p['kt']
            ch, hp = h // 2, slice((h % 2) * 64, (h % 2) * 64 + 64)
            qtok = slice(qt * 512, (qt + 1) * 512)
            diag = kt - 4 * qt
            zb, ex, sp, sph, spl = b['zb'], b['ex'], b['sp'], b['sph'], b['spl']
            self.mm(zb.ap[:], KT.ap[hp, ch, kt * 128:(kt + 1) * 128], QT.ap[hp, ch, qtok], True, True,
                    [KTr[ch], QTr[ch][qt]], [zb.res])
            self.act(ex.ap, zb.ap[:], AF.Exp, [zb.res], [ex.res], scale=0.125)
            if diag >= 0:
                self.tt('pool', ex.ap, ex.ap, MSK[diag].ap, ALU.mult, [ex.res, MSK[diag].res], [ex.res])
            self.act(sp.ap, ex.ap, AF.Ln, [ex.res], [sp.res], bias=self.cst(1.0), scale=1.0)
            self.act(sph.ap, ex.ap, AF.Ln, [ex.res], [sph.res], bias=self.cst(1.0), scale=1.0)
            self.tt('dve', spl.ap, sp.ap, sph.ap, ALU.subtract, [sp.res, sph.res], [spl.res])

        def stage1(p, prev):
            b = bufs(p)
            diag = p['kt'] - 4 * p['qt']
            zb, sp, sph, spl, arg, att, xs = b['zb'], b['sp'], b['sph'], b['spl'], b['arg'], b['att'], b['xs']
            first = p['idx'] == 0
            if not first:
                pb = bufs(prev)
                self.mm(xs.ap[:], TRC.ap, pb['sph'].ap, False, False, [TRC.res, pb['sph'].res], [xs.res], skip=True)
                self.mm(xs.ap[:], TRC.ap, pb['spl'].ap, False, False, [TRC.res, pb['spl'].res], [xs.res], skip=True)
            self.mm(xs.ap[:], TRI.ap, sph.ap, first, False, [TRI.res, sph.res], [xs.res], skip=not first)
            self.mm(xs.ap[:], TRI.ap, spl.ap, False, True, [TRI.res, spl.res], [xs.res], skip=not first)
            self.stt(arg.ap, zb.ap[:], 0.125, sp.ap, ALU.mult, ALU.subtract, [zb.res, sp.res], [arg.res])
            self.tt('dve', arg.ap, arg.ap, xs.ap[:], ALU.subtract, [arg.res, xs.res], [arg.res])

        def stage1b(p):
            b = bufs(p)
            diag = p['kt'] - 4 * p['qt']
            arg, att = b['arg'], b['att']
            self.act(att.ap, arg.ap, AF.Exp, [arg.res], [att.res])
            if diag >= 0:
                self.tt('pool', att.ap, att.ap, MSK[diag].ap, ALU.mult, [att.res, MSK[diag].res], [att.res])

        def stage2(p):
            b = bufs(p)
            h, qt, kt = p['h'], p['qt'], p['kt']
            ch, hp = h // 2, slice((h % 2) * 64, (h % 2) * 64 + 64)
            qtok = slice(qt * 512, (qt + 1) * 512)
            ob, att = b['ob'], b['att']
            self.mm(ob.ap[hp, :], VT.ap[:, kt, h * 64:(h + 1) * 64], att.ap, p['idx'] == 0, p['idx'] == p['nkt'] - 1,
                    [VT.res, att.res], [ob.res])
            if p['idx'] == p['nkt'] - 1:
                self.copy('act', QT.ap[hp, ch, qtok], ob.ap[hp, :], [ob.res], [QTr[ch][qt]])

        N = len(pairs)
        for step in range(N + 5):
            if 0 <= step - 2 < N:
                p = pairs[step - 2]
                stage1(p, pairs[step - 4] if p['idx'] > 0 else None)
            if 0 <= step - 3 < N:
                stage1b(pairs[step - 3])
            if step < N:
                stage0(pairs[step])
            if 0 <= step - 5 < N:
                stage2(pairs[step - 5])
        P.barrier()
        self.aoff = 0
        E = [A(512, BF16, "E0s"), A(512, BF16, "E1s"), A(512, F32, "E2s")]
        lt = [A(512, F32, f"lnts{i}") for i in range(12)]
        assert self.aoff <= NKC * S // 2, self.aoff
        self.aoff = save
        self.sb_deferred = []

        for tt in range(4):
            t0 = tt * 512
            HDv = None
            self.mem_attn_sb(MQ, t0, E)
            self.out_proj_ln_sb(QT, QTr, MQ, t0, lt)
        while self.sb_deferred:
            dt0, drstd, dmr, ddc = self.sb_deferred.pop(0)
            self.ln_apply(dt0, 512, drstd, dmr, (1 * 3 + 1) * 8, lt[10:12], False, dcs=[ddc])

    def mem_attn_sb(self, MQ, t0, E):
        T = 512
        for h in range(4):
            cq, hp = h // 2, h % 2
            pr = slice(hp * 64, (hp + 1) * 64)
            for mt in range(2):
                b = self.bank()
                self.mm(b.ap[:, 0:T], self.MK.ap[pr, cq, mt * 128:(mt + 1) * 128], MQ.ap[pr, cq, t0:t0 + T], True, True,
                        [self.MK.res, MQ.res], [b.res])
                self.act(E[mt].ap[:, 0:T], b.ap[:, 0:T], AF.Exp, [b.res], [E[mt].res], scale=0.125)
            bn = self.bank()
            bd = self.bank()
            for mt in range(2):
                self.mm(bn.ap[pr, 0:T], self.MV.ap[:, mt, cq, hp * 64:(hp + 1) * 64], E[mt].ap[:, 0:T], mt == 0, mt == 1,
                        [self.MV.res, E[mt].res], [bn.res])
            for mt in range(2):
                self.mm(bd.ap[pr, 0:T], self.ones_b.ap[:, 0:64], E[mt].ap[:, 0:T], mt == 0, mt == 1,
                        [self.ones_b.res, E[mt].res], [bd.res])
            rd = E[2]
            self.P.op('dve', (lambda o, i: lambda e: e.reciprocal(out=o, in_=i))(rd.ap[pr, 0:T], bd.ap[pr, 0:T]),
                      reads=[bd.res], writes=[rd.res])
            self.tt('dve', MQ.ap[pr, cq, t0:t0 + T], bn.ap[pr, 0:T], rd.ap[pr, 0:T], ALU.mult,
                    [bn.res, rd.res], [MQ.res])

    def out_proj_ln_sb(self, QT, QTr, MQ, t0, tmps):
        T = 512
        tt = t0 // 512
        lncol = (1 * 3 + 1) * 8
        s1, s2 = self.ps[0], self.ps[1]
        self.ring = list(range(2, 8))
        for dc in range(NKC):
            wo = self.w_get('A')
            by = self.bank()
            for cch in range(NKC):
                if cch < 6:
                    rhs, rr = QT.ap[:, cch, t0:t0 + T], QTr[cch][tt]
                else:
                    rhs, rr = MQ.ap[:, cch - 6, t0:t0 + T], MQ.res
                self.mm(by.ap[:, 0:T], wo.ap[:, cch, :], rhs, cch == 0, cch == NKC - 1, [wo.res, rr], [by.res])
            self.resid_stats(dc, t0, T, by, ALPHA, s1, s2, tmps[2 + dc % 2])
            if self.sb_deferred:
                dt0, drstd, dmr, ddc = self.sb_deferred.pop(0)
                self.ln_apply(dt0, 512, drstd, dmr, lncol, tmps[10:12], False, dcs=[ddc])
        self.ring = list(range(8))
        rstd, mr = (tmps[6], tmps[7]) if tt % 2 == 0 else (tmps[8], tmps[9])
        self.ln_stats(T, s1, s2, LN_EPS, tmps[4], tmps[5], rstd, mr)
        for dc in range(NKC):
            self.sb_deferred.append((t0, rstd, mr, dc))


def tile_a(w):
    lead = w.shape[:-2]
    n = w.shape[-1] // 128
    w = w.reshape(lead + (NKC, 128, n, 128))
    nd = len(lead)
    w = np.transpose(w, tuple(range(nd)) + (nd + 2, nd + 1, nd + 0, nd + 3))
    return np.ascontiguousarray(w).reshape(lead + (n, 128, NKC * 128))


def tile_d(w):
    lead = w.shape[:-2]
    w = w.reshape(lead + (NFC, 128, NKC, 128))
    nd = len(lead)
    w = np.transpose(w, tuple(range(nd)) + (nd + 2, nd + 1, nd + 0, nd + 3))
    return np.ascontiguousarray(w).reshape(lead + (NKC, 128, NFC * 128))


def cols(v, n):
    return np.asarray(v, dtype=np.float32).reshape(n, 128).T


def prep_shared(inputs):
    f = lambda a: np.asarray(a, dtype=np.float32)
    sh = {}
    for which in (1, 2):
        sh[f"g{which}"] = tile_a(f(inputs[f"ffn{which}_w_gate"]))
        sh[f"u{which}"] = tile_a(f(inputs[f"ffn{which}_w_up"]))
        sh[f"d{which}"] = tile_d(f(inputs[f"ffn{which}_w_down"]))
    sh["lng"] = np.ascontiguousarray(f(inputs["ln_g"]).reshape(6, NKC, 128).transpose(2, 0, 1).reshape(128, 48))
    sh["lnb"] = np.ascontiguousarray(f(inputs["ln_b"]).reshape(6, NKC, 128).transpose(2, 0, 1).reshape(128, 48))
    sh["wout"] = tile_a(f(inputs["w_out"]))
    sh["wmem"] = tile_a(f(inputs["w_mem_kv"]))
    sh["rwin"] = tile_a(f(inputs["rwkv_w_in"])[0])
    sh["sbin"] = tile_a(f(inputs["sb_w_in"])[0])
    rp = np.zeros((128, 64), np.float32)
    rp[:, PC_MU:PC_MU + 20] = cols(f(inputs["rwkv_mu"])[0], 20)
    for off, key in ((PC_W0, "rwkv_w0"), (PC_A0, "rwkv_a0"), (PC_KK, "rwkv_k_k"), (PC_KA, "rwkv_k_a"),
                     (PC_RK, "rwkv_r_k"), (PC_LG, "rwkv_lnx_g"), (PC_LB, "rwkv_lnx_b")):
        rp[:, off:off + 6] = cols(f(inputs[key])[0].reshape(-1), 6)
    sh["rpar"] = rp
    sh["lw"] = np.ascontiguousarray(np.concatenate([f(inputs["rwkv_w_up"])[0], f(inputs["rwkv_a_up"])[0]], axis=0))
    sh["gw"] = np.ascontiguousarray(f(inputs["rwkv_g_up"])[0])
    return sh


_NC_CACHE = {}


def get_nc(stop_after=None, start_at=0, dbg=(), rw_blocks=8):
    key = (stop_after, start_at, tuple(dbg), rw_blocks)
    if key not in _NC_CACHE:
        _NC_CACHE[key] = Builder(stop_after, start_at, dbg, rw_blocks).build()
    return _NC_CACHE[key]


def make_in_maps(inputs, cores=8, x_override=None):
    sh = prep_shared(inputs)
    x = np.asarray(inputs["x"], dtype=np.float32) if x_override is None else x_override
    mem = np.asarray(inputs["mem"], dtype=np.float32)
    in_maps = []
    for b in range(cores):
        m = dict(sh)
        m["xT"] = np.ascontiguousarray(x[b].T)
        m["memT"] = np.ascontiguousarray(mem[b].T)
        in_maps.append(m)
    return in_maps


def run(inputs, cores=8, stop_after=None, start_at=0, dbg=(), trace=False, x_override=None, rw_blocks=8):
    sh = prep_shared(inputs)
    x = np.asarray(inputs["x"], dtype=np.float32) if x_override is None else x_override
    mem = np.asarray(inputs["mem"], dtype=np.float32)
    in_maps = []
    for b in range(cores):
        m = dict(sh)
        m["xT"] = np.ascontiguousarray(x[b].T)
        m["memT"] = np.ascontiguousarray(mem[b].T)
        in_maps.append(m)
    nc = get_nc(stop_after, start_at, dbg, rw_blocks)
    res = run_bass_kernel_spmd(nc, in_maps, core_ids=list(range(cores)), trace=trace)
    out = np.stack([np.ascontiguousarray(r["outT"].T) for r in res.results], axis=0)
    return out, res


def kernel(**inputs):
    out, _ = run(inputs, cores=8)
    return out.astype(np.float32)
```
